# Optimizing a Trainium2 kernel written in Bass

```python
import jax, jax.numpy as jnp
from jax import lax
import numpy as np

D_MODEL = 1024
BATCH = 8
SEQ = 2048
DEPTH = 4
DEC_BATCH = 128
DEC_SEQ = 4
PAST_LEN = 8192
PAGE_SIZE = 128

D_PLE = 256
M_HEADS = 4
M_DQK = D_MODEL // 8
M_DV = D_MODEL // 4
M_CHUNK = 64
S_HEADS = 16
S_KV_HEADS = 4
S_HEAD_DIM = D_MODEL // S_HEADS
S_GROUP = S_HEADS // S_KV_HEADS
WINDOW = 128
REL_BUCKETS = 32
REL_MAX_DIST = WINDOW
N_GROUPS = 4
EXPERTS_PER_GROUP = 8
N_EXPERTS = N_GROUPS * EXPERTS_PER_GROUP
TOP_K = 2
D_EXPERT = D_MODEL // 2
MOE_BLOCK = 128
DN_ALPHA = (2 * DEPTH) ** 0.25
DN_BETA = (8 * DEPTH) ** -0.25
LN_EPS = 1e-5

_IN_SIZES = (
    M_HEADS * M_DQK,
    M_HEADS * M_DQK,
    M_HEADS * M_DV,
    M_HEADS * M_DV,
    M_HEADS,
    M_HEADS,
    S_HEADS * S_HEAD_DIM,
    S_KV_HEADS * S_HEAD_DIM,
    S_KV_HEADS * S_HEAD_DIM,
    D_MODEL,
    D_MODEL,
)
_IN_SPLITS = tuple(sum(_IN_SIZES[:j + 1]) for j in range(len(_IN_SIZES) - 1))
D_IN = sum(_IN_SIZES)
F_OFF = _IN_SPLITS[4]

kernel_name = 'hybrid_mlstm_swa_hmoe_decoder_step'


def layer_norm(x, g, b):
    xf = x.astype(jnp.float32)
    mu = xf.mean(-1, keepdims=True)
    var = jnp.square(xf - mu).mean(-1, keepdims=True)
    return ((xf - mu) * lax.rsqrt(var + LN_EPS) * g.astype(jnp.float32) + b.astype(jnp.float32)).astype(x.dtype)


def rel_bucket(dist):
    n = np.maximum(dist, 0)
    max_exact = REL_BUCKETS // 2
    large = max_exact + (np.log(np.maximum(n, 1) / max_exact) / np.log(REL_MAX_DIST / max_exact)
                         * (REL_BUCKETS - max_exact)).astype(np.int32)
    large = np.minimum(large, REL_BUCKETS - 1)
    return np.where(n < max_exact, n, large).astype(np.int32)


def rel_bias(rel_table, dist):
    bias = rel_table[rel_bucket(dist)].astype(jnp.float32)
    return jnp.transpose(bias, (2, 0, 1)).reshape((S_KV_HEADS, S_GROUP) + dist.shape)


def sink_softmax(s, sink):
    sk = sink.astype(jnp.float32)[:, :, None, None]
    m = jnp.maximum(s.max(-1, keepdims=True), sk)
    p = jnp.exp(s - m)
    return p / (p.sum(-1, keepdims=True) + jnp.exp(sk - m))


def swa_prompt(q, k, v, rel_table, sink):
    B, T = q.shape[:2]
    nb = T // WINDOW
    qb = q.reshape(B, nb, WINDOW, S_KV_HEADS, S_GROUP, S_HEAD_DIM)
    kb = k.reshape(B, nb, WINDOW, S_KV_HEADS, S_HEAD_DIM)
    vb = v.reshape(B, nb, WINDOW, S_KV_HEADS, S_HEAD_DIM)
    shift = ((0, 0), (1, 0), (0, 0), (0, 0), (0, 0))
    kk = jnp.concatenate([jnp.pad(kb, shift)[:, :-1], kb], axis=2)
    vv = jnp.concatenate([jnp.pad(vb, shift)[:, :-1], vb], axis=2)
    s = jnp.einsum('bnqhgd,bnkhd->bnhgqk', qb, kk).astype(jnp.float32) * (S_HEAD_DIM ** -0.5)
    qi = np.arange(WINDOW)[:, None]
    kj = np.arange(2 * WINDOW)[None, :]
    dist = qi + WINDOW - kj
    band = (dist >= 0) & (dist <= WINDOW)
    blk = np.arange(nb)[:, None, None]
    mask = band[None] & ((blk > 0) | (kj >= WINDOW)[None])
    s = jnp.where(mask[None, :, None, None], s + rel_bias(rel_table, dist), -jnp.inf)
    p = sink_softmax(s, sink.reshape(S_KV_HEADS, S_GROUP))
    o = jnp.einsum('bnhgqk,bnkhd->bnqhgd', p.astype(vv.dtype), vv)
    return o.reshape(B, T, S_HEADS * S_HEAD_DIM), k[:, -WINDOW:], v[:, -WINDOW:]


def swa_sample(q, k, v, buf_k, buf_v, rel_table, sink):
    B, T = q.shape[:2]
    kk = jnp.concatenate([buf_k.astype(k.dtype), k], axis=1)
    vv = jnp.concatenate([buf_v.astype(v.dtype), v], axis=1)
    qg = q.reshape(B, T, S_KV_HEADS, S_GROUP, S_HEAD_DIM)
    s = jnp.einsum('bqhgd,bkhd->bhgqk', qg, kk).astype(jnp.float32) * (S_HEAD_DIM ** -0.5)
    dist = np.arange(T)[:, None] + WINDOW - np.arange(WINDOW + T)[None, :]
    mask = (dist >= 0) & (dist <= WINDOW)
    s = jnp.where(mask, s + rel_bias(rel_table, dist), -jnp.inf)
    p = sink_softmax(s, sink.reshape(S_KV_HEADS, S_GROUP))
    o = jnp.einsum('bhgqk,bkhd->bqhgd', p.astype(vv.dtype), vv)
    return o.reshape(B, T, S_HEADS * S_HEAD_DIM), kk[:, -WINDOW:], vv[:, -WINDOW:]


def mlstm_mix(q, k, v, ig, fg, C0, n0, m0):
    B, T = q.shape[:2]
    L = M_CHUNK if T % M_CHUNK == 0 else T
    nc = T // L

    def to_chunks(a):
        a = a.reshape((B, nc, L) + a.shape[2:])
        return jnp.moveaxis(jnp.moveaxis(a, 1, 0), 2, 3)

    causal = np.tril(np.ones((L, L), dtype=bool))

    def step(carry, inp):
        C, n, m = carry
        qc, kc, vc, ic, lfc = inp
        b = jnp.cumsum(lfc, axis=-1)
        dmat = jnp.where(causal, b[..., :, None] - b[..., None, :] + ic[..., None, :], -jnp.inf)
        inter = b + m[..., None]
        mhat = jnp.maximum(inter, dmat.max(-1))
        w_intra = jnp.exp(dmat - mhat[..., None])
        w_inter = jnp.exp(inter - mhat)
        s = jnp.einsum('bhld,bhsd->bhls', qc, kc) * w_intra
        num = jnp.einsum('bhls,bhsv->bhlv', s, vc) + w_inter[..., None] * jnp.einsum('bhld,bhdv->bhlv', qc, C)
        den = s.sum(-1) + w_inter * jnp.einsum('bhld,bhd->bhl', qc, n)
        h = num / jnp.maximum(jnp.abs(den), jnp.exp(-mhat))[..., None]
        b_end = b[..., -1]
        g = ic + b_end[..., None] - b
        m_new = jnp.maximum(b_end + m, g.max(-1))
        decay = jnp.exp(b_end + m - m_new)
        kw = kc * jnp.exp(g - m_new[..., None])[..., None]
        C_new = decay[..., None, None] * C + jnp.einsum('bhld,bhlv->bhdv', kw, vc)
        n_new = decay[..., None] * n + kw.sum(2)
        return (C_new, n_new, m_new), h

    xs = (to_chunks(q), to_chunks(k), to_chunks(v), to_chunks(ig), to_chunks(jax.nn.log_sigmoid(fg)))
    (C, n, m), h = lax.scan(step, (C0, n0, m0), xs)
    h = jnp.moveaxis(jnp.moveaxis(h, 3, 2), 0, 1).reshape(B, T, M_HEADS, M_DV)
    return h, C, n, m


def routed_experts(xt, eid, tok, wt, w_gate, w_up, w_down):
    M, D = xt.shape
    A = eid.shape[0]
    nblk = -(-A // MOE_BLOCK) + N_EXPERTS
    P = nblk * MOE_BLOCK
    order = jnp.argsort(eid)
    eid_s, tok_s, wt_s = eid[order], tok[order], wt[order]
    counts = jnp.bincount(eid, length=N_EXPERTS)
    padded = (counts + MOE_BLOCK - 1) // MOE_BLOCK * MOE_BLOCK
    pad_end = jnp.cumsum(padded)
    pad_start = pad_end - padded
    start = jnp.cumsum(counts) - counts
    pos = pad_start[eid_s] + jnp.arange(A) - start[eid_s]
    row_tok = jnp.full((P,), M, jnp.int32).at[pos].set(tok_s)
    row_w = jnp.zeros((P,), jnp.float32).at[pos].set(wt_s)
    blk_e = jnp.minimum(jnp.searchsorted(pad_end, jnp.arange(nblk) * MOE_BLOCK, side='right'), N_EXPERTS - 1)
    x_pad = jnp.concatenate([xt, jnp.zeros((1, D), xt.dtype)], axis=0)
    xb = x_pad[row_tok].reshape(nblk, MOE_BLOCK, D)

    def expert_block(args):
        xblk, e = args
        h = jax.nn.silu(xblk @ w_gate[e]) * (xblk @ w_up[e])
        return h @ w_down[e]

    yb = lax.map(expert_block, (xb, blk_e))
    y = jnp.zeros((M + 1, D), jnp.float32).at[row_tok].add(yb.reshape(P, D).astype(jnp.float32) * row_w[:, None])
    return y[:M].astype(xt.dtype)


def hier_moe(x, w_rg, b_rg, w_re, b_re, w_eg, w_eu, w_ed):
    B, T, D = x.shape
    M = B * T
    xt = x.reshape(M, D)
    g_logits = (xt @ w_rg + b_rg).astype(jnp.float32)
    g_prob = jax.nn.softmax(g_logits, axis=-1)
    g_idx = jnp.argmax(g_logits, axis=-1)
    g_w = jnp.take_along_axis(g_prob, g_idx[:, None], axis=-1)
    e_logits = (xt @ w_re + b_re).astype(jnp.float32).reshape(M, N_GROUPS, EXPERTS_PER_GROUP)
    e_logits = jnp.take_along_axis(e_logits, g_idx[:, None, None], axis=1)[:, 0]
    top_p, top_i = lax.top_k(jax.nn.softmax(e_logits, axis=-1), TOP_K)
    gate = g_w * top_p / top_p.sum(-1, keepdims=True)
    eid = (g_idx[:, None] * EXPERTS_PER_GROUP + top_i).reshape(-1).astype(jnp.int32)
    tok = jnp.repeat(jnp.arange(M, dtype=jnp.int32), TOP_K)
    y = routed_experts(xt, eid, tok, gate.reshape(-1), w_eg, w_eu, w_ed)
    return y.reshape(B, T, D)


def layer(x, p_i, C0, n0, m0, buf_k, buf_v, w_in, b_in, mh_gain, w_a, w_b, w_out, rel_table, sink,
          ln_g, ln_b, w_rg, b_rg, w_re, b_re, w_eg, w_eu, w_ed, w_pg, w_pp):
    B, T, _ = x.shape
    f32 = jnp.float32
    z = x @ w_in + b_in
    qm, km, vm, og, ig, fg, qs, ks, vs, ga, gb = jnp.split(z, _IN_SPLITS, axis=-1)
    q = qm.reshape(B, T, M_HEADS, M_DQK).astype(f32) * (M_DQK ** -0.5)
    k = km.reshape(B, T, M_HEADS, M_DQK).astype(f32)
    v = vm.reshape(B, T, M_HEADS, M_DV).astype(f32)
    h, C, n, m = mlstm_mix(q, k, v, ig.astype(f32), fg.astype(f32), C0.astype(f32), n0.astype(f32), m0.astype(f32))
    mu = h.mean(-1, keepdims=True)
    var = jnp.square(h - mu).mean(-1, keepdims=True)
    h = (h - mu) * lax.rsqrt(var + LN_EPS) * mh_gain.reshape(M_HEADS, M_DV).astype(f32)
    ya = (h.reshape(B, T, M_HEADS * M_DV) * jax.nn.sigmoid(og.astype(f32))).astype(x.dtype)
    qs = qs.reshape(B, T, S_HEADS, S_HEAD_DIM)
    ks = ks.reshape(B, T, S_KV_HEADS, S_HEAD_DIM)
    vs = vs.reshape(B, T, S_KV_HEADS, S_HEAD_DIM)
    if buf_k is None:
        yb, kw, vw = swa_prompt(qs, ks, vs, rel_table, sink)
    else:
        yb, kw, vw = swa_sample(qs, ks, vs, buf_k, buf_v, rel_table, sink)
    mix = (jax.nn.sigmoid(ga) * (ya @ w_a) + jax.nn.sigmoid(gb) * (yb @ w_b)) @ w_out
    x = layer_norm(DN_ALPHA * x + mix, ln_g[0], ln_b[0])
    x = layer_norm(DN_ALPHA * x + hier_moe(x, w_rg, b_rg, w_re, b_re, w_eg, w_eu, w_ed), ln_g[1], ln_b[1])
    ple = jax.nn.sigmoid(x @ w_pg) * (p_i @ w_pp)
    x = layer_norm(DN_ALPHA * x + ple, ln_g[2], ln_b[2])
    return x, (C, n, m, kw, vw)


def setup_inputs(seed: int = 0) -> dict:
    key = jax.random.key(seed)
    ks = jax.random.split(key, 32)
    f32 = jnp.float32

    def nrm(k, shape, scale):
        return jax.random.normal(k, shape, f32) * scale

    b_in = nrm(ks[10], (DEPTH, D_IN), 0.01)
    b_in = b_in.at[:, F_OFF:F_OFF + M_HEADS].add(jnp.linspace(3.0, 6.0, M_HEADS))
    return {
        'x_prompt': nrm(ks[0], (BATCH, SEQ, D_MODEL), 1.0),
        'x_sample': nrm(ks[1], (DEC_BATCH, DEC_SEQ, D_MODEL), 1.0),
        'state_mlstm_C': nrm(ks[2], (DEPTH, DEC_BATCH, M_HEADS, M_DQK, M_DV), 0.5),
        'state_mlstm_n': nrm(ks[3], (DEPTH, DEC_BATCH, M_HEADS, M_DQK), 0.5),
        'state_mlstm_m': nrm(ks[4], (DEPTH, DEC_BATCH, M_HEADS), 1.0),
        'state_swa_k': nrm(ks[5], (DEPTH, DEC_BATCH, WINDOW, S_KV_HEADS, S_HEAD_DIM), 1.0),
        'state_swa_v': nrm(ks[6], (DEPTH, DEC_BATCH, WINDOW, S_KV_HEADS, S_HEAD_DIM), 1.0),
        'p_prompt': nrm(ks[7], (DEPTH, BATCH, SEQ, D_PLE), 1.0),
        'p_sample': nrm(ks[8], (DEPTH, DEC_BATCH, DEC_SEQ, D_PLE), 1.0),
        'w_in': nrm(ks[9], (DEPTH, D_MODEL, D_IN), D_MODEL ** -0.5),
        'b_in': b_in,
        'mh_gain': 1.0 + nrm(ks[11], (DEPTH, M_HEADS * M_DV), 0.02),
        'w_a': nrm(ks[12], (DEPTH, M_HEADS * M_DV, D_MODEL), (M_HEADS * M_DV) ** -0.5),
        'w_b': nrm(ks[13], (DEPTH, S_HEADS * S_HEAD_DIM, D_MODEL), (S_HEADS * S_HEAD_DIM) ** -0.5),
        'w_out': nrm(ks[14], (DEPTH, D_MODEL, D_MODEL), D_MODEL ** -0.5 * DN_BETA),
        'rel_table': nrm(ks[15], (REL_BUCKETS, S_HEADS), 0.5),
        'w_sink': nrm(ks[16], (DEPTH, S_HEADS), 0.5),
        'ln_g': 1.0 + nrm(ks[17], (DEPTH, 3, D_MODEL), 0.02),
        'ln_b': nrm(ks[18], (DEPTH, 3, D_MODEL), 0.01),
        'w_rg': nrm(ks[19], (DEPTH, D_MODEL, N_GROUPS), D_MODEL ** -0.5),
        'b_rg': nrm(ks[20], (DEPTH, N_GROUPS), 0.01),
        'w_re': nrm(ks[21], (DEPTH, D_MODEL, N_EXPERTS), D_MODEL ** -0.5),
        'b_re': nrm(ks[22], (DEPTH, N_EXPERTS), 0.01),
        'w_eg': nrm(ks[23], (DEPTH, N_EXPERTS, D_MODEL, D_EXPERT), D_MODEL ** -0.5 * DN_BETA),
        'w_eu': nrm(ks[24], (DEPTH, N_EXPERTS, D_MODEL, D_EXPERT), D_MODEL ** -0.5 * DN_BETA),
        'w_ed': nrm(ks[25], (DEPTH, N_EXPERTS, D_EXPERT, D_MODEL), D_EXPERT ** -0.5 * DN_BETA),
        'w_pg': nrm(ks[26], (DEPTH, D_MODEL, D_MODEL), D_MODEL ** -0.5),
        'w_pp': nrm(ks[27], (DEPTH, D_PLE, D_MODEL), D_PLE ** -0.5 * DN_BETA),
    }


def reference(x_prompt, x_sample, state_mlstm_C, state_mlstm_n, state_mlstm_m, state_swa_k, state_swa_v,
              p_prompt, p_sample, w_in, b_in, mh_gain, w_a, w_b, w_out, rel_table, w_sink, ln_g, ln_b,
              w_rg, b_rg, w_re, b_re, w_eg, w_eu, w_ed, w_pg, w_pp):
    bp = x_prompt.shape[0]
    zC = jnp.zeros((bp, M_HEADS, M_DQK, M_DV), jnp.float32)
    zn = jnp.zeros((bp, M_HEADS, M_DQK), jnp.float32)
    zm = jnp.zeros((bp, M_HEADS), jnp.float32)
    xp, xs = x_prompt, x_sample
    new_p, new_s = [], []
    for i in range(DEPTH):
        wts = (w_in[i], b_in[i], mh_gain[i], w_a[i], w_b[i], w_out[i], rel_table, w_sink[i], ln_g[i], ln_b[i],
               w_rg[i], b_rg[i], w_re[i], b_re[i], w_eg[i], w_eu[i], w_ed[i], w_pg[i], w_pp[i])
        xp, sp = layer(xp, p_prompt[i], zC, zn, zm, None, None, *wts)
        xs, ss = layer(xs, p_sample[i], state_mlstm_C[i], state_mlstm_n[i], state_mlstm_m[i],
                       state_swa_k[i], state_swa_v[i], *wts)
        new_p.append(sp)
        new_s.append(ss)

    def stk(lst, j):
        return jnp.stack([s[j] for s in lst], axis=0)

    return (xp, xs,
            stk(new_p, 0), stk(new_p, 1), stk(new_p, 2), stk(new_p, 3), stk(new_p, 4),
            stk(new_s, 0), stk(new_s, 1), stk(new_s, 2), stk(new_s, 3), stk(new_s, 4))
```

```python
import contextlib
import numpy as np
import concourse.bass as bass
import concourse.mybir as mybir
from concourse.bass_utils import run_bass_kernel_spmd

F32 = mybir.dt.float32
BF16 = mybir.dt.bfloat16
AF = mybir.ActivationFunctionType
ALU = mybir.AluOpType

ENGS = ['pe', 'act', 'dve', 'pool', 'sp']
N_DMA_SEM = 8
SAME_ENGINE_SYNC = True

D = 1024
NT = 17
TP = 2048
TS = 64
TALL = TP + TS
DIN = 6664
O_MQ, O_MK, O_MV, O_OG, O_IG, O_FG, O_SQ, O_SK, O_SV, O_GA, O_GB = 0, 512, 1024, 2048, 3072, 3076, 3080, 4104, 4360, 4616, 5640
ALPHA = float((2 * 4) ** 0.25)
EPS = 1e-5
BIG = 30000.0
FW = 400
CH = 256
TPC = CH // 128


class Tok:
    __slots__ = ('w', 'rs', 'const')

    def __init__(self, const=False):
        self.w = None
        self.rs = []
        self.const = const


class Op:
    __slots__ = ('eng', 'fn', 'deps', 'sig', 'count', 'is_dma', 'dma_id', 'waits')


class Prog:
    def __init__(self, nc):
        self.nc = nc
        self.ops = {e: [] for e in ENGS}
        self.n_dma = 0
        self.dmas = []
        self.stack = contextlib.ExitStack()
        self._n = 0

    def sb(self, shape, dtype):
        self._n += 1
        return self.stack.enter_context(self.nc.sbuf_tensor(f"sb{self._n}", list(shape), dtype))

    def ps(self, shape, dtype=F32):
        self._n += 1
        return self.stack.enter_context(self.nc.psum_tensor(f"ps{self._n}", list(shape), dtype))

    def op(self, eng, fn, r=(), w=()):
        o = Op()
        o.eng = eng
        o.fn = fn
        o.deps = set()
        o.sig = False
        o.is_dma = False
        o.count = 0
        for t in r:
            if t.w is not None:
                o.deps.add(t.w)
        for t in w:
            if t.w is not None:
                o.deps.add(t.w)
            for x in t.rs:
                o.deps.add(x)
        for t in r:
            if not t.const:
                t.rs.append(o)
        for t in w:
            t.w = o
            t.rs = []
        o.deps.discard(o)
        self.ops[eng].append(o)
        return o

    def dma(self, fn, r=(), w=()):
        o = self.op('sp', fn, r, w)
        o.is_dma = True
        o.dma_id = self.n_dma
        if self.n_dma >= N_DMA_SEM:
            o.deps.add(self.dmas[self.n_dma - N_DMA_SEM])
        self.n_dma += 1
        self.dmas.append(o)
        return o

    def barrier(self):
        last = []
        for e in ENGS:
            if e == 'sp':
                continue
            if self.ops[e]:
                last.append(self.ops[e][-1])
        last += self.dmas[-N_DMA_SEM:]
        for e in ENGS:
            o = self.op(e, None)
            for d in last:
                if d is not o:
                    o.deps.add(d)

    def emit(self):
        nc = self.nc

        def skip(d, o):
            return d.eng == o.eng and (d.eng in ('pe', 'sp') or not SAME_ENGINE_SYNC)

        for e in ENGS:
            for o in self.ops[e]:
                for d in o.deps:
                    if d.is_dma or skip(d, o):
                        continue
                    d.sig = True
        for e in ENGS:
            c = 0
            for o in self.ops[e]:
                if o.sig and not o.is_dma and o.fn is not None:
                    c += 1
                o.count = c
        sems = {e: self.stack.enter_context(nc.semaphore(f"s_{e}")) for e in ENGS}
        dsems = [self.stack.enter_context(nc.semaphore(f"s_dma{i}")) for i in range(N_DMA_SEM)]

        def dma_target(d):
            return dsems[d.dma_id % N_DMA_SEM], 16 * (d.dma_id // N_DMA_SEM + 1)

        nwaits = 0
        for e in ENGS:
            waited = {}
            for o in self.ops[e]:
                need = {}
                for d in o.deps:
                    if d.is_dma:
                        s, v = dma_target(d)
                    else:
                        if skip(d, o):
                            continue
                        s, v = sems[d.eng], d.count
                    if need.get(s, 0) < v:
                        need[s] = v
                o.waits = []
                for s, v in need.items():
                    if waited.get(s, 0) < v:
                        waited[s] = v
                        o.waits.append((s, v))
                        nwaits += 1
        self.stats = {e: len(self.ops[e]) for e in ENGS}
        self.stats['waits'] = nwaits
        final_dma = {}
        for d in self.dmas:
            s, v = dma_target(d)
            final_dma[s] = max(final_dma.get(s, 0), v)

        def replay(ename, eng):
            for o in self.ops[ename]:
                for s, v in o.waits:
                    eng.wait_ge(s, v)
                if o.fn is None:
                    continue
                ins = o.fn(eng)
                if o.is_dma:
                    s, v = dma_target(o)
                    ins.then_inc(s, 16)
                elif o.sig:
                    ins.then_inc(sems[ename], 1)
            if ename == 'sp':
                for s, v in final_dma.items():
                    eng.wait_ge(s, v)

        with nc.Block() as block:
            @block.tensor
            def _(eng):
                replay('pe', eng)

            @block.scalar
            def _(eng):
                replay('act', eng)

            @block.vector
            def _(eng):
                replay('dve', eng)

            @block.gpsimd
            def _(eng):
                replay('pool', eng)

            @block.sync
            def _(eng):
                replay('sp', eng)
        self.stack.close()


def rel_bucket_np(dist):
    n = np.maximum(dist, 0)
    max_exact = 16
    large = max_exact + (np.log(np.maximum(n, 1) / max_exact) / np.log(128 / max_exact) * (32 - max_exact)).astype(np.int32)
    large = np.minimum(large, 31)
    return np.where(n < max_exact, n, large).astype(np.int32)


def make_consts():
    c = {}
    c['c_ident'] = np.eye(128, dtype=np.float32)
    s = np.arange(128)[:, None]
    l = np.arange(128)[None, :]
    c['c_maskbig'] = np.where(s > l, BIG, 0.0).astype(np.float32)
    i = np.arange(FW)
    dist = i - 144
    valid = (dist >= 0) & (dist <= 128)
    oh = np.zeros((32, FW), np.float32)
    b = rel_bucket_np(dist)
    oh[b[valid], i[valid]] = 1.0
    c['c_ohd'] = oh
    c['c_maskvec'] = np.where(valid, 0.0, -BIG).astype(np.float32)[None, :]
    sel = np.zeros((4, 4, 128), np.float32)
    for h in range(4):
        sel[h, h, :] = 1.0
    c['c_sel4'] = sel.reshape(4, 512)
    return c


def build(depth=4):
    nc = bass.Bass("TRN2", target_bir_lowering=False)
    P = Prog(nc)

    def din(name, shape):
        return nc.dram_tensor(name, list(shape), F32, kind="ExternalInput").ap()

    def dout(name, shape):
        return nc.dram_tensor(name, list(shape), F32, kind="ExternalOutput").ap()

    xin = din("xin", [TALL, D])
    pin = din("pin", [4, TALL, 256])
    sC = din("sC", [4, 16, 4, 128, 256])
    sn = din("sn", [4, 16, 4, 128])
    sm = din("sm", [4, 64])
    sk = din("sk", [4, 16, 128, 256])
    sv = din("sv", [4, 16, 128, 256])
    w_in = din("w_in", [4, D, DIN])
    b_in = din("b_in", [4, DIN])
    mh_gain = din("mh_gain", [4, D])
    w_a = din("w_a", [4, D, D])
    w_b = din("w_b", [4, D, D])
    w_out = din("w_out", [4, D, D])
    rel_table = din("rel_table", [32, 16])
    w_sink = din("w_sink", [4, 16])
    ln_g = din("ln_g", [4, 3, D])
    ln_b = din("ln_b", [4, 3, D])
    w_rg = din("w_rg", [4, D, 4])
    b_rg = din("b_rg", [4, 4])
    w_re = din("w_re", [4, D, 32])
    b_re = din("b_re", [4, 32])
    w_eg = din("w_eg", [4, 32, D, 512])
    w_eu = din("w_eu", [4, 32, D, 512])
    w_ed = din("w_ed", [4, 32, 512, D])
    w_pg = din("w_pg", [4, D, D])
    w_pp = din("w_pp", [4, 256, D])
    c_ident = din("c_ident", [128, 128])
    c_maskbig = din("c_maskbig", [128, 128])
    c_ohd = din("c_ohd", [32, FW])
    c_maskvec = din("c_maskvec", [1, FW])
    c_sel4 = din("c_sel4", [4, 512])

    o_y = dout("o_y", [TALL, D])
    o_Cp = dout("o_Cp", [4, 4, 128, 256])
    o_np = dout("o_np", [4, 4, 128])
    o_mp = dout("o_mp", [4, 4])
    o_kp = dout("o_kp", [4, 128, 256])
    o_vp = dout("o_vp", [4, 128, 256])
    o_Cs = dout("o_Cs", [4, 16, 4, 128, 256])
    o_ns = dout("o_ns", [4, 16, 4, 128])
    o_ms = dout("o_ms", [4, 64])
    o_ks = dout("o_ks", [4, 16, 128, 256])
    o_vs = dout("o_vs", [4, 16, 128, 256])
    scratch = nc.dram_tensor("scratch", [16, 128, FW], F32, kind="Internal").ap()

    def mm(out, lhsT, rhs, start, stop, r, w):
        P.op('pe', lambda e: e.matmul(out, lhsT=lhsT, rhs=rhs, start=start, stop=stop), r=r, w=w)

    def tr(out, in_, ident, r, w):
        P.op('pe', lambda e: e.transpose(out=out, in_=in_, identity=ident), r=r, w=w)

    def act(out, in_, func, r, w, bias=None, scale=1.0):
        if bias is None:
            P.op('act', lambda e: e.activation(out=out, in_=in_, func=func, scale=scale), r=r, w=w)
        else:
            P.op('act', lambda e: e.activation(out=out, in_=in_, func=func, bias=bias, scale=scale), r=r, w=w)

    def tt(eng, out, in0, in1, op, r, w):
        P.op(eng, lambda e: e.tensor_tensor(out=out, in0=in0, in1=in1, op=op), r=r, w=w)

    def ts(eng, out, in0, s1, op0, r, w, s2=None, op1=None):
        if op1 is None:
            P.op(eng, lambda e: e.tensor_scalar(out=out, in0=in0, scalar1=s1, scalar2=None, op0=op0), r=r, w=w)
        else:
            P.op(eng, lambda e: e.tensor_scalar(out=out, in0=in0, scalar1=s1, scalar2=s2, op0=op0, op1=op1), r=r, w=w)

    def stt(out, in0, scalar, in1, op0, op1, r, w, accum=None):
        if accum is None:
            P.op('dve', lambda e: e.scalar_tensor_tensor(out=out, in0=in0, scalar=scalar, in1=in1, op0=op0, op1=op1), r=r, w=w)
        else:
            P.op('dve', lambda e: e.scalar_tensor_tensor(out=out, in0=in0, scalar=scalar, in1=in1, op0=op0, op1=op1, accum_out=accum), r=r, w=w)

    def cp(eng, out, in_, r, w):
        if eng == 'act':
            P.op('act', lambda e: e.copy(out=out, in_=in_), r=r, w=w)
        else:
            P.op(eng, lambda e: e.tensor_copy(out=out, in_=in_), r=r, w=w)

    def dma(out, in_, r=(), w=(), slow=False):
        if slow:
            P.dma(lambda e: e.dma_start(out=out, in_=in_, allow_slow_non_contiguous=True), r=r, w=w)
        else:
            P.dma(lambda e: e.dma_start(out=out, in_=in_), r=r, w=w)

    def memset(eng, ap, val, w):
        P.op(eng, lambda e: e.memset(ap, val), w=w)

    class Rot:
        def __init__(self, bufs):
            self.bufs = bufs
            self.toks = [Tok() for _ in bufs]
            self.i = 0

        def next(self):
            k = self.i % len(self.bufs)
            self.i += 1
            return self.bufs[k], self.toks[k]

    banks = Rot([P.ps([128, 512], F32) for _ in range(8)])

    def psum():
        b, t = banks.next()
        return b, t

    x_tok = P.sb([128, NT, D], F32)
    t_x = [Tok() for _ in range(NT)]
    ident_f = P.sb([128, 128], F32)
    ident_b = P.sb([128, 128], BF16)
    ones_b = P.sb([128, 128], BF16)
    maskbig = P.sb([128, 128], F32)
    sel4 = P.sb([4, 4, 128], F32)
    EB = P.sb([128, 16, 2, 128], BF16)
    t_c = Tok()
    NSTG = 3
    stg = Rot([P.sb([128, 1024], F32) for _ in range(NSTG)])
    NRING = 4
    ring = Rot([P.sb([128, 4096], BF16) for _ in range(NRING)])
    gate_full = P.sb([128, NT, 32], F32)
    t_gate = [Tok() for _ in range(NT)]
    gbt = P.sb([128, 2, D], F32)
    t_gbt = Tok()
    bcol = P.sb([128, 48], F32)
    t_bcol = Tok()
    esink = P.sb([128, 16], F32)
    t_esink = Tok()
    brt = P.sb([128, 36], F32)
    t_brt = Tok()
    small = Rot([P.sb([128, 16], F32) for _ in range(6)])
    ARENA_W = 18700
    arena = P.sb([128, ARENA_W], F32)

    class Arena:
        def __init__(self):
            self.off = 0

        def reset(self):
            self.off = 0

        def alloc(self, shape, dtype):
            n = int(np.prod(shape[1:]))
            nw = n if dtype == F32 else (n + 1) // 2
            assert self.off + nw <= ARENA_W, (self.off, nw)
            v = arena[0:shape[0], self.off:self.off + nw]
            self.off += nw
            if dtype != F32:
                v = v.bitcast(dtype)
                if n % 2:
                    v = v[:, 0:n]
            if len(shape) > 2:
                names = " ".join(f"d{i}" for i in range(1, len(shape)))
                v = v.rearrange(f"p ({names}) -> p {names}", **{f"d{i}": shape[i] for i in range(1, len(shape) - 1)})
            return v

        def rot(self, n, shape, dtype):
            return Rot([self.alloc(shape, dtype) for _ in range(n)])

    AR = Arena()

    def wblock(srcs):
        slot, tok = ring.next()
        views = []
        off = 0
        for s in srcs:
            shp = list(s.shape)
            n = int(np.prod(shp[1:]))
            assert n <= 1024
            st, stok = stg.next()
            sv_ = st[0:shp[0], 0:n]
            dv = slot[0:shp[0], off:off + n]
            if len(shp) == 3:
                sv_ = sv_.rearrange("p (a b) -> p a b", a=shp[1])
                dv = dv.rearrange("p (a b) -> p a b", a=shp[1])
            dma(sv_, s, w=[stok])
            cp('pool', dv, sv_, r=[stok], w=[tok])
            views.append(dv)
            off += n
        return views, tok

    def wcols(w2d, c0, ncols, kcs=8):
        v = w2d[:, c0:c0 + ncols].rearrange("(kc p) n -> p kc n", p=128)
        per = max(1, 1024 // ncols)
        return [v[:, k0:min(k0 + per, kcs), :] for k0 in range(0, kcs, per)]

    def join_views(views):
        return views

    def wmat(w2d, c0, ncols, kcs=8):
        slot, tok = ring.next()
        v = w2d[:, c0:c0 + ncols].rearrange("(kc p) n -> p kc n", p=128)
        per = max(1, 1024 // ncols)
        full = slot[:, 0:kcs * ncols].rearrange("p (a b) -> p a b", a=kcs)
        for k0 in range(0, kcs, per):
            k1 = min(k0 + per, kcs)
            st, stok = stg.next()
            sv_ = st[:, 0:(k1 - k0) * ncols].rearrange("p (a b) -> p a b", a=k1 - k0)
            dma(sv_, v[:, k0:k1, :], w=[stok])
            cp('pool', full[:, k0:k1, :], sv_, r=[stok], w=[tok])
        return full, tok

    def wmulti(parts):
        slot, tok = ring.next()
        tot = sum(p[2] for p in parts)
        assert 8 * tot <= 4096
        full = slot[:, 0:8 * tot].rearrange("p (a b) -> p a b", a=8)
        o = 0
        for (w2d, c0, ncols) in parts:
            v = w2d[:, c0:c0 + ncols].rearrange("(kc p) n -> p kc n", p=128)
            per = max(1, 1024 // ncols)
            for k0 in range(0, 8, per):
                k1 = min(k0 + per, 8)
                st, stok = stg.next()
                sv_ = st[:, 0:(k1 - k0) * ncols].rearrange("p (a b) -> p a b", a=k1 - k0)
                dma(sv_, v[:, k0:k1, :], w=[stok])
                cp('pool', full[:, k0:k1, o:o + ncols], sv_, r=[stok], w=[tok])
            o += ncols
        return full, tok

    dma(ident_f[:], c_ident, w=[t_c])
    dma(maskbig[:], c_maskbig, w=[t_c])
    dma(sel4[:].rearrange("p a b -> p (a b)"), c_sel4, w=[t_c])
    cp('dve', ident_b[:], ident_f[:], r=[t_c], w=[t_c])
    memset('dve', ones_b[:], 1.0, w=[t_c])
    AR.reset()
    rt = AR.alloc([32, 16], F32)
    ohd = AR.alloc([32, FW], F32)
    mvec = AR.alloc([1, FW], F32)
    one1 = AR.alloc([1, 16], F32)
    fsb = AR.alloc([16, FW], F32)
    t_tmp = Tok()
    dma(rt, rel_table, w=[t_tmp])
    dma(ohd, c_ohd, w=[t_tmp])
    dma(mvec, c_maskvec, w=[t_tmp])
    memset('dve', one1, 1.0, w=[t_tmp])
    pb, pt_ = psum()
    mm(pb[0:16, 0:FW], rt, ohd, True, False, r=[t_tmp], w=[pt_])
    mm(pb[0:16, 0:FW], one1, mvec, False, True, r=[t_tmp], w=[pt_])
    t_f = Tok()
    cp('dve', fsb, pb[0:16, 0:FW], r=[pt_], w=[t_f])
    t_scr = Tok()
    dma(scratch, fsb.unsqueeze(1).to_broadcast([16, 128, FW]), r=[t_f], w=[t_scr])
    btmp = AR.rot(3, [128, 128], F32)
    for h in range(16):
        for kind, cc in ((0, 144), (1, 272)):
            base = scratch[h, 0, cc:cc + 128]
            src = bass.AP(tensor=base.tensor, offset=base.offset, ap=[[FW - 1, 128], [1, 128]])
            bt_, btk = btmp.next()
            dma(bt_, src, r=[t_scr], w=[btk])
            act(EB[:, h, kind, :], bt_, AF.Exp, r=[btk], w=[t_c])
    t_c.const = True
    P.barrier()

    dma(x_tok[:, 0:8, :], xin[0:1024, :].rearrange("(j p) d -> p j d", p=128), w=t_x[0:8])
    dma(x_tok[:, 8:16, :], xin[1024:2048, :].rearrange("(j p) d -> p j d", p=128), w=t_x[8:16])
    dma(x_tok[0:64, 16, :], xin[2048:2112, :], w=[t_x[16]])

    def tile_rows(j):
        return 64 if j == 16 else 128

    def transpose_tiles(dst, t_dst, tiles, col0):
        k = 0
        for j in tiles:
            n = tile_rows(j)
            c = col0 + (j - tiles[0]) * 128
            for half in range(2):
                pb, ptk = psum()
                for q in range(4):
                    kc = half * 4 + q
                    tr(pb[:, q * 128:q * 128 + n], x_tok[0:n, j, kc * 128:(kc + 1) * 128], ident_f[0:n, 0:n], r=[t_x[j], t_c], w=[ptk])
                src = pb[:, :].rearrange("p (a b) -> p a b", a=4)[:, :, 0:n]
                cp('act' if k % 2 == 0 else 'dve', dst[:, half * 4:half * 4 + 4, c:c + n], src, r=[ptk], w=[t_dst])
                k += 1

    def layer_norm(j, l, idx, t_g):
        n = tile_rows(j)
        xs = x_tok[0:n, j, :]
        s_, st_ = small.next()
        P.op('dve', lambda e: e.bn_stats(out=s_[0:n, 0:6], in_=x_tok[0:n, j, 0:512]), r=[t_x[j]], w=[st_])
        P.op('dve', lambda e: e.bn_stats(out=s_[0:n, 6:12], in_=x_tok[0:n, j, 512:1024]), r=[t_x[j]], w=[st_])
        P.op('dve', lambda e: e.bn_aggr(out=s_[0:n, 12:14], in_=s_[0:n, 0:12]), r=[st_], w=[st_])
        ts('dve', s_[0:n, 14:15], s_[0:n, 13:14], EPS, ALU.add, r=[st_], w=[st_])
        act(s_[0:n, 14:15], s_[0:n, 14:15], AF.Sqrt, r=[st_], w=[st_])
        P.op('dve', lambda e: e.reciprocal(out=s_[0:n, 15:16], in_=s_[0:n, 14:15]), r=[st_], w=[st_])
        ts('dve', xs, xs, s_[0:n, 12:13], ALU.subtract, r=[t_x[j], st_], w=[t_x[j]], s2=s_[0:n, 15:16], op1=ALU.mult)
        tt('pool', xs, xs, gbt[0:n, 0, :], ALU.mult, r=[t_x[j], t_g], w=[t_x[j]])
        tt('pool', xs, xs, gbt[0:n, 1, :], ALU.add, r=[t_x[j], t_g], w=[t_x[j]])

    def load_gb(l, idx):
        dma(gbt[:, 0, :], ln_g[l, idx:idx + 1, :].to_broadcast([128, D]), w=[t_gbt])
        dma(gbt[:, 1, :], ln_b[l, idx:idx + 1, :].to_broadcast([128, D]), w=[t_gbt])

    SL_MQ, SL_MK, SL_IG, SL_FG, SL_SQ, SL_SK, SL_GA, SL_GB = 0, 4, 8, 9, 10, 26, 30, 38

    def load_bcol(l):
        def col(slot, c0, n):
            dma(bcol[0:n, slot:slot + 1], b_in[l, c0:c0 + n].unsqueeze(1), w=[t_bcol])
        for h in range(4):
            col(SL_MQ + h, O_MQ + h * 128, 128)
            col(SL_MK + h, O_MK + h * 128, 128)
        col(SL_IG, O_IG, 4)
        col(SL_FG, O_FG, 4)
        for hh in range(16):
            col(SL_SQ + hh, O_SQ + hh * 64, 64)
        for g in range(4):
            col(SL_SK + g, O_SK + g * 64, 64)
        for m in range(8):
            col(SL_GA + m, O_GA + m * 128, 128)
            col(SL_GB + m, O_GB + m * 128, 128)

    for l in range(depth):
        AR.reset()
        load_bcol(l)
        dma(esink[:], w_sink[l:l + 1, :].to_broadcast([128, 16]), w=[t_esink])
        act(esink[:], esink[:], AF.Exp, r=[t_esink], w=[t_esink])
        gain_r = AR.rot(2, [128, 256], F32)
        xT_c = AR.alloc([128, 8, CH], BF16)
        t_xT = Tok()
        yaT = AR.alloc([128, 8, CH], BF16)
        t_yaT = Tok()
        ybT = AR.alloc([128, 8, CH], BF16)
        t_ybT = Tok()
        merged = AR.alloc([128, 8, CH], BF16)
        t_mg = Tok()
        fgr = AR.alloc([4, CH], F32)
        Brow = AR.alloc([4, CH], F32)
        Urow = AR.alloc([4, CH], F32)
        Arow = AR.alloc([4, CH], F32)
        onec = AR.alloc([4, 2], F32)
        t_rows = Tok()
        memset('dve', onec, 1.0, w=[t_rows])
        ones_row = onec[:, 0:1].to_broadcast([4, CH])
        carry = AR.alloc([4, 2], F32)
        t_carry = Tok()
        cols = AR.alloc([128, TPC, 8], F32)
        t_cols = Tok()
        cols_s = AR.alloc([4, 16, 8], F32)
        m0row = AR.alloc([4, 16], F32)
        m0rep = AR.alloc([128, 64], F32)
        t_m0 = Tok()
        msrow = AR.alloc([4, 16], F32)
        acs_c = AR.alloc([128, 4], F32)
        t_acs = Tok()
        memset('dve', acs_c, 0.0, w=[t_acs])
        ArepU = AR.rot(1, [128, CH], F32)
        qT_r = AR.rot(1, [128, CH], BF16)
        kT_r = AR.rot(1, [128, CH], BF16)
        bt_r = AR.rot(2, [128, 512], F32)
        vaug_r = AR.rot(2, [128, 257], BF16)
        for vb_ in vaug_r.bufs:
            memset('pool', vb_[:, 256:257], 1.0, w=[Tok()])
        ogt_r = AR.rot(2, [128, 256], F32)
        AM_r = AR.rot(2, [128, 128], F32)
        WT_r = AR.rot(2, [128, 128], F32)
        Wi_r = AR.rot(2, [128, 128], F32)
        STw_r = AR.rot(2, [128, 128], BF16)
        qw_r = AR.rot(2, [128, 128], BF16)
        kw_r = AR.rot(2, [128, 128], BF16)
        ya_r = AR.rot(2, [128, 256], F32)
        C_f = AR.alloc([128, 4, 257], F32)
        C_b = AR.alloc([128, 4, 257], BF16)
        t_C = [Tok() for _ in range(4)]
        for h in range(4):
            memset('pool', C_f[:, h, :], 0.0, w=[t_C[h]])
            memset('pool', C_b[:, h, :], 0.0, w=[t_C[h]])
        Cs_f = AR.rot(2, [128, 257], F32)
        Cs_b = AR.rot(2, [128, 257], BF16)
        Cs_o = AR.rot(2, [128, 257], F32)
        qTg = AR.alloc([64, 4, CH], BF16)
        t_qTg = Tok()
        kTg = AR.alloc([64, 4, CH + 128], BF16)
        t_kTg = [Tok() for _ in range(4)]
        vtd = AR.alloc([128, TPC + 1, 512], BF16)
        t_vtd = [Tok() for _ in range(TPC + 1)]
        T512 = AR.rot(4, [128, 512], F32)
        kvo_r = E_r = rd_r = sg_r = T512
        PT_r = AR.rot(3, [128, 512], BF16)
        kbuf_r = AR.rot(2, [128, 64], F32)
        vbuf_r = AR.rot(2, [128, 64], F32)
        kbT_r = AR.rot(2, [64, 1, 128], BF16)
        vbd_r = AR.rot(2, [128, 256], BF16)

        def mlstm_core(n, h, kTv, qTv, ArU, Ucol, emcol, Acs, Ace, vaug, gs, t_in, Cb_ap, Cf_ap, Cout_ap, t_Cst, t_Cout, yaT_dst):
            pS, tS = psum()
            mm(pS[0:n, 0:n], kTv, qTv, True, True, r=t_in, w=[tS])
            AM, tAM = AM_r.next()
            tt('pool', AM[0:n, 0:n], ArU[0:n, :], maskbig[0:n, 0:n], ALU.add, r=t_in + [t_c], w=[tAM])
            WT, tWT = WT_r.next()
            act(WT[0:n, 0:n], AM[0:n, 0:n], AF.Exp, r=[tAM] + t_in, w=[tWT], bias=Ucol, scale=-1.0)
            STw, tST = STw_r.next()
            tt('dve', STw[0:n, 0:n], pS[0:n, 0:n], WT[0:n, 0:n], ALU.mult, r=[tS, tWT], w=[tST])
            Wi, tWi = Wi_r.next()
            act(Wi[:, 0:n], ArU, AF.Exp, r=t_in, w=[tWi], bias=Acs, scale=-1.0)
            qw, tqw = qw_r.next()
            tt('pool', qw[:, 0:n], qTv, Wi[:, 0:n], ALU.mult, r=t_in + [tWi], w=[tqw])
            pN, tN = psum()
            mm(pN[0:n, 0:257], STw[0:n, 0:n], vaug, True, False, r=[tST] + t_in, w=[tN])
            mm(pN[0:n, 0:257], qw[:, 0:n], Cb_ap, False, True, r=[tqw, t_Cst], w=[tN])
            s_, st_ = small.next()
            act(s_[0:n, 0:1], pN[0:n, 256:257], AF.Abs, r=[tN], w=[st_])
            ts('dve', s_[0:n, 1:2], s_[0:n, 0:1], emcol, ALU.max, r=[st_] + t_in, w=[st_])
            P.op('dve', lambda e: e.bn_stats(out=s_[0:n, 2:8], in_=pN[0:n, 0:256]), r=[tN], w=[st_])
            P.op('dve', lambda e: e.bn_aggr(out=s_[0:n, 8:10], in_=s_[0:n, 2:8]), r=[st_], w=[st_])
            ts('dve', s_[0:n, 10:11], s_[0:n, 1:2], s_[0:n, 1:2], ALU.mult, r=[st_], w=[st_], s2=EPS, op1=ALU.mult)
            tt('dve', s_[0:n, 10:11], s_[0:n, 10:11], s_[0:n, 9:10], ALU.add, r=[st_], w=[st_])
            act(s_[0:n, 10:11], s_[0:n, 10:11], AF.Sqrt, r=[st_], w=[st_])
            P.op('dve', lambda e: e.reciprocal(out=s_[0:n, 11:12], in_=s_[0:n, 10:11]), r=[st_], w=[st_])
            ya, tya = ya_r.next()
            ts('dve', ya[0:n, :], pN[0:n, 0:256], s_[0:n, 8:9], ALU.subtract, r=[tN, st_], w=[tya], s2=s_[0:n, 11:12], op1=ALU.mult)
            tt('pool', ya[0:n, :], ya[0:n, :], gs, ALU.mult, r=[tya] + t_in, w=[tya])
            pY, tY = psum()
            for vc in range(2):
                tr(pY[:, vc * 128:vc * 128 + n], ya[0:n, vc * 128:(vc + 1) * 128], ident_f[0:n, 0:n], r=[tya, t_c], w=[tY])
            cp('act', yaT_dst, pY[:, 0:256].rearrange("p (a b) -> p a b", a=2)[:, :, 0:n], r=[tY], w=[t_yaT])
            s2_, st2 = small.next()
            ts('dve', s2_[0:n, 0:1], Ucol, Ace[0:n, :], ALU.subtract, r=t_in, w=[st2])
            act(s2_[0:n, 0:1], s2_[0:n, 0:1], AF.Exp, r=[st2], w=[st2])
            tt('dve', s2_[:, 1:2], Acs, Ace, ALU.subtract, r=t_in, w=[st2])
            act(s2_[:, 1:2], s2_[:, 1:2], AF.Exp, r=[st2], w=[st2])
            pK, tK = psum()
            mm(pK[0:n, 0:128], kTv, ident_b[:], True, True, r=t_in + [t_c], w=[tK])
            kw, tkw = kw_r.next()
            ts('dve', kw[0:n, :], pK[0:n, 0:128], s2_[0:n, 0:1], ALU.mult, r=[tK, st2], w=[tkw])
            pU, tU = psum()
            mm(pU[:, 0:257], kw[0:n, :], vaug, True, True, r=[tkw] + t_in, w=[tU])
            stt(Cout_ap, Cf_ap, s2_[:, 1:2], pU[:, 0:257], ALU.mult, ALU.add, r=[t_Cst, st2, tU], w=[t_Cout])

        NPC = TP // CH
        chunks = [(c, CH) for c in range(NPC)] + [(NPC, 64)]
        for (c, N) in chunks:
            is_s = (c == NPC)
            tiles = [16] if is_s else list(range(TPC * c, TPC * c + TPC))
            transpose_tiles(xT_c, t_xT, tiles, 0)
            wg_v, wg_t = wblock([w_in[l][:, O_IG:O_IG + 8].rearrange("(kc p) n -> p kc n", p=128)])
            wg = wg_v[0]
            pI, tI = psum()
            pF, tF = psum()
            for kc in range(8):
                mm(pI[0:4, 0:N], wg[:, kc, 0:4], xT_c[:, kc, 0:N], kc == 0, kc == 7, r=[wg_t, t_xT], w=[tI])
            for kc in range(8):
                mm(pF[0:4, 0:N], wg[:, kc, 4:8], xT_c[:, kc, 0:N], kc == 0, kc == 7, r=[wg_t, t_xT], w=[tF])
            act(Urow[:, 0:N], pI[0:4, 0:N], AF.Identity, r=[tI, t_bcol], w=[t_rows], bias=bcol[0:4, SL_IG:SL_IG + 1])
            act(fgr[:, 0:N], pF[0:4, 0:N], AF.Identity, r=[tF, t_bcol], w=[t_rows], bias=bcol[0:4, SL_FG:SL_FG + 1])
            act(fgr[:, 0:N], fgr[:, 0:N], AF.Exp, r=[t_rows], w=[t_rows], scale=-1.0)
            act(fgr[:, 0:N], fgr[:, 0:N], AF.Ln, r=[t_rows], w=[t_rows], bias=1.0)
            if not is_s:
                if c == 0:
                    P.op('dve', lambda e: e.tensor_tensor_scan(out=Brow[:, :], data0=ones_row[:, :], data1=fgr[:, :], initial=0.0, op0=ALU.mult, op1=ALU.subtract), r=[t_rows], w=[t_rows])
                else:
                    P.op('dve', lambda e: e.tensor_tensor_scan(out=Brow[:, :], data0=ones_row[:, :], data1=fgr[:, :], initial=carry[:, 0:1], op0=ALU.mult, op1=ALU.subtract), r=[t_rows, t_carry], w=[t_rows])
                tt('dve', Urow[:, :], Urow[:, :], Brow[:, :], ALU.subtract, r=[t_rows], w=[t_rows])
                if c == 0:
                    P.op('dve', lambda e: e.tensor_tensor_scan(out=Arow[:, :], data0=Urow[:, :], data1=Urow[:, :], initial=0.0, op0=ALU.max, op1=ALU.max), r=[t_rows], w=[t_rows])
                else:
                    P.op('dve', lambda e: e.tensor_tensor_scan(out=Arow[:, :], data0=Urow[:, :], data1=Urow[:, :], initial=carry[:, 1:2], op0=ALU.max, op1=ALU.max), r=[t_rows, t_carry], w=[t_rows])
                cp('dve', carry[:, 0:1], Brow[:, CH - 1:CH], r=[t_rows], w=[t_carry])
                cp('dve', carry[:, 1:2], Arow[:, CH - 1:CH], r=[t_rows], w=[t_carry])
            else:
                dma(m0row, sm[l].rearrange("(s h) -> h s", h=4), w=[t_m0], slow=True)
                dma(m0rep, sm[l:l + 1, :].to_broadcast([128, 64]), w=[t_m0])
                f3 = fgr[:, 0:64].rearrange("p (s t) -> p s t", t=4)
                B3 = Brow[:, 0:64].rearrange("p (s t) -> p s t", t=4)
                U3 = Urow[:, 0:64].rearrange("p (s t) -> p s t", t=4)
                A3 = Arow[:, 0:64].rearrange("p (s t) -> p s t", t=4)
                ts('dve', B3[:, :, 0], f3[:, :, 0], -1.0, ALU.mult, r=[t_rows], w=[t_rows])
                for t in range(1, 4):
                    tt('dve', B3[:, :, t], B3[:, :, t - 1], f3[:, :, t], ALU.subtract, r=[t_rows], w=[t_rows])
                tt('dve', Urow[:, 0:64], Urow[:, 0:64], Brow[:, 0:64], ALU.subtract, r=[t_rows], w=[t_rows])
                tt('dve', A3[:, :, 0], U3[:, :, 0], m0row[:, :], ALU.max, r=[t_rows, t_m0], w=[t_rows])
                for t in range(1, 4):
                    tt('dve', A3[:, :, t], A3[:, :, t - 1], U3[:, :, t], ALU.max, r=[t_rows], w=[t_rows])
                tt('dve', msrow[:, :], A3[:, :, 3], B3[:, :, 3], ALU.add, r=[t_rows], w=[t_m0])
                dma(o_ms[l].rearrange("(s h) -> h s", h=4), msrow[:, :], r=[t_m0], slow=True)
            stt(fgr[:, 0:N], Arow[:, 0:N], -1.0, Brow[:, 0:N], ALU.mult, ALU.subtract, r=[t_rows], w=[t_rows])
            pC, tC = psum()
            if not is_s:
                for jj in range(TPC):
                    tr(pC[:, jj * 8:jj * 8 + 4], Urow[:, jj * 128:(jj + 1) * 128], ident_f[0:4, 0:4], r=[t_rows, t_c], w=[tC])
                    tr(pC[:, jj * 8 + 4:jj * 8 + 8], fgr[:, jj * 128:(jj + 1) * 128], ident_f[0:4, 0:4], r=[t_rows, t_c], w=[tC])
                cp('dve', cols[:, :, 0:4], pC[:, 0:8 * TPC].rearrange("p (a b) -> p a b", a=TPC)[:, :, 0:4], r=[tC], w=[t_cols])
                act(cols[:, :, 4:8], pC[:, 0:8 * TPC].rearrange("p (a b) -> p a b", a=TPC)[:, :, 4:8], AF.Exp, r=[tC], w=[t_cols])
            else:
                for s in range(16):
                    tr(pC[0:4, s * 8:s * 8 + 4], Urow[:, s * 4:(s + 1) * 4], ident_f[0:4, 0:4], r=[t_rows, t_c], w=[tC])
                    tr(pC[0:4, s * 8 + 4:s * 8 + 8], fgr[:, s * 4:(s + 1) * 4], ident_f[0:4, 0:4], r=[t_rows, t_c], w=[tC])
                cp('dve', cols_s[:, :, 0:4], pC[0:4, 0:128].rearrange("p (a b) -> p a b", a=16)[:, :, 0:4], r=[tC], w=[t_cols])
                act(cols_s[:, :, 4:8], pC[0:4, 0:128].rearrange("p (a b) -> p a b", a=16)[:, :, 4:8], AF.Exp, r=[tC], w=[t_cols])

            for h in range(4):
                ArU, tAr = ArepU.next()
                pA, tA = psum()
                mm(pA[:, 0:N], sel4[:, h, :], Arow[:, 0:N], True, True, r=[t_c, t_rows], w=[tA])
                cp('act', ArU[:, 0:N], pA[:, 0:N], r=[tA], w=[tAr])
                wA, wA_t = wmulti([(w_in[l], O_MQ + h * 128, 128), (w_in[l], O_MK + h * 128, 128)])
                wB, wB_t = wmulti([(w_in[l], O_MV + h * 256, 256), (w_in[l], O_OG + h * 256, 256)])
                qT, tq = qT_r.next()
                kT, tk = kT_r.next()
                pq, tpq = psum()
                for kc in range(8):
                    mm(pq[:, 0:N], wA[:, kc, 0:128], xT_c[:, kc, 0:N], kc == 0, kc == 7, r=[wA_t, t_xT], w=[tpq])
                ts('dve', qT[:, 0:N], pq[:, 0:N], bcol[:, SL_MQ + h:SL_MQ + h + 1], ALU.add, r=[tpq, t_bcol], w=[tq], s2=float(128 ** -0.5), op1=ALU.mult)
                pk, tpk = psum()
                for kc in range(8):
                    mm(pk[:, 0:N], wA[:, kc, 128:256], xT_c[:, kc, 0:N], kc == 0, kc == 7, r=[wA_t, t_xT], w=[tpk])
                act(kT[:, 0:N], pk[:, 0:N], AF.Identity, r=[tpk, t_bcol], w=[tk], bias=bcol[:, SL_MK + h:SL_MK + h + 1])
                gain, t_gain = gain_r.next()
                dma(gain[:, :], mh_gain[l:l + 1, h * 256:(h + 1) * 256].to_broadcast([128, 256]), w=[t_gain])
                bt_, tbt = bt_r.next()
                dma(bt_[:, 0:256], b_in[l:l + 1, O_MV + h * 256:O_MV + (h + 1) * 256].to_broadcast([128, 256]), w=[tbt])
                dma(bt_[:, 256:512], b_in[l:l + 1, O_OG + h * 256:O_OG + (h + 1) * 256].to_broadcast([128, 256]), w=[tbt])
                units = [(jj, 128, jj * 128) for jj in range(TPC)] if not is_s else [(s, 4, s * 4) for s in range(16)]
                for (u, n, c0) in units:
                    cs = slice(c0, c0 + n)
                    pv, tpv = psum()
                    for kc in range(8):
                        mm(pv[0:n, :], xT_c[:, kc, cs], wB[:, kc, :], kc == 0, kc == 7, r=[wB_t, t_xT], w=[tpv])
                    va, tva = vaug_r.next()
                    tt('dve', va[0:n, 0:256], pv[0:n, 0:256], bt_[0:n, 0:256], ALU.add, r=[tpv, tbt], w=[tva])
                    og, tog = ogt_r.next()
                    tt('dve', og[0:n, :], pv[0:n, 256:512], bt_[0:n, 256:512], ALU.add, r=[tpv, tbt], w=[tog])
                    act(og[0:n, :], og[0:n, :], AF.Sigmoid, r=[tog], w=[tog])
                    tt('pool', og[0:n, :], og[0:n, :], gain[0:n, :], ALU.mult, r=[tog, t_gain], w=[tog])
                    t_in = [tq, tk, tAr, t_cols, tva, tog, t_acs, t_m0]
                    yaT_dst = yaT[:, 2 * h:2 * h + 2, cs]
                    if not is_s:
                        Acs = acs_c[:, h:h + 1] if u == 0 else ArU[:, c0 - 1:c0]
                        Ace = ArU[:, c0 + n - 1:c0 + n]
                        mlstm_core(n, h, kT[:, cs], qT[:, cs], ArU[:, cs], cols[:, u, h:h + 1], cols[:, u, 4 + h:5 + h],
                                   Acs, Ace, va[0:n, :], og[0:n, :], t_in, C_b[:, h, :], C_f[:, h, :], C_f[:, h, :], t_C[h], t_C[h], yaT_dst)
                        cp('act', C_b[:, h, :], C_f[:, h, :], r=[t_C[h]], w=[t_C[h]])
                    else:
                        cf, tcf = Cs_f.next()
                        cb, tcb = Cs_b.next()
                        co, tco = Cs_o.next()
                        dma(cf[:, 0:256], sC[l, u, h], w=[tcf])
                        dma(cf[:, 256:257], sn[l, u, h, :].unsqueeze(1), w=[tcf])
                        cp('pool', cb[:, :], cf[:, :], r=[tcf], w=[tcb])
                        Acs = m0rep[:, u * 4 + h:u * 4 + h + 1]
                        Ace = ArU[:, c0 + n - 1:c0 + n]
                        mlstm_core(n, h, kT[:, cs], qT[:, cs], ArU[:, cs], cols_s[0:4, u, h:h + 1], cols_s[0:4, u, 4 + h:5 + h],
                                   Acs, Ace, va[0:n, :], og[0:n, :], t_in + [tcb], cb[:, :], cf[:, :], co[:, :], tcf, tco, yaT_dst)
                        dma(o_Cs[l, u, h], co[:, 0:256], r=[tco])
                        dma(o_ns[l, u, h, :].unsqueeze(1), co[:, 256:257], r=[tco])
                if not is_s:
                    cp('dve', acs_c[:, h:h + 1], ArU[:, CH - 1:CH], r=[tAr], w=[t_acs])
                    if c == NPC - 1:
                        dma(o_Cp[l, h], C_f[:, h, 0:256], r=[t_C[h]])
                        dma(o_np[l, h, :].unsqueeze(1), C_f[:, h, 256:257], r=[t_C[h]])
            if c == NPC - 1:
                s_, st_ = small.next()
                tt('dve', s_[0:4, 0:1], carry[:, 0:1], carry[:, 1:2], ALU.add, r=[t_carry], w=[st_])
                dma(o_mp[l, :].unsqueeze(1), s_[0:4, 0:1], r=[st_])

            wKV, wKV_t = wmulti([(w_in[l], O_SK, 256), (w_in[l], O_SV, 256)])
            btk_, tbtk = bt_r.next()
            dma(btk_[:, :], b_in[l:l + 1, O_SK:O_SK + 512].to_broadcast([128, 512]), w=[tbtk])
            if not is_s:
                for jj in range(TPC):
                    j = TPC * c + jj
                    pkv, tpkv = psum()
                    for kc in range(8):
                        mm(pkv[:, :], xT_c[:, kc, jj * 128:(jj + 1) * 128], wKV[:, kc, :], kc == 0, kc == 7, r=[wKV_t, t_xT], w=[tpkv])
                    dst = vtd[:, jj + 1, :].rearrange("p (g u d) -> p g u d", g=4, u=2)
                    src = pkv[:, 256:512].rearrange("p (g d) -> p g d", g=4).unsqueeze(2).to_broadcast([128, 4, 2, 64])
                    bsrc = btk_[:, 256:512].rearrange("p (g d) -> p g d", g=4).unsqueeze(2).to_broadcast([128, 4, 2, 64])
                    tt('dve', dst, src, bsrc, ALU.add, r=[tpkv, tbtk], w=[t_vtd[jj + 1]])
                    if j == 15:
                        kvo, tkvo = kvo_r.next()
                        tt('dve', kvo[:, :], pkv[:, :], btk_[:, :], ALU.add, r=[tpkv, tbtk], w=[tkvo])
                        dma(o_kp[l], kvo[:, 0:256], r=[tkvo])
                        dma(o_vp[l], kvo[:, 256:512], r=[tkvo])
            else:
                dma(o_ks[l, :, 0:124, :], sk[l, :, 4:128, :])
                dma(o_vs[l, :, 0:124, :], sv[l, :, 4:128, :])
            for g in range(4):
                wS, wS_t = wmulti([(w_in[l], O_SQ + g * 256, 256), (w_in[l], O_SK + g * 64, 64)])
                for i in range(4):
                    pq, tpq = psum()
                    for kc in range(8):
                        mm(pq[0:64, 0:N], wS[:, kc, i * 64:(i + 1) * 64], xT_c[:, kc, 0:N], kc == 0, kc == 7, r=[wS_t, t_xT], w=[tpq])
                    act(qTg[:, i, 0:N], pq[0:64, 0:N], AF.Identity, r=[tpq, t_bcol], w=[t_qTg], bias=bcol[0:64, SL_SQ + 4 * g + i:SL_SQ + 4 * g + i + 1])
                pk, tpk = psum()
                for kc in range(8):
                    mm(pk[0:64, 0:N], wS[:, kc, 256:320], xT_c[:, kc, 0:N], kc == 0, kc == 7, r=[wS_t, t_xT], w=[tpk])
                act(kTg[:, g, 128:128 + N], pk[0:64, 0:N], AF.Identity, r=[tpk, t_bcol], w=[t_kTg[g]], bias=bcol[0:64, SL_SK + g:SL_SK + g + 1])
                if not is_s:
                    for jj in range(TPC):
                        j = TPC * c + jj
                        cs = slice(jj * 128, (jj + 1) * 128)
                        pNm, tNm = psum()
                        pDn, tDn = psum()
                        kts = ([(1, jj)] if j > 0 else []) + [(0, jj + 1)]
                        for ki, (kind, slot) in enumerate(kts):
                            pS, tS = psum()
                            mm(pS[:, :].rearrange("p (a b) -> p a b", a=4), kTg[:, g, slot * 128:(slot + 1) * 128], qTg[:, :, cs], True, True, r=[t_kTg[g], t_qTg], w=[tS])
                            E_, tE = E_r.next()
                            act(E_[:, :], pS[:, :], AF.Exp, r=[tS], w=[tE], scale=0.125)
                            PT, tPT = PT_r.next()
                            tt('dve', PT[:, :].rearrange("p (a b) -> p a b", a=4), E_[:, :].rearrange("p (a b) -> p a b", a=4), EB[:, 4 * g:4 * g + 4, kind, :], ALU.mult, r=[tE, t_c], w=[tPT])
                            first = ki == 0
                            last = ki == len(kts) - 1
                            mm(pNm[:, :], vtd[:, slot, g * 128:(g + 1) * 128], PT[:, :], first, last, r=[t_vtd[slot], tPT], w=[tNm])
                            mm(pDn[:, :], ones_b[:, :], PT[:, :], first, last, r=[tPT, t_c], w=[tDn])
                        rd, trd = rd_r.next()
                        for i in range(4):
                            ts('dve', rd[:, i * 128:(i + 1) * 128], pDn[:, i * 128:(i + 1) * 128], esink[:, 4 * g + i:4 * g + i + 1], ALU.add, r=[tDn, t_esink], w=[trd])
                        P.op('dve', lambda e, rd=rd: e.reciprocal(out=rd[:, :], in_=rd[:, :]), r=[trd], w=[trd])
                        for par in range(2):
                            ps_ = slice(par * 64, (par + 1) * 64)
                            nv = pNm[ps_, :].rearrange("p (a b) -> p a b", a=4)[:, par::2, :]
                            rv = rd[ps_, :].rearrange("p (a b) -> p a b", a=4)[:, par::2, :]
                            tt('dve', ybT[ps_, 2 * g:2 * g + 2, cs], nv, rv, ALU.mult, r=[tNm, trd], w=[t_ybT])
                else:
                    for s in range(16):
                        cs = slice(4 * s, 4 * s + 4)
                        kb, tkb = kbuf_r.next()
                        vb, tvb = vbuf_r.next()
                        dma(kb[:, :], sk[l, s, :, g * 64:(g + 1) * 64], w=[tkb])
                        dma(vb[:, :], sv[l, s, :, g * 64:(g + 1) * 64], w=[tvb])
                        pT_, tT_ = psum()
                        tr(pT_[0:64, 0:128], kb[:, :], ident_f[:, :], r=[tkb, t_c], w=[tT_])
                        kbT, tkbT = kbT_r.next()
                        cp('act', kbT[:, 0, :], pT_[0:64, 0:128], r=[tT_], w=[tkbT])
                        vbd, tvbd = vbd_r.next()
                        cp('pool', vbd[:, 0:128].rearrange("p (u d) -> p u d", u=2), vb[:, :].unsqueeze(1).to_broadcast([128, 2, 64]), r=[tvb], w=[tvbd])
                        pkv, tpkv = psum()
                        for kc in range(8):
                            mm(pkv[0:4, :], xT_c[:, kc, cs], wKV[:, kc, :], kc == 0, kc == 7, r=[wKV_t, t_xT], w=[tpkv])
                        kvo, tkvo = kvo_r.next()
                        tt('dve', kvo[0:4, :], pkv[0:4, :], btk_[0:4, :], ALU.add, r=[tpkv, tbtk], w=[tkvo])
                        if g == 0:
                            dma(o_ks[l, s, 124:128, :], kvo[0:4, 0:256], r=[tkvo])
                            dma(o_vs[l, s, 124:128, :], kvo[0:4, 256:512], r=[tkvo])
                        cp('pool', vbd[0:4, 128:256].rearrange("p (u d) -> p u d", u=2), kvo[0:4, 256 + g * 64:256 + (g + 1) * 64].unsqueeze(1).to_broadcast([4, 2, 64]), r=[tkvo], w=[tvbd])
                        pS1, tS1 = psum()
                        mm(pS1[:, 0:16].rearrange("p (a b) -> p a b", a=4), kbT[:, 0, :], qTg[:, :, cs], True, True, r=[tkbT, t_qTg], w=[tS1])
                        pS2, tS2 = psum()
                        mm(pS2[0:4, 0:16].rearrange("p (a b) -> p a b", a=4), kTg[:, g, 128 + 4 * s:128 + 4 * s + 4], qTg[:, :, cs], True, True, r=[t_kTg[g], t_qTg], w=[tS2])
                        E_, tE = E_r.next()
                        act(E_[:, 0:16], pS1[:, 0:16], AF.Exp, r=[tS1], w=[tE], scale=0.125)
                        act(E_[0:4, 16:32], pS2[0:4, 0:16], AF.Exp, r=[tS2], w=[tE], scale=0.125)
                        PT, tPT = PT_r.next()
                        tt('dve', PT[:, 0:16].rearrange("p (a b) -> p a b", a=4), E_[:, 0:16].rearrange("p (a b) -> p a b", a=4), EB[:, 4 * g:4 * g + 4, 1, 0:4], ALU.mult, r=[tE, t_c], w=[tPT])
                        tt('dve', PT[0:4, 16:32].rearrange("p (a b) -> p a b", a=4), E_[0:4, 16:32].rearrange("p (a b) -> p a b", a=4), EB[0:4, 4 * g:4 * g + 4, 0, 0:4], ALU.mult, r=[tE, t_c], w=[tPT])
                        pNm, tNm = psum()
                        pDn, tDn = psum()
                        mm(pNm[:, 0:16], vbd[:, 0:128], PT[:, 0:16], True, False, r=[tvbd, tPT], w=[tNm])
                        mm(pNm[:, 0:16], vbd[0:4, 128:256], PT[0:4, 16:32], False, True, r=[tvbd, tPT], w=[tNm])
                        mm(pDn[:, 0:16], ones_b[:, :], PT[:, 0:16], True, False, r=[tPT, t_c], w=[tDn])
                        mm(pDn[:, 0:16], ones_b[0:4, :], PT[0:4, 16:32], False, True, r=[tPT, t_c], w=[tDn])
                        rd, trd = rd_r.next()
                        for i in range(4):
                            ts('dve', rd[:, i * 4:(i + 1) * 4], pDn[:, i * 4:(i + 1) * 4], esink[:, 4 * g + i:4 * g + i + 1], ALU.add, r=[tDn, t_esink], w=[trd])
                        P.op('dve', lambda e, rd=rd: e.reciprocal(out=rd[:, 0:16], in_=rd[:, 0:16]), r=[trd], w=[trd])
                        for par in range(2):
                            ps_ = slice(par * 64, (par + 1) * 64)
                            nv = pNm[ps_, 0:16].rearrange("p (a b) -> p a b", a=4)[:, par::2, :]
                            rv = rd[ps_, 0:16].rearrange("p (a b) -> p a b", a=4)[:, par::2, :]
                            tt('dve', ybT[ps_, 2 * g:2 * g + 2, cs], nv, rv, ALU.mult, r=[tNm, trd], w=[t_ybT])
                if not is_s:
                    cp('pool', kTg[:, g, 0:128], kTg[:, g, CH:CH + 128], r=[t_kTg[g]], w=[t_kTg[g]])
            if not is_s:
                cp('pool', vtd[:, 0, :], vtd[:, TPC, :], r=[t_vtd[TPC]], w=[t_vtd[0]])

            for m in range(8):
                wM, wM_t = wmulti([(w_a[l], m * 128, 128), (w_b[l], m * 128, 128), (w_in[l], O_GA + m * 128, 128), (w_in[l], O_GB + m * 128, 128)])
                res = []
                for (wi, gi, srcT, tsrc, slot) in ((0, 2, yaT, t_yaT, SL_GA + m), (1, 3, ybT, t_ybT, SL_GB + m)):
                    pg, tpg = psum()
                    for kc in range(8):
                        mm(pg[:, 0:N], wM[:, kc, gi * 128:(gi + 1) * 128], xT_c[:, kc, 0:N], kc == 0, kc == 7, r=[wM_t, t_xT], w=[tpg])
                    sg, tsg = sg_r.next()
                    act(sg[:, 0:N], pg[:, 0:N], AF.Sigmoid, r=[tpg, t_bcol], w=[tsg], bias=bcol[:, slot:slot + 1])
                    pa, tpa = psum()
                    for kc in range(8):
                        mm(pa[:, 0:N], wM[:, kc, wi * 128:(wi + 1) * 128], srcT[:, kc, 0:N], kc == 0, kc == 7, r=[wM_t, tsrc], w=[tpa])
                    tt('dve', sg[:, 0:N], sg[:, 0:N], pa[:, 0:N], ALU.mult, r=[tsg, tpa], w=[tsg])
                    res.append((sg, tsg))
                tt('pool', merged[:, m, 0:N], res[0][0][:, 0:N], res[1][0][:, 0:N], ALU.add, r=[res[0][1], res[1][1]], w=[t_mg])
            if c == 0:
                load_gb(l, 0)
            for half in range(2):
                wO, wO_t = wmat(w_out[l], half * 512, 512)
                for u, j in enumerate(tiles):
                    n = tile_rows(j)
                    po, tpo = psum()
                    for kc in range(8):
                        mm(po[0:n, :], merged[:, kc, u * 128:u * 128 + n], wO[:, kc, :], kc == 0, kc == 7, r=[wO_t, t_mg], w=[tpo])
                    xs = x_tok[0:n, j, half * 512:(half + 1) * 512]
                    stt(xs, xs, ALPHA, po[0:n, :], ALU.mult, ALU.add, r=[t_x[j], tpo], w=[t_x[j]])
            for j in tiles:
                layer_norm(j, l, 0, t_gbt)

        P.barrier()
        AR.reset()
        xT_all = AR.alloc([128, 8, TALL], BF16)
        t_xTa = Tok()
        HT = AR.alloc([128, 4, TALL], BF16)
        t_HT = [Tok() for _ in range(5)]
        sgm_r = AR.rot(2, [128, 512], F32)
        lg_r = AR.rot(2, [128, 80], F32)
        transpose_tiles(xT_all, t_xTa, list(range(NT)), 0)
        load_gb(l, 1)
        dma(brt[:, 0:4], b_rg[l:l + 1, :].to_broadcast([128, 4]), w=[t_brt])
        dma(brt[:, 4:36], b_re[l:l + 1, :].to_broadcast([128, 32]), w=[t_brt])
        wR, wR_t = wmulti([(w_rg[l], 0, 4), (w_re[l], 0, 32)])
        for j in range(NT):
            n = tile_rows(j)
            c0 = j * 128
            pr, tpr = psum()
            for kc in range(8):
                mm(pr[0:n, 0:36], xT_all[:, kc, c0:c0 + n], wR[:, kc, :], kc == 0, kc == 7, r=[wR_t, t_xTa], w=[tpr])
            lg, tlg = lg_r.next()
            L = lg[0:n, :]
            tt('dve', L[:, 0:36], pr[0:n, 0:36], brt[0:n, :], ALU.add, r=[tpr, t_brt], w=[tlg])
            s_, st_ = small.next()
            S = s_[0:n, :]
            P.op('dve', lambda e, S=S, L=L: e.reduce_max(out=S[:, 0:1], in_=L[:, 0:4], axis=mybir.AxisListType.X), r=[tlg], w=[st_])
            ts('dve', S[:, 1:2], S[:, 0:1], -1.0, ALU.mult, r=[st_], w=[st_])
            P.op('act', lambda e, S=S, L=L: e.activation(out=L[:, 36:40], in_=L[:, 0:4], func=AF.Exp, bias=S[:, 1:2], scale=1.0, accum_out=S[:, 2:3]), r=[tlg, st_], w=[tlg, st_])
            ts('dve', L[:, 40:44], L[:, 0:4], S[:, 0:1], ALU.is_equal, r=[tlg, st_], w=[tlg])
            ts('dve', L[:, 40:44], L[:, 40:44], BIG, ALU.mult, r=[tlg], w=[tlg], s2=-BIG, op1=ALU.add)
            tt('dve', L[:, 44:76].rearrange("p (g i) -> p g i", g=4), L[:, 4:36].rearrange("p (g i) -> p g i", g=4), L[:, 40:44].unsqueeze(2).to_broadcast([n, 4, 8]), ALU.add, r=[tlg], w=[tlg])
            P.op('dve', lambda e, S=S, L=L: e.max(out=S[:, 8:16], in_=L[:, 44:76]), r=[tlg], w=[st_])
            ts('dve', S[:, 3:4], S[:, 8:9], -1.0, ALU.mult, r=[st_], w=[st_])
            ts('dve', L[:, 4:36], L[:, 44:76], S[:, 9:10], ALU.is_ge, r=[tlg, st_], w=[tlg])
            act(L[:, 44:76], L[:, 44:76], AF.Exp, r=[tlg, st_], w=[tlg], bias=S[:, 3:4])
            stt(L[:, 44:76], L[:, 44:76], 1.0, L[:, 4:36], ALU.mult, ALU.mult, r=[tlg], w=[tlg, st_], accum=S[:, 4:5])
            tt('dve', S[:, 5:6], S[:, 4:5], S[:, 2:3], ALU.mult, r=[st_], w=[st_])
            P.op('dve', lambda e, S=S: e.reciprocal(out=S[:, 6:7], in_=S[:, 5:6]), r=[st_], w=[st_])
            ts('dve', gate_full[0:n, j, :], L[:, 44:76], S[:, 6:7], ALU.mult, r=[tlg, st_], w=[t_gate[j]])
            ts('pool', x_tok[0:n, j, :], x_tok[0:n, j, :], ALPHA, ALU.mult, r=[t_x[j], t_xTa], w=[t_x[j]])
        mchunks = [(cc * 512, 512) for cc in range(4)] + [(2048, 64)]
        for ex in range(32):
            wG, wG_t = wmat(w_eg[l, ex], 0, 512)
            wU, wU_t = wmat(w_eu[l, ex], 0, 512)
            wD, wD_t = wmat(w_ed[l, ex], 0, 1024, kcs=4)
            for ci, (c0, N) in enumerate(mchunks):
                for fc in range(4):
                    pg, tpg = psum()
                    pu, tpu = psum()
                    for kc in range(8):
                        mm(pg[:, 0:N], wG[:, kc, fc * 128:(fc + 1) * 128], xT_all[:, kc, c0:c0 + N], kc == 0, kc == 7, r=[wG_t, t_xTa], w=[tpg])
                    for kc in range(8):
                        mm(pu[:, 0:N], wU[:, kc, fc * 128:(fc + 1) * 128], xT_all[:, kc, c0:c0 + N], kc == 0, kc == 7, r=[wU_t, t_xTa], w=[tpu])
                    sg, tsg = sgm_r.next()
                    act(sg[:, 0:N], pg[:, 0:N], AF.Silu, r=[tpg], w=[tsg])
                    tt('dve', HT[:, fc, c0:c0 + N], sg[:, 0:N], pu[:, 0:N], ALU.mult, r=[tsg, tpu], w=[t_HT[ci]])
            for j in range(NT):
                n = tile_rows(j)
                ci = min(j // 4, 4)
                for half in range(2):
                    py, tpy = psum()
                    for fc in range(4):
                        mm(py[0:n, :], HT[:, fc, j * 128:j * 128 + n], wD[:, fc, half * 512:(half + 1) * 512], fc == 0, fc == 3, r=[wD_t, t_HT[ci]], w=[tpy])
                    xs = x_tok[0:n, j, half * 512:(half + 1) * 512]
                    stt(xs, py[0:n, :], gate_full[0:n, j, ex:ex + 1], xs, ALU.mult, ALU.add, r=[tpy, t_gate[j], t_x[j]], w=[t_x[j]])
        for j in range(NT):
            layer_norm(j, l, 1, t_gbt)

        P.barrier()
        AR.reset()
        xT_all = AR.alloc([128, 8, TALL], BF16)
        t_xTa = Tok()
        pT_all = AR.alloc([128, 2, TALL], BF16)
        t_pT = Tok()
        ptile_r = AR.rot(2, [128, 256], F32)
        sgp_r = AR.rot(2, [128, 512], F32)
        transpose_tiles(xT_all, t_xTa, list(range(NT)), 0)
        load_gb(l, 2)
        for j in range(NT):
            n = tile_rows(j)
            c0 = j * 128
            pt, tpt = ptile_r.next()
            dma(pt[0:n, :], pin[l, c0:c0 + n, :], w=[tpt])
            pb, ptk = psum()
            for q in range(2):
                tr(pb[:, q * 128:q * 128 + n], pt[0:n, q * 128:(q + 1) * 128], ident_f[0:n, 0:n], r=[tpt, t_c], w=[ptk])
            cp('act', pT_all[:, :, c0:c0 + n], pb[:, 0:256].rearrange("p (a b) -> p a b", a=2)[:, :, 0:n], r=[ptk], w=[t_pT])
            ts('pool', x_tok[0:n, j, :], x_tok[0:n, j, :], ALPHA, ALU.mult, r=[t_x[j], t_xTa], w=[t_x[j]])
        for half in range(2):
            wPG, wPG_t = wmat(w_pg[l], half * 512, 512)
            wPP, wPP_t = wmat(w_pp[l], half * 512, 512, kcs=2)
            for j in range(NT):
                n = tile_rows(j)
                c0 = j * 128
                p1, tp1 = psum()
                for kc in range(8):
                    mm(p1[0:n, :], xT_all[:, kc, c0:c0 + n], wPG[:, kc, :], kc == 0, kc == 7, r=[wPG_t, t_xTa], w=[tp1])
                p2, tp2 = psum()
                for kc in range(2):
                    mm(p2[0:n, :], pT_all[:, kc, c0:c0 + n], wPP[:, kc, :], kc == 0, kc == 1, r=[wPP_t, t_pT], w=[tp2])
                sg, tsg = sgp_r.next()
                act(sg[0:n, :], p1[0:n, :], AF.Sigmoid, r=[tp1], w=[tsg])
                tt('dve', sg[0:n, :], sg[0:n, :], p2[0:n, :], ALU.mult, r=[tsg, tp2], w=[tsg])
                xs = x_tok[0:n, j, half * 512:(half + 1) * 512]
                tt('pool', xs, xs, sg[0:n, :], ALU.add, r=[t_x[j], tsg], w=[t_x[j]])
        for j in range(NT):
            layer_norm(j, l, 2, t_gbt)
        P.barrier()

    dma(o_y[0:1024, :].rearrange("(j p) d -> p j d", p=128), x_tok[:, 0:8, :], r=t_x[0:8])
    dma(o_y[1024:2048, :].rearrange("(j p) d -> p j d", p=128), x_tok[:, 8:16, :], r=t_x[8:16])
    dma(o_y[2048:2112, :], x_tok[0:64, 16, :], r=[t_x[16]])
    P.emit()
    return nc, P


_CACHE = {}


def kernel(x_prompt, x_sample, state_mlstm_C, state_mlstm_n, state_mlstm_m, state_swa_k, state_swa_v,
           p_prompt, p_sample, w_in, b_in, mh_gain, w_a, w_b, w_out, rel_table, w_sink, ln_g, ln_b,
           w_rg, b_rg, w_re, b_re, w_eg, w_eu, w_ed, w_pg, w_pp, _depth=4):
    f = lambda a: np.ascontiguousarray(np.asarray(a, dtype=np.float32))
    if _depth not in _CACHE:
        _CACHE[_depth] = build(_depth)
    nc, P = _CACHE[_depth]
    consts = make_consts()
    shared = dict(w_in=f(w_in), b_in=f(b_in), mh_gain=f(mh_gain), w_a=f(w_a), w_b=f(w_b), w_out=f(w_out),
                  rel_table=f(rel_table), w_sink=f(w_sink), ln_g=f(ln_g), ln_b=f(ln_b), w_rg=f(w_rg), b_rg=f(b_rg),
                  w_re=f(w_re), b_re=f(b_re), w_eg=f(w_eg), w_eu=f(w_eu), w_ed=f(w_ed), w_pg=f(w_pg), w_pp=f(w_pp))
    shared.update(consts)
    x_prompt = f(x_prompt); x_sample = f(x_sample); p_prompt = f(p_prompt); p_sample = f(p_sample)
    sCa = f(state_mlstm_C); sna = f(state_mlstm_n); sma = f(state_mlstm_m); ska = f(state_swa_k); sva = f(state_swa_v)
    in_maps = []
    for c in range(8):
        sl = slice(16 * c, 16 * c + 16)
        m = dict(shared)
        m['xin'] = np.ascontiguousarray(np.concatenate([x_prompt[c], x_sample[sl].reshape(64, D)], 0))
        m['pin'] = np.ascontiguousarray(np.concatenate([p_prompt[:, c], p_sample[:, sl].reshape(4, 64, 256)], 1))
        m['sC'] = np.ascontiguousarray(sCa[:, sl])
        m['sn'] = np.ascontiguousarray(sna[:, sl])
        m['sm'] = np.ascontiguousarray(sma[:, sl].reshape(4, 64))
        m['sk'] = np.ascontiguousarray(ska[:, sl].reshape(4, 16, 128, 256))
        m['sv'] = np.ascontiguousarray(sva[:, sl].reshape(4, 16, 128, 256))
        in_maps.append(m)
    res = run_bass_kernel_spmd(nc, in_maps, core_ids=list(range(8)))
    R = res.results
    y_p = np.stack([R[c]['o_y'][:2048] for c in range(8)], 0)
    y_s = np.concatenate([R[c]['o_y'][2048:].reshape(16, 4, D) for c in range(8)], 0)
    C_p = np.stack([R[c]['o_Cp'] for c in range(8)], 1)
    n_p = np.stack([R[c]['o_np'] for c in range(8)], 1)
    m_p = np.stack([R[c]['o_mp'] for c in range(8)], 1)
    k_p = np.stack([R[c]['o_kp'].reshape(4, 128, 4, 64) for c in range(8)], 1)
    v_p = np.stack([R[c]['o_vp'].reshape(4, 128, 4, 64) for c in range(8)], 1)
    C_s = np.concatenate([R[c]['o_Cs'] for c in range(8)], 1)
    n_s = np.concatenate([R[c]['o_ns'] for c in range(8)], 1)
    m_s = np.concatenate([R[c]['o_ms'].reshape(4, 16, 4) for c in range(8)], 1)
    k_s = np.concatenate([R[c]['o_ks'].reshape(4, 16, 128, 4, 64) for c in range(8)], 1)
    v_s = np.concatenate([R[c]['o_vs'].reshape(4, 16, 128, 4, 64) for c in range(8)], 1)
    outs = (y_p, y_s, C_p, n_p, m_p, k_p, v_p, C_s, n_s, m_s, k_s, v_s)
    return tuple(np.ascontiguousarray(o, dtype=np.float32) for o in outs)
```

```python
import contextlib
import numpy as np
import concourse.bass as bass
import concourse.mybir as mybir
from concourse.bass_utils import run_bass_kernel_spmd

F32 = mybir.dt.float32
BF16 = mybir.dt.bfloat16
AF = mybir.ActivationFunctionType
ALU = mybir.AluOpType

ENGS = ['pe', 'act', 'dve', 'pool', 'sp']
N_DMA_SEM = 8
SAME_ENGINE_SYNC = True

D = 1024
NT = 17
TP = 2048
TS = 64
TALL = TP + TS
DIN = 6664
O_MQ, O_MK, O_MV, O_OG, O_IG, O_FG, O_SQ, O_SK, O_SV, O_GA, O_GB = 0, 512, 1024, 2048, 3072, 3076, 3080, 4104, 4360, 4616, 5640
ALPHA = float((2 * 4) ** 0.25)
EPS = 1e-5
BIG = 30000.0
FW = 400
CH = 256
TPC = CH // 128


class Tok:
    __slots__ = ('w', 'rs', 'const')

    def __init__(self, const=False):
        self.w = None
        self.rs = []
        self.const = const


class Op:
    __slots__ = ('eng', 'fn', 'deps', 'sig', 'count', 'is_dma', 'dma_id', 'waits')


class Prog:
    def __init__(self, nc):
        self.nc = nc
        self.ops = {e: [] for e in ENGS}
        self.n_dma = 0
        self.dmas = []
        self.stack = contextlib.ExitStack()
        self._n = 0

    def sb(self, shape, dtype):
        self._n += 1
        return self.stack.enter_context(self.nc.sbuf_tensor(f"sb{self._n}", list(shape), dtype))

    def ps(self, shape, dtype=F32):
        self._n += 1
        return self.stack.enter_context(self.nc.psum_tensor(f"ps{self._n}", list(shape), dtype))

    def op(self, eng, fn, r=(), w=()):
        o = Op()
        o.eng = eng
        o.fn = fn
        o.deps = set()
        o.sig = False
        o.is_dma = False
        o.count = 0
        for t in r:
            if t.w is not None:
                o.deps.add(t.w)
        for t in w:
            if t.w is not None:
                o.deps.add(t.w)
            for x in t.rs:
                o.deps.add(x)
        for t in r:
            if not t.const:
                t.rs.append(o)
        for t in w:
            t.w = o
            t.rs = []
        o.deps.discard(o)
        self.ops[eng].append(o)
        return o

    def dma(self, fn, r=(), w=()):
        o = self.op('sp', fn, r, w)
        o.is_dma = True
        o.dma_id = self.n_dma
        if self.n_dma >= N_DMA_SEM:
            o.deps.add(self.dmas[self.n_dma - N_DMA_SEM])
        self.n_dma += 1
        self.dmas.append(o)
        return o

    def barrier(self):
        last = []
        for e in ENGS:
            if e == 'sp':
                continue
            if self.ops[e]:
                last.append(self.ops[e][-1])
        last += self.dmas[-N_DMA_SEM:]
        for e in ENGS:
            o = self.op(e, None)
            for d in last:
                if d is not o:
                    o.deps.add(d)

    def emit(self):
        nc = self.nc

        def skip(d, o):
            return d.eng == o.eng and (d.eng in ('pe', 'sp') or not SAME_ENGINE_SYNC)

        for e in ENGS:
            for o in self.ops[e]:
                for d in o.deps:
                    if d.is_dma or skip(d, o):
                        continue
                    d.sig = True
        for e in ENGS:
            c = 0
            for o in self.ops[e]:
                if o.sig and not o.is_dma and o.fn is not None:
                    c += 1
                o.count = c
        sems = {e: self.stack.enter_context(nc.semaphore(f"s_{e}")) for e in ENGS}
        dsems = [self.stack.enter_context(nc.semaphore(f"s_dma{i}")) for i in range(N_DMA_SEM)]

        def dma_target(d):
            return dsems[d.dma_id % N_DMA_SEM], 16 * (d.dma_id // N_DMA_SEM + 1)

        nwaits = 0
        for e in ENGS:
            waited = {}
            for o in self.ops[e]:
                need = {}
                for d in o.deps:
                    if d.is_dma:
                        s, v = dma_target(d)
                    else:
                        if skip(d, o):
                            continue
                        s, v = sems[d.eng], d.count
                    if need.get(s, 0) < v:
                        need[s] = v
                o.waits = []
                for s, v in need.items():
                    if waited.get(s, 0) < v:
                        waited[s] = v
                        o.waits.append((s, v))
                        nwaits += 1
        self.stats = {e: len(self.ops[e]) for e in ENGS}
        self.stats['waits'] = nwaits
        final_dma = {}
        for d in self.dmas:
            s, v = dma_target(d)
            final_dma[s] = max(final_dma.get(s, 0), v)

        def replay(ename, eng):
            for o in self.ops[ename]:
                for s, v in o.waits:
                    eng.wait_ge(s, v)
                if o.fn is None:
                    continue
                ins = o.fn(eng)
                if o.is_dma:
                    s, v = dma_target(o)
                    ins.then_inc(s, 16)
                elif o.sig:
                    ins.then_inc(sems[ename], 1)
            if ename == 'sp':
                for s, v in final_dma.items():
                    eng.wait_ge(s, v)

        with nc.Block() as block:
            @block.tensor
            def _(eng):
                replay('pe', eng)

            @block.scalar
            def _(eng):
                replay('act', eng)

            @block.vector
            def _(eng):
                replay('dve', eng)

            @block.gpsimd
            def _(eng):
                replay('pool', eng)

            @block.sync
            def _(eng):
                replay('sp', eng)
        self.stack.close()


def rel_bucket_np(dist):
    n = np.maximum(dist, 0)
    max_exact = 16
    large = max_exact + (np.log(np.maximum(n, 1) / max_exact) / np.log(128 / max_exact) * (32 - max_exact)).astype(np.int32)
    large = np.minimum(large, 31)
    return np.where(n < max_exact, n, large).astype(np.int32)


def make_consts():
    c = {}
    c['c_ident'] = np.eye(128, dtype=np.float32)
    s = np.arange(128)[:, None]
    l = np.arange(128)[None, :]
    c['c_maskbig'] = np.where(s > l, BIG, 0.0).astype(np.float32)
    i = np.arange(FW)
    dist = i - 144
    valid = (dist >= 0) & (dist <= 128)
    oh = np.zeros((32, FW), np.float32)
    b = rel_bucket_np(dist)
    oh[b[valid], i[valid]] = 1.0
    c['c_ohd'] = oh
    c['c_maskvec'] = np.where(valid, 0.0, -BIG).astype(np.float32)[None, :]
    sel = np.zeros((4, 4, 128), np.float32)
    for h in range(4):
        sel[h, h, :] = 1.0
    c['c_sel4'] = sel.reshape(4, 512)
    return c


def build(depth=4):
    nc = bass.Bass("TRN2", target_bir_lowering=False)
    P = Prog(nc)

    def din(name, shape):
        return nc.dram_tensor(name, list(shape), F32, kind="ExternalInput").ap()

    def dout(name, shape):
        return nc.dram_tensor(name, list(shape), F32, kind="ExternalOutput").ap()

    xin = din("xin", [TALL, D])
    pin = din("pin", [4, TALL, 256])
    sC = din("sC", [4, 16, 4, 128, 256])
    sn = din("sn", [4, 16, 4, 128])
    sm = din("sm", [4, 64])
    sk = din("sk", [4, 16, 128, 256])
    sv = din("sv", [4, 16, 128, 256])
    w_in = din("w_in", [4, D, DIN])
    b_in = din("b_in", [4, DIN])
    mh_gain = din("mh_gain", [4, D])
    w_a = din("w_a", [4, D, D])
    w_b = din("w_b", [4, D, D])
    w_out = din("w_out", [4, D, D])
    rel_table = din("rel_table", [32, 16])
    w_sink = din("w_sink", [4, 16])
    ln_g = din("ln_g", [4, 3, D])
    ln_b = din("ln_b", [4, 3, D])
    w_rg = din("w_rg", [4, D, 4])
    b_rg = din("b_rg", [4, 4])
    w_re = din("w_re", [4, D, 32])
    b_re = din("b_re", [4, 32])
    w_eg = din("w_eg", [4, 32, D, 512])
    w_eu = din("w_eu", [4, 32, D, 512])
    w_ed = din("w_ed", [4, 32, 512, D])
    w_pg = din("w_pg", [4, D, D])
    w_pp = din("w_pp", [4, 256, D])
    c_ident = din("c_ident", [128, 128])
    c_maskbig = din("c_maskbig", [128, 128])
    c_ohd = din("c_ohd", [32, FW])
    c_maskvec = din("c_maskvec", [1, FW])
    c_sel4 = din("c_sel4", [4, 512])

    o_y = dout("o_y", [TALL, D])
    o_Cp = dout("o_Cp", [4, 4, 128, 256])
    o_np = dout("o_np", [4, 4, 128])
    o_mp = dout("o_mp", [4, 4])
    o_kp = dout("o_kp", [4, 128, 256])
    o_vp = dout("o_vp", [4, 128, 256])
    o_Cs = dout("o_Cs", [4, 16, 4, 128, 256])
    o_ns = dout("o_ns", [4, 16, 4, 128])
    o_ms = dout("o_ms", [4, 64])
    o_ks = dout("o_ks", [4, 16, 128, 256])
    o_vs = dout("o_vs", [4, 16, 128, 256])
    scratch = nc.dram_tensor("scratch", [16, 128, FW], F32, kind="Internal").ap()

    def mm(out, lhsT, rhs, start, stop, r, w):
        P.op('pe', lambda e: e.matmul(out, lhsT=lhsT, rhs=rhs, start=start, stop=stop), r=r, w=w)

    def tr(out, in_, ident, r, w):
        P.op('pe', lambda e: e.transpose(out=out, in_=in_, identity=ident), r=r, w=w)

    def act(out, in_, func, r, w, bias=None, scale=1.0):
        if bias is None:
            P.op('act', lambda e: e.activation(out=out, in_=in_, func=func, scale=scale), r=r, w=w)
        else:
            P.op('act', lambda e: e.activation(out=out, in_=in_, func=func, bias=bias, scale=scale), r=r, w=w)

    def tt(eng, out, in0, in1, op, r, w):
        P.op(eng, lambda e: e.tensor_tensor(out=out, in0=in0, in1=in1, op=op), r=r, w=w)

    def ts(eng, out, in0, s1, op0, r, w, s2=None, op1=None):
        if op1 is None:
            P.op(eng, lambda e: e.tensor_scalar(out=out, in0=in0, scalar1=s1, scalar2=None, op0=op0), r=r, w=w)
        else:
            P.op(eng, lambda e: e.tensor_scalar(out=out, in0=in0, scalar1=s1, scalar2=s2, op0=op0, op1=op1), r=r, w=w)

    def stt(out, in0, scalar, in1, op0, op1, r, w, accum=None):
        if accum is None:
            P.op('dve', lambda e: e.scalar_tensor_tensor(out=out, in0=in0, scalar=scalar, in1=in1, op0=op0, op1=op1), r=r, w=w)
        else:
            P.op('dve', lambda e: e.scalar_tensor_tensor(out=out, in0=in0, scalar=scalar, in1=in1, op0=op0, op1=op1, accum_out=accum), r=r, w=w)

    def cp(eng, out, in_, r, w):
        if eng == 'act':
            P.op('act', lambda e: e.copy(out=out, in_=in_), r=r, w=w)
        else:
            P.op(eng, lambda e: e.tensor_copy(out=out, in_=in_), r=r, w=w)

    def dma(out, in_, r=(), w=(), slow=False):
        if slow:
            P.dma(lambda e: e.dma_start(out=out, in_=in_, allow_slow_non_contiguous=True), r=r, w=w)
        else:
            P.dma(lambda e: e.dma_start(out=out, in_=in_), r=r, w=w)

    def memset(eng, ap, val, w):
        P.op(eng, lambda e: e.memset(ap, val), w=w)

    class Rot:
        def __init__(self, bufs):
            self.bufs = bufs
            self.toks = [Tok() for _ in bufs]
            self.i = 0

        def next(self):
            k = self.i % len(self.bufs)
            self.i += 1
            return self.bufs[k], self.toks[k]

    banks = Rot([P.ps([128, 512], F32) for _ in range(8)])

    def psum():
        b, t = banks.next()
        return b, t

    x_tok = P.sb([128, NT, D], F32)
    t_x = [Tok() for _ in range(NT)]
    ident_f = P.sb([128, 128], F32)
    ident_b = P.sb([128, 128], BF16)
    ones_b = P.sb([128, 128], BF16)
    maskbig = P.sb([128, 128], F32)
    sel4 = P.sb([4, 4, 128], F32)
    EB = P.sb([128, 16, 2, 128], BF16)
    mhalf = P.sb([128, 1], F32)
    bcolh = P.sb([128, 16], F32)
    t_bcolh = Tok()
    t_c = Tok()
    NSTG = 3
    stg = Rot([P.sb([128, 1024], F32) for _ in range(NSTG)])
    NRING = 4
    ring = Rot([P.sb([128, 4096], BF16) for _ in range(NRING)])
    gate_full = P.sb([128, NT, 32], F32)
    t_gate = [Tok() for _ in range(NT)]
    gbt = P.sb([128, 2, D], F32)
    t_gbt = Tok()
    bcol = P.sb([128, 48], F32)
    t_bcol = Tok()
    esink = P.sb([128, 16], F32)
    t_esink = Tok()
    brt = P.sb([128, 36], F32)
    t_brt = Tok()
    small = Rot([P.sb([128, 16], F32) for _ in range(6)])
    ARENA_W = 18700
    arena = P.sb([128, ARENA_W], F32)

    class Arena:
        def __init__(self):
            self.off = 0

        def reset(self):
            self.off = 0

        def alloc(self, shape, dtype):
            n = int(np.prod(shape[1:]))
            nw = n if dtype == F32 else (n + 1) // 2
            assert self.off + nw <= ARENA_W, (self.off, nw)
            v = arena[0:shape[0], self.off:self.off + nw]
            self.off += nw
            if dtype != F32:
                v = v.bitcast(dtype)
                if n % 2:
                    v = v[:, 0:n]
            if len(shape) > 2:
                names = " ".join(f"d{i}" for i in range(1, len(shape)))
                v = v.rearrange(f"p ({names}) -> p {names}", **{f"d{i}": shape[i] for i in range(1, len(shape) - 1)})
            return v

        def rot(self, n, shape, dtype):
            return Rot([self.alloc(shape, dtype) for _ in range(n)])

    AR = Arena()

    def wblock(srcs):
        slot, tok = ring.next()
        views = []
        off = 0
        for s in srcs:
            shp = list(s.shape)
            n = int(np.prod(shp[1:]))
            assert n <= 1024
            st, stok = stg.next()
            sv_ = st[0:shp[0], 0:n]
            dv = slot[0:shp[0], off:off + n]
            if len(shp) == 3:
                sv_ = sv_.rearrange("p (a b) -> p a b", a=shp[1])
                dv = dv.rearrange("p (a b) -> p a b", a=shp[1])
            dma(sv_, s, w=[stok])
            cast(dv, sv_, r=[stok], w=[tok])
            views.append(dv)
            off += n
        return views, tok

    cast_state = {'i': 0, 'pat': ['act', 'dve', 'act', 'dve', 'pool']}

    def cast(dst, src, r, w):
        pat = cast_state['pat']
        eng = pat[cast_state['i'] % len(pat)]
        cast_state['i'] += 1
        cp(eng, dst, src, r=r, w=w)

    def wcols(w2d, c0, ncols, kcs=8):
        v = w2d[:, c0:c0 + ncols].rearrange("(kc p) n -> p kc n", p=128)
        per = max(1, 1024 // ncols)
        return [v[:, k0:min(k0 + per, kcs), :] for k0 in range(0, kcs, per)]

    def join_views(views):
        return views

    def wmat(w2d, c0, ncols, kcs=8):
        slot, tok = ring.next()
        v = w2d[:, c0:c0 + ncols].rearrange("(kc p) n -> p kc n", p=128)
        per = max(1, 1024 // ncols)
        full = slot[:, 0:kcs * ncols].rearrange("p (a b) -> p a b", a=kcs)
        for k0 in range(0, kcs, per):
            k1 = min(k0 + per, kcs)
            st, stok = stg.next()
            sv_ = st[:, 0:(k1 - k0) * ncols].rearrange("p (a b) -> p a b", a=k1 - k0)
            dma(sv_, v[:, k0:k1, :], w=[stok])
            cast(full[:, k0:k1, :], sv_, r=[stok], w=[tok])
        return full, tok

    def wmulti(parts):
        slot, tok = ring.next()
        tot = sum(p[2] for p in parts)
        assert 8 * tot <= 4096
        full = slot[:, 0:8 * tot].rearrange("p (a b) -> p a b", a=8)
        o = 0
        for (w2d, c0, ncols) in parts:
            v = w2d[:, c0:c0 + ncols].rearrange("(kc p) n -> p kc n", p=128)
            per = max(1, 1024 // ncols)
            for k0 in range(0, 8, per):
                k1 = min(k0 + per, 8)
                st, stok = stg.next()
                sv_ = st[:, 0:(k1 - k0) * ncols].rearrange("p (a b) -> p a b", a=k1 - k0)
                dma(sv_, v[:, k0:k1, :], w=[stok])
                cast(full[:, k0:k1, o:o + ncols], sv_, r=[stok], w=[tok])
            o += ncols
        return full, tok

    dma(ident_f[:], c_ident, w=[t_c])
    dma(maskbig[:], c_maskbig, w=[t_c])
    dma(sel4[:].rearrange("p a b -> p (a b)"), c_sel4, w=[t_c])
    cp('dve', ident_b[:], ident_f[:], r=[t_c], w=[t_c])
    memset('dve', ones_b[:], 1.0, w=[t_c])
    memset('dve', mhalf[:], -0.5, w=[t_c])
    AR.reset()
    rt = AR.alloc([32, 16], F32)
    ohd = AR.alloc([32, FW], F32)
    mvec = AR.alloc([1, FW], F32)
    one1 = AR.alloc([1, 16], F32)
    fsb = AR.alloc([16, FW], F32)
    t_tmp = Tok()
    dma(rt, rel_table, w=[t_tmp])
    dma(ohd, c_ohd, w=[t_tmp])
    dma(mvec, c_maskvec, w=[t_tmp])
    memset('dve', one1, 1.0, w=[t_tmp])
    pb, pt_ = psum()
    mm(pb[0:16, 0:FW], rt, ohd, True, False, r=[t_tmp], w=[pt_])
    mm(pb[0:16, 0:FW], one1, mvec, False, True, r=[t_tmp], w=[pt_])
    t_f = Tok()
    cp('dve', fsb, pb[0:16, 0:FW], r=[pt_], w=[t_f])
    t_scr = Tok()
    dma(scratch, fsb.unsqueeze(1).to_broadcast([16, 128, FW]), r=[t_f], w=[t_scr])
    btmp = AR.rot(3, [128, 128], F32)
    for h in range(16):
        for kind, cc in ((0, 144), (1, 272)):
            base = scratch[h, 0, cc:cc + 128]
            src = bass.AP(tensor=base.tensor, offset=base.offset, ap=[[FW - 1, 128], [1, 128]])
            bt_, btk = btmp.next()
            dma(bt_, src, r=[t_scr], w=[btk])
            act(EB[:, h, kind, :], bt_, AF.Exp, r=[btk], w=[t_c])
    t_c.const = True
    P.barrier()

    dma(x_tok[:, 0:8, :], xin[0:1024, :].rearrange("(j p) d -> p j d", p=128), w=t_x[0:8])
    dma(x_tok[:, 8:16, :], xin[1024:2048, :].rearrange("(j p) d -> p j d", p=128), w=t_x[8:16])
    dma(x_tok[0:64, 16, :], xin[2048:2112, :], w=[t_x[16]])

    def tile_rows(j):
        return 64 if j == 16 else 128

    def transpose_tiles(dst, t_dst, tiles, col0):
        k = 0
        for j in tiles:
            n = tile_rows(j)
            c = col0 + (j - tiles[0]) * 128
            for half in range(2):
                pb, ptk = psum()
                for q in range(4):
                    kc = half * 4 + q
                    tr(pb[:, q * 128:q * 128 + n], x_tok[0:n, j, kc * 128:(kc + 1) * 128], ident_f[0:n, 0:n], r=[t_x[j], t_c], w=[ptk])
                src = pb[:, :].rearrange("p (a b) -> p a b", a=4)[:, :, 0:n]
                cp('act' if k % 2 == 0 else 'dve', dst[:, half * 4:half * 4 + 4, c:c + n], src, r=[ptk], w=[t_dst])
                k += 1

    def layer_norm(j, l, idx, t_g):
        n = tile_rows(j)
        xs = x_tok[0:n, j, :]
        s_, st_ = small.next()
        P.op('dve', lambda e: e.bn_stats(out=s_[0:n, 0:6], in_=x_tok[0:n, j, 0:512]), r=[t_x[j]], w=[st_])
        P.op('dve', lambda e: e.bn_stats(out=s_[0:n, 6:12], in_=x_tok[0:n, j, 512:1024]), r=[t_x[j]], w=[st_])
        P.op('dve', lambda e: e.bn_aggr(out=s_[0:n, 12:14], in_=s_[0:n, 0:12]), r=[st_], w=[st_])
        ts('dve', s_[0:n, 14:15], s_[0:n, 13:14], EPS, ALU.add, r=[st_], w=[st_])
        tt('pool', s_[0:n, 15:16], s_[0:n, 14:15], mhalf[0:n, 0:1], ALU.pow, r=[st_, t_c], w=[st_])
        ts('dve', s_[0:n, 14:15], s_[0:n, 12:13], s_[0:n, 15:16], ALU.mult, r=[st_], w=[st_], s2=-1.0, op1=ALU.mult)
        act(xs, xs, AF.Identity, r=[t_x[j], st_], w=[t_x[j]], bias=s_[0:n, 14:15], scale=s_[0:n, 15:16])
        tt('dve', xs, xs, gbt[0:n, 0, :], ALU.mult, r=[t_x[j], t_g], w=[t_x[j]])
        tt('pool', xs, xs, gbt[0:n, 1, :], ALU.add, r=[t_x[j], t_g], w=[t_x[j]])

    def load_gb(l, idx):
        dma(gbt[:, 0, :], ln_g[l, idx:idx + 1, :].to_broadcast([128, D]), w=[t_gbt])
        dma(gbt[:, 1, :], ln_b[l, idx:idx + 1, :].to_broadcast([128, D]), w=[t_gbt])

    SL_MQ, SL_MK, SL_IG, SL_FG, SL_SQ, SL_SK, SL_GA, SL_GB = 0, 4, 8, 9, 10, 26, 30, 38

    def load_bcol(l):
        def col(slot, c0, n):
            dma(bcol[0:n, slot:slot + 1], b_in[l, c0:c0 + n].unsqueeze(1), w=[t_bcol])
        for h in range(4):
            col(SL_MQ + h, O_MQ + h * 128, 128)
            col(SL_MK + h, O_MK + h * 128, 128)
        col(SL_IG, O_IG, 4)
        col(SL_FG, O_FG, 4)
        for hh in range(16):
            col(SL_SQ + hh, O_SQ + hh * 64, 64)
        for g in range(4):
            col(SL_SK + g, O_SK + g * 64, 64)
        for m in range(8):
            col(SL_GA + m, O_GA + m * 128, 128)
            col(SL_GB + m, O_GB + m * 128, 128)

    for l in range(depth):
        AR.reset()
        load_bcol(l)
        ts('dve', bcolh[:, :], bcol[:, SL_GA:SL_GA + 16], 0.5, ALU.mult, r=[t_bcol], w=[t_bcolh])
        dma(esink[:], w_sink[l:l + 1, :].to_broadcast([128, 16]), w=[t_esink])
        act(esink[:], esink[:], AF.Exp, r=[t_esink], w=[t_esink])
        gain_r = AR.rot(2, [128, 256], F32)
        xT_c = AR.alloc([128, 8, CH], BF16)
        t_xT = Tok()
        yaT = AR.alloc([128, 8, CH], BF16)
        t_yaT = Tok()
        ybT = AR.alloc([128, 8, CH], BF16)
        t_ybT = Tok()
        merged = AR.alloc([128, 8, CH], BF16)
        t_mg = Tok()
        fgr = AR.alloc([4, CH], F32)
        Brow = AR.alloc([4, CH], F32)
        Urow = AR.alloc([4, CH], F32)
        Arow = AR.alloc([4, CH], F32)
        onec = AR.alloc([4, 2], F32)
        t_rows = Tok()
        memset('dve', onec, 1.0, w=[t_rows])
        ones_row = onec[:, 0:1].to_broadcast([4, CH])
        carry = AR.alloc([4, 2], F32)
        t_carry = Tok()
        cols = AR.alloc([128, TPC, 8], F32)
        t_cols = Tok()
        cols_s = AR.alloc([4, 16, 8], F32)
        m0row = AR.alloc([4, 16], F32)
        m0rep = AR.alloc([128, 64], F32)
        t_m0 = Tok()
        msrow = AR.alloc([4, 16], F32)
        acs_c = AR.alloc([128, 4], F32)
        t_acs = Tok()
        memset('dve', acs_c, 0.0, w=[t_acs])
        ArepU = AR.rot(1, [128, CH], F32)
        qT_r = AR.rot(1, [128, CH], BF16)
        kT_r = AR.rot(1, [128, CH], BF16)
        bt_r = AR.rot(2, [128, 512], F32)
        vaug_r = AR.rot(2, [128, 257], BF16)
        for vb_ in vaug_r.bufs:
            memset('pool', vb_[:, 256:257], 1.0, w=[Tok()])
        ogt_r = AR.rot(2, [128, 256], F32)
        AM_r = AR.rot(2, [128, 128], F32)
        WT_r = AR.rot(2, [128, 128], F32)
        Wi_r = AR.rot(2, [128, 128], F32)
        STw_r = AR.rot(2, [128, 128], BF16)
        qw_r = AR.rot(2, [128, 128], BF16)
        kw_r = AR.rot(2, [128, 128], BF16)
        ya_r = AR.rot(2, [128, 256], F32)
        C_f = AR.alloc([128, 4, 257], F32)
        C_b = AR.alloc([128, 4, 257], BF16)
        t_C = [Tok() for _ in range(4)]
        for h in range(4):
            memset('pool', C_f[:, h, :], 0.0, w=[t_C[h]])
            memset('pool', C_b[:, h, :], 0.0, w=[t_C[h]])
        Cs_f = AR.rot(2, [128, 257], F32)
        Cs_b = AR.rot(2, [128, 257], BF16)
        Cs_o = AR.rot(2, [128, 257], F32)
        qTg = AR.alloc([64, 4, CH], BF16)
        t_qTg = Tok()
        kTg = AR.alloc([64, 4, CH + 128], BF16)
        t_kTg = [Tok() for _ in range(4)]
        vtd = AR.alloc([128, TPC + 1, 512], BF16)
        t_vtd = [Tok() for _ in range(TPC + 1)]
        T512 = AR.rot(4, [128, 512], F32)
        kvo_r = E_r = rd_r = sg_r = T512
        PT_r = AR.rot(3, [128, 512], BF16)
        kbuf_r = AR.rot(2, [128, 64], F32)
        vbuf_r = AR.rot(2, [128, 64], F32)
        kbT_r = AR.rot(2, [64, 1, 128], BF16)
        vbd_r = AR.rot(2, [128, 256], BF16)

        def mlstm_core(n, h, kTv, qTv, ArU, Ucol, emcol, Acs, Ace, vaug, gs, t_in, Cb_ap, Cf_ap, Cout_ap, t_Cst, t_Cout, yaT_dst):
            pS, tS = psum()
            mm(pS[0:n, 0:n], kTv, qTv, True, True, r=t_in, w=[tS])
            AM, tAM = AM_r.next()
            tt('pool', AM[0:n, 0:n], ArU[0:n, :], maskbig[0:n, 0:n], ALU.add, r=t_in + [t_c], w=[tAM])
            WT, tWT = WT_r.next()
            act(WT[0:n, 0:n], AM[0:n, 0:n], AF.Exp, r=[tAM] + t_in, w=[tWT], bias=Ucol, scale=-1.0)
            STw, tST = STw_r.next()
            tt('dve', STw[0:n, 0:n], pS[0:n, 0:n], WT[0:n, 0:n], ALU.mult, r=[tS, tWT], w=[tST])
            Wi, tWi = Wi_r.next()
            act(Wi[:, 0:n], ArU, AF.Exp, r=t_in, w=[tWi], bias=Acs, scale=-1.0)
            qw, tqw = qw_r.next()
            tt('pool', qw[:, 0:n], qTv, Wi[:, 0:n], ALU.mult, r=t_in + [tWi], w=[tqw])
            pN, tN = psum()
            mm(pN[0:n, 0:257], STw[0:n, 0:n], vaug, True, False, r=[tST] + t_in, w=[tN])
            mm(pN[0:n, 0:257], qw[:, 0:n], Cb_ap, False, True, r=[tqw, t_Cst], w=[tN])
            s_, st_ = small.next()
            act(s_[0:n, 0:1], pN[0:n, 256:257], AF.Abs, r=[tN], w=[st_])
            ts('dve', s_[0:n, 1:2], s_[0:n, 0:1], emcol, ALU.max, r=[st_] + t_in, w=[st_])
            P.op('dve', lambda e: e.bn_stats(out=s_[0:n, 2:8], in_=pN[0:n, 0:256]), r=[tN], w=[st_])
            P.op('dve', lambda e: e.bn_aggr(out=s_[0:n, 8:10], in_=s_[0:n, 2:8]), r=[st_], w=[st_])
            ts('dve', s_[0:n, 10:11], s_[0:n, 1:2], s_[0:n, 1:2], ALU.mult, r=[st_], w=[st_], s2=EPS, op1=ALU.mult)
            tt('dve', s_[0:n, 10:11], s_[0:n, 10:11], s_[0:n, 9:10], ALU.add, r=[st_], w=[st_])
            tt('pool', s_[0:n, 11:12], s_[0:n, 10:11], mhalf[0:n, 0:1], ALU.pow, r=[st_, t_c], w=[st_])
            ya, tya = ya_r.next()
            ts('dve', ya[0:n, :], pN[0:n, 0:256], s_[0:n, 8:9], ALU.subtract, r=[tN, st_], w=[tya], s2=s_[0:n, 11:12], op1=ALU.mult)
            tt('pool', ya[0:n, :], ya[0:n, :], gs, ALU.mult, r=[tya] + t_in, w=[tya])
            pY, tY = psum()
            for vc in range(2):
                tr(pY[:, vc * 128:vc * 128 + n], ya[0:n, vc * 128:(vc + 1) * 128], ident_f[0:n, 0:n], r=[tya, t_c], w=[tY])
            cp('act', yaT_dst, pY[:, 0:256].rearrange("p (a b) -> p a b", a=2)[:, :, 0:n], r=[tY], w=[t_yaT])
            s2_, st2 = small.next()
            ts('dve', s2_[0:n, 0:1], Ucol, Ace[0:n, :], ALU.subtract, r=t_in, w=[st2])
            act(s2_[0:n, 0:1], s2_[0:n, 0:1], AF.Exp, r=[st2], w=[st2])
            tt('dve', s2_[:, 1:2], Acs, Ace, ALU.subtract, r=t_in, w=[st2])
            act(s2_[:, 1:2], s2_[:, 1:2], AF.Exp, r=[st2], w=[st2])
            pK, tK = psum()
            mm(pK[0:n, 0:128], kTv, ident_b[:], True, True, r=t_in + [t_c], w=[tK])
            kw, tkw = kw_r.next()
            ts('dve', kw[0:n, :], pK[0:n, 0:128], s2_[0:n, 0:1], ALU.mult, r=[tK, st2], w=[tkw])
            pU, tU = psum()
            mm(pU[:, 0:257], kw[0:n, :], vaug, True, True, r=[tkw] + t_in, w=[tU])
            stt(Cout_ap, Cf_ap, s2_[:, 1:2], pU[:, 0:257], ALU.mult, ALU.add, r=[t_Cst, st2, tU], w=[t_Cout])

        NPC = TP // CH
        chunks = [(c, CH) for c in range(NPC)] + [(NPC, 64)]
        for (c, N) in chunks:
            is_s = (c == NPC)
            tiles = [16] if is_s else list(range(TPC * c, TPC * c + TPC))
            transpose_tiles(xT_c, t_xT, tiles, 0)
            wg_v, wg_t = wblock([w_in[l][:, O_IG:O_IG + 8].rearrange("(kc p) n -> p kc n", p=128)])
            wg = wg_v[0]
            pI, tI = psum()
            pF, tF = psum()
            for kc in range(8):
                mm(pI[0:4, 0:N], wg[:, kc, 0:4], xT_c[:, kc, 0:N], kc == 0, kc == 7, r=[wg_t, t_xT], w=[tI])
            for kc in range(8):
                mm(pF[0:4, 0:N], wg[:, kc, 4:8], xT_c[:, kc, 0:N], kc == 0, kc == 7, r=[wg_t, t_xT], w=[tF])
            act(Urow[:, 0:N], pI[0:4, 0:N], AF.Identity, r=[tI, t_bcol], w=[t_rows], bias=bcol[0:4, SL_IG:SL_IG + 1])
            act(fgr[:, 0:N], pF[0:4, 0:N], AF.Identity, r=[tF, t_bcol], w=[t_rows], bias=bcol[0:4, SL_FG:SL_FG + 1])
            act(fgr[:, 0:N], fgr[:, 0:N], AF.Exp, r=[t_rows], w=[t_rows], scale=-1.0)
            act(fgr[:, 0:N], fgr[:, 0:N], AF.Ln, r=[t_rows], w=[t_rows], bias=1.0)
            if not is_s:
                if c == 0:
                    P.op('dve', lambda e: e.tensor_tensor_scan(out=Brow[:, :], data0=ones_row[:, :], data1=fgr[:, :], initial=0.0, op0=ALU.mult, op1=ALU.subtract), r=[t_rows], w=[t_rows])
                else:
                    P.op('dve', lambda e: e.tensor_tensor_scan(out=Brow[:, :], data0=ones_row[:, :], data1=fgr[:, :], initial=carry[:, 0:1], op0=ALU.mult, op1=ALU.subtract), r=[t_rows, t_carry], w=[t_rows])
                tt('dve', Urow[:, :], Urow[:, :], Brow[:, :], ALU.subtract, r=[t_rows], w=[t_rows])
                if c == 0:
                    P.op('dve', lambda e: e.tensor_tensor_scan(out=Arow[:, :], data0=Urow[:, :], data1=Urow[:, :], initial=0.0, op0=ALU.max, op1=ALU.max), r=[t_rows], w=[t_rows])
                else:
                    P.op('dve', lambda e: e.tensor_tensor_scan(out=Arow[:, :], data0=Urow[:, :], data1=Urow[:, :], initial=carry[:, 1:2], op0=ALU.max, op1=ALU.max), r=[t_rows, t_carry], w=[t_rows])
                cp('dve', carry[:, 0:1], Brow[:, CH - 1:CH], r=[t_rows], w=[t_carry])
                cp('dve', carry[:, 1:2], Arow[:, CH - 1:CH], r=[t_rows], w=[t_carry])
            else:
                dma(m0row, sm[l].rearrange("(s h) -> h s", h=4), w=[t_m0], slow=True)
                dma(m0rep, sm[l:l + 1, :].to_broadcast([128, 64]), w=[t_m0])
                f3 = fgr[:, 0:64].rearrange("p (s t) -> p s t", t=4)
                B3 = Brow[:, 0:64].rearrange("p (s t) -> p s t", t=4)
                U3 = Urow[:, 0:64].rearrange("p (s t) -> p s t", t=4)
                A3 = Arow[:, 0:64].rearrange("p (s t) -> p s t", t=4)
                ts('dve', B3[:, :, 0], f3[:, :, 0], -1.0, ALU.mult, r=[t_rows], w=[t_rows])
                for t in range(1, 4):
                    tt('dve', B3[:, :, t], B3[:, :, t - 1], f3[:, :, t], ALU.subtract, r=[t_rows], w=[t_rows])
                tt('dve', Urow[:, 0:64], Urow[:, 0:64], Brow[:, 0:64], ALU.subtract, r=[t_rows], w=[t_rows])
                tt('dve', A3[:, :, 0], U3[:, :, 0], m0row[:, :], ALU.max, r=[t_rows, t_m0], w=[t_rows])
                for t in range(1, 4):
                    tt('dve', A3[:, :, t], A3[:, :, t - 1], U3[:, :, t], ALU.max, r=[t_rows], w=[t_rows])
                tt('dve', msrow[:, :], A3[:, :, 3], B3[:, :, 3], ALU.add, r=[t_rows], w=[t_m0])
                dma(o_ms[l].rearrange("(s h) -> h s", h=4), msrow[:, :], r=[t_m0], slow=True)
            stt(fgr[:, 0:N], Arow[:, 0:N], -1.0, Brow[:, 0:N], ALU.mult, ALU.subtract, r=[t_rows], w=[t_rows])
            pC, tC = psum()
            if not is_s:
                for jj in range(TPC):
                    tr(pC[:, jj * 8:jj * 8 + 4], Urow[:, jj * 128:(jj + 1) * 128], ident_f[0:4, 0:4], r=[t_rows, t_c], w=[tC])
                    tr(pC[:, jj * 8 + 4:jj * 8 + 8], fgr[:, jj * 128:(jj + 1) * 128], ident_f[0:4, 0:4], r=[t_rows, t_c], w=[tC])
                cp('dve', cols[:, :, 0:4], pC[:, 0:8 * TPC].rearrange("p (a b) -> p a b", a=TPC)[:, :, 0:4], r=[tC], w=[t_cols])
                act(cols[:, :, 4:8], pC[:, 0:8 * TPC].rearrange("p (a b) -> p a b", a=TPC)[:, :, 4:8], AF.Exp, r=[tC], w=[t_cols])
            else:
                for s in range(16):
                    tr(pC[0:4, s * 8:s * 8 + 4], Urow[:, s * 4:(s + 1) * 4], ident_f[0:4, 0:4], r=[t_rows, t_c], w=[tC])
                    tr(pC[0:4, s * 8 + 4:s * 8 + 8], fgr[:, s * 4:(s + 1) * 4], ident_f[0:4, 0:4], r=[t_rows, t_c], w=[tC])
                cp('dve', cols_s[:, :, 0:4], pC[0:4, 0:128].rearrange("p (a b) -> p a b", a=16)[:, :, 0:4], r=[tC], w=[t_cols])
                act(cols_s[:, :, 4:8], pC[0:4, 0:128].rearrange("p (a b) -> p a b", a=16)[:, :, 4:8], AF.Exp, r=[tC], w=[t_cols])

            for h in range(4):
                ArU, tAr = ArepU.next()
                pA, tA = psum()
                mm(pA[:, 0:N], sel4[:, h, :], Arow[:, 0:N], True, True, r=[t_c, t_rows], w=[tA])
                cp('act', ArU[:, 0:N], pA[:, 0:N], r=[tA], w=[tAr])
                wA, wA_t = wmulti([(w_in[l], O_MQ + h * 128, 128), (w_in[l], O_MK + h * 128, 128)])
                wB, wB_t = wmulti([(w_in[l], O_MV + h * 256, 256), (w_in[l], O_OG + h * 256, 256)])
                qT, tq = qT_r.next()
                kT, tk = kT_r.next()
                pq, tpq = psum()
                for kc in range(8):
                    mm(pq[:, 0:N], wA[:, kc, 0:128], xT_c[:, kc, 0:N], kc == 0, kc == 7, r=[wA_t, t_xT], w=[tpq])
                ts('dve', qT[:, 0:N], pq[:, 0:N], bcol[:, SL_MQ + h:SL_MQ + h + 1], ALU.add, r=[tpq, t_bcol], w=[tq], s2=float(128 ** -0.5), op1=ALU.mult)
                pk, tpk = psum()
                for kc in range(8):
                    mm(pk[:, 0:N], wA[:, kc, 128:256], xT_c[:, kc, 0:N], kc == 0, kc == 7, r=[wA_t, t_xT], w=[tpk])
                act(kT[:, 0:N], pk[:, 0:N], AF.Identity, r=[tpk, t_bcol], w=[tk], bias=bcol[:, SL_MK + h:SL_MK + h + 1])
                gain, t_gain = gain_r.next()
                dma(gain[:, :], mh_gain[l:l + 1, h * 256:(h + 1) * 256].to_broadcast([128, 256]), w=[t_gain])
                bt_, tbt = bt_r.next()
                dma(bt_[:, 0:256], b_in[l:l + 1, O_MV + h * 256:O_MV + (h + 1) * 256].to_broadcast([128, 256]), w=[tbt])
                dma(bt_[:, 256:512], b_in[l:l + 1, O_OG + h * 256:O_OG + (h + 1) * 256].to_broadcast([128, 256]), w=[tbt])
                units = [(jj, 128, jj * 128) for jj in range(TPC)] if not is_s else [(s, 4, s * 4) for s in range(16)]
                for (u, n, c0) in units:
                    cs = slice(c0, c0 + n)
                    pv, tpv = psum()
                    for kc in range(8):
                        mm(pv[0:n, :], xT_c[:, kc, cs], wB[:, kc, :], kc == 0, kc == 7, r=[wB_t, t_xT], w=[tpv])
                    va, tva = vaug_r.next()
                    tt('dve', va[0:n, 0:256], pv[0:n, 0:256], bt_[0:n, 0:256], ALU.add, r=[tpv, tbt], w=[tva])
                    og, tog = ogt_r.next()
                    tt('dve', og[0:n, :], pv[0:n, 256:512], bt_[0:n, 256:512], ALU.add, r=[tpv, tbt], w=[tog])
                    act(og[0:n, :], og[0:n, :], AF.Tanh, r=[tog], w=[tog], scale=0.5)
                    act(og[0:n, :], og[0:n, :], AF.Identity, r=[tog], w=[tog], bias=0.5, scale=0.5)
                    tt('pool', og[0:n, :], og[0:n, :], gain[0:n, :], ALU.mult, r=[tog, t_gain], w=[tog])
                    t_in = [tq, tk, tAr, t_cols, tva, tog, t_acs, t_m0]
                    yaT_dst = yaT[:, 2 * h:2 * h + 2, cs]
                    if not is_s:
                        Acs = acs_c[:, h:h + 1] if u == 0 else ArU[:, c0 - 1:c0]
                        Ace = ArU[:, c0 + n - 1:c0 + n]
                        mlstm_core(n, h, kT[:, cs], qT[:, cs], ArU[:, cs], cols[:, u, h:h + 1], cols[:, u, 4 + h:5 + h],
                                   Acs, Ace, va[0:n, :], og[0:n, :], t_in, C_b[:, h, :], C_f[:, h, :], C_f[:, h, :], t_C[h], t_C[h], yaT_dst)
                        cp('act', C_b[:, h, :], C_f[:, h, :], r=[t_C[h]], w=[t_C[h]])
                    else:
                        cf, tcf = Cs_f.next()
                        cb, tcb = Cs_b.next()
                        co, tco = Cs_o.next()
                        dma(cf[:, 0:256], sC[l, u, h], w=[tcf])
                        dma(cf[:, 256:257], sn[l, u, h, :].unsqueeze(1), w=[tcf])
                        cp('pool', cb[:, :], cf[:, :], r=[tcf], w=[tcb])
                        Acs = m0rep[:, u * 4 + h:u * 4 + h + 1]
                        Ace = ArU[:, c0 + n - 1:c0 + n]
                        mlstm_core(n, h, kT[:, cs], qT[:, cs], ArU[:, cs], cols_s[0:4, u, h:h + 1], cols_s[0:4, u, 4 + h:5 + h],
                                   Acs, Ace, va[0:n, :], og[0:n, :], t_in + [tcb], cb[:, :], cf[:, :], co[:, :], tcf, tco, yaT_dst)
                        dma(o_Cs[l, u, h], co[:, 0:256], r=[tco])
                        dma(o_ns[l, u, h, :].unsqueeze(1), co[:, 256:257], r=[tco])
                if not is_s:
                    cp('dve', acs_c[:, h:h + 1], ArU[:, CH - 1:CH], r=[tAr], w=[t_acs])
                    if c == NPC - 1:
                        dma(o_Cp[l, h], C_f[:, h, 0:256], r=[t_C[h]])
                        dma(o_np[l, h, :].unsqueeze(1), C_f[:, h, 256:257], r=[t_C[h]])
            if c == NPC - 1:
                s_, st_ = small.next()
                tt('dve', s_[0:4, 0:1], carry[:, 0:1], carry[:, 1:2], ALU.add, r=[t_carry], w=[st_])
                dma(o_mp[l, :].unsqueeze(1), s_[0:4, 0:1], r=[st_])

            wKV, wKV_t = wmulti([(w_in[l], O_SK, 256), (w_in[l], O_SV, 256)])
            btk_, tbtk = bt_r.next()
            dma(btk_[:, :], b_in[l:l + 1, O_SK:O_SK + 512].to_broadcast([128, 512]), w=[tbtk])
            if not is_s:
                for jj in range(TPC):
                    j = TPC * c + jj
                    pkv, tpkv = psum()
                    for kc in range(8):
                        mm(pkv[:, :], xT_c[:, kc, jj * 128:(jj + 1) * 128], wKV[:, kc, :], kc == 0, kc == 7, r=[wKV_t, t_xT], w=[tpkv])
                    dst = vtd[:, jj + 1, :].rearrange("p (g u d) -> p g u d", g=4, u=2)
                    src = pkv[:, 256:512].rearrange("p (g d) -> p g d", g=4).unsqueeze(2).to_broadcast([128, 4, 2, 64])
                    bsrc = btk_[:, 256:512].rearrange("p (g d) -> p g d", g=4).unsqueeze(2).to_broadcast([128, 4, 2, 64])
                    tt('dve', dst, src, bsrc, ALU.add, r=[tpkv, tbtk], w=[t_vtd[jj + 1]])
                    if j == 15:
                        kvo, tkvo = kvo_r.next()
                        tt('dve', kvo[:, :], pkv[:, :], btk_[:, :], ALU.add, r=[tpkv, tbtk], w=[tkvo])
                        dma(o_kp[l], kvo[:, 0:256], r=[tkvo])
                        dma(o_vp[l], kvo[:, 256:512], r=[tkvo])
            else:
                dma(o_ks[l, :, 0:124, :], sk[l, :, 4:128, :])
                dma(o_vs[l, :, 0:124, :], sv[l, :, 4:128, :])
            for g in range(4):
                wS, wS_t = wmulti([(w_in[l], O_SQ + g * 256, 256), (w_in[l], O_SK + g * 64, 64)])
                for i in range(4):
                    pq, tpq = psum()
                    for kc in range(8):
                        mm(pq[0:64, 0:N], wS[:, kc, i * 64:(i + 1) * 64], xT_c[:, kc, 0:N], kc == 0, kc == 7, r=[wS_t, t_xT], w=[tpq])
                    act(qTg[:, i, 0:N], pq[0:64, 0:N], AF.Identity, r=[tpq, t_bcol], w=[t_qTg], bias=bcol[0:64, SL_SQ + 4 * g + i:SL_SQ + 4 * g + i + 1])
                pk, tpk = psum()
                for kc in range(8):
                    mm(pk[0:64, 0:N], wS[:, kc, 256:320], xT_c[:, kc, 0:N], kc == 0, kc == 7, r=[wS_t, t_xT], w=[tpk])
                act(kTg[:, g, 128:128 + N], pk[0:64, 0:N], AF.Identity, r=[tpk, t_bcol], w=[t_kTg[g]], bias=bcol[0:64, SL_SK + g:SL_SK + g + 1])
                if not is_s:
                    for jj in range(TPC):
                        j = TPC * c + jj
                        cs = slice(jj * 128, (jj + 1) * 128)
                        pNm, tNm = psum()
                        pDn, tDn = psum()
                        kts = ([(1, jj)] if j > 0 else []) + [(0, jj + 1)]
                        for ki, (kind, slot) in enumerate(kts):
                            pS, tS = psum()
                            mm(pS[:, :].rearrange("p (a b) -> p a b", a=4), kTg[:, g, slot * 128:(slot + 1) * 128], qTg[:, :, cs], True, True, r=[t_kTg[g], t_qTg], w=[tS])
                            E_, tE = E_r.next()
                            act(E_[:, :], pS[:, :], AF.Exp, r=[tS], w=[tE], scale=0.125)
                            PT, tPT = PT_r.next()
                            tt('dve', PT[:, :].rearrange("p (a b) -> p a b", a=4), E_[:, :].rearrange("p (a b) -> p a b", a=4), EB[:, 4 * g:4 * g + 4, kind, :], ALU.mult, r=[tE, t_c], w=[tPT])
                            first = ki == 0
                            last = ki == len(kts) - 1
                            mm(pNm[:, :], vtd[:, slot, g * 128:(g + 1) * 128], PT[:, :], first, last, r=[t_vtd[slot], tPT], w=[tNm])
                            mm(pDn[:, :], ones_b[:, :], PT[:, :], first, last, r=[tPT, t_c], w=[tDn])
                        rd, trd = rd_r.next()
                        for i in range(4):
                            ts('dve', rd[:, i * 128:(i + 1) * 128], pDn[:, i * 128:(i + 1) * 128], esink[:, 4 * g + i:4 * g + i + 1], ALU.add, r=[tDn, t_esink], w=[trd])
                        P.op('dve', lambda e, rd=rd: e.reciprocal(out=rd[:, :], in_=rd[:, :]), r=[trd], w=[trd])
                        for par in range(2):
                            ps_ = slice(par * 64, (par + 1) * 64)
                            nv = pNm[ps_, :].rearrange("p (a b) -> p a b", a=4)[:, par::2, :]
                            rv = rd[ps_, :].rearrange("p (a b) -> p a b", a=4)[:, par::2, :]
                            tt('dve', ybT[ps_, 2 * g:2 * g + 2, cs], nv, rv, ALU.mult, r=[tNm, trd], w=[t_ybT])
                else:
                    for s in range(16):
                        cs = slice(4 * s, 4 * s + 4)
                        kb, tkb = kbuf_r.next()
                        vb, tvb = vbuf_r.next()
                        dma(kb[:, :], sk[l, s, :, g * 64:(g + 1) * 64], w=[tkb])
                        dma(vb[:, :], sv[l, s, :, g * 64:(g + 1) * 64], w=[tvb])
                        pT_, tT_ = psum()
                        tr(pT_[0:64, 0:128], kb[:, :], ident_f[:, :], r=[tkb, t_c], w=[tT_])
                        kbT, tkbT = kbT_r.next()
                        cp('act', kbT[:, 0, :], pT_[0:64, 0:128], r=[tT_], w=[tkbT])
                        vbd, tvbd = vbd_r.next()
                        cp('pool', vbd[:, 0:128].rearrange("p (u d) -> p u d", u=2), vb[:, :].unsqueeze(1).to_broadcast([128, 2, 64]), r=[tvb], w=[tvbd])
                        pkv, tpkv = psum()
                        for kc in range(8):
                            mm(pkv[0:4, :], xT_c[:, kc, cs], wKV[:, kc, :], kc == 0, kc == 7, r=[wKV_t, t_xT], w=[tpkv])
                        kvo, tkvo = kvo_r.next()
                        tt('dve', kvo[0:4, :], pkv[0:4, :], btk_[0:4, :], ALU.add, r=[tpkv, tbtk], w=[tkvo])
                        if g == 0:
                            dma(o_ks[l, s, 124:128, :], kvo[0:4, 0:256], r=[tkvo])
                            dma(o_vs[l, s, 124:128, :], kvo[0:4, 256:512], r=[tkvo])
                        cp('pool', vbd[0:4, 128:256].rearrange("p (u d) -> p u d", u=2), kvo[0:4, 256 + g * 64:256 + (g + 1) * 64].unsqueeze(1).to_broadcast([4, 2, 64]), r=[tkvo], w=[tvbd])
                        pS1, tS1 = psum()
                        mm(pS1[:, 0:16].rearrange("p (a b) -> p a b", a=4), kbT[:, 0, :], qTg[:, :, cs], True, True, r=[tkbT, t_qTg], w=[tS1])
                        pS2, tS2 = psum()
                        mm(pS2[0:4, 0:16].rearrange("p (a b) -> p a b", a=4), kTg[:, g, 128 + 4 * s:128 + 4 * s + 4], qTg[:, :, cs], True, True, r=[t_kTg[g], t_qTg], w=[tS2])
                        E_, tE = E_r.next()
                        act(E_[:, 0:16], pS1[:, 0:16], AF.Exp, r=[tS1], w=[tE], scale=0.125)
                        act(E_[0:4, 16:32], pS2[0:4, 0:16], AF.Exp, r=[tS2], w=[tE], scale=0.125)
                        PT, tPT = PT_r.next()
                        tt('dve', PT[:, 0:16].rearrange("p (a b) -> p a b", a=4), E_[:, 0:16].rearrange("p (a b) -> p a b", a=4), EB[:, 4 * g:4 * g + 4, 1, 0:4], ALU.mult, r=[tE, t_c], w=[tPT])
                        tt('dve', PT[0:4, 16:32].rearrange("p (a b) -> p a b", a=4), E_[0:4, 16:32].rearrange("p (a b) -> p a b", a=4), EB[0:4, 4 * g:4 * g + 4, 0, 0:4], ALU.mult, r=[tE, t_c], w=[tPT])
                        pNm, tNm = psum()
                        pDn, tDn = psum()
                        mm(pNm[:, 0:16], vbd[:, 0:128], PT[:, 0:16], True, False, r=[tvbd, tPT], w=[tNm])
                        mm(pNm[:, 0:16], vbd[0:4, 128:256], PT[0:4, 16:32], False, True, r=[tvbd, tPT], w=[tNm])
                        mm(pDn[:, 0:16], ones_b[:, :], PT[:, 0:16], True, False, r=[tPT, t_c], w=[tDn])
                        mm(pDn[:, 0:16], ones_b[0:4, :], PT[0:4, 16:32], False, True, r=[tPT, t_c], w=[tDn])
                        rd, trd = rd_r.next()
                        for i in range(4):
                            ts('dve', rd[:, i * 4:(i + 1) * 4], pDn[:, i * 4:(i + 1) * 4], esink[:, 4 * g + i:4 * g + i + 1], ALU.add, r=[tDn, t_esink], w=[trd])
                        P.op('dve', lambda e, rd=rd: e.reciprocal(out=rd[:, 0:16], in_=rd[:, 0:16]), r=[trd], w=[trd])
                        for par in range(2):
                            ps_ = slice(par * 64, (par + 1) * 64)
                            nv = pNm[ps_, 0:16].rearrange("p (a b) -> p a b", a=4)[:, par::2, :]
                            rv = rd[ps_, 0:16].rearrange("p (a b) -> p a b", a=4)[:, par::2, :]
                            tt('dve', ybT[ps_, 2 * g:2 * g + 2, cs], nv, rv, ALU.mult, r=[tNm, trd], w=[t_ybT])
                if not is_s:
                    cp('pool', kTg[:, g, 0:128], kTg[:, g, CH:CH + 128], r=[t_kTg[g]], w=[t_kTg[g]])
            if not is_s:
                cp('pool', vtd[:, 0, :], vtd[:, TPC, :], r=[t_vtd[TPC]], w=[t_vtd[0]])

            for m in range(8):
                wM, wM_t = wmulti([(w_a[l], m * 128, 128), (w_b[l], m * 128, 128), (w_in[l], O_GA + m * 128, 128), (w_in[l], O_GB + m * 128, 128)])
                res = []
                for (wi, gi, srcT, tsrc, slot) in ((0, 2, yaT, t_yaT, SL_GA + m), (1, 3, ybT, t_ybT, SL_GB + m)):
                    pg, tpg = psum()
                    for kc in range(8):
                        mm(pg[:, 0:N], wM[:, kc, gi * 128:(gi + 1) * 128], xT_c[:, kc, 0:N], kc == 0, kc == 7, r=[wM_t, t_xT], w=[tpg])
                    sg, tsg = sg_r.next()
                    act(sg[:, 0:N], pg[:, 0:N], AF.Tanh, r=[tpg, t_bcolh], w=[tsg], bias=bcolh[:, slot - SL_GA:slot - SL_GA + 1], scale=0.5)
                    act(sg[:, 0:N], sg[:, 0:N], AF.Identity, r=[tsg], w=[tsg], bias=0.5, scale=0.5)
                    pa, tpa = psum()
                    for kc in range(8):
                        mm(pa[:, 0:N], wM[:, kc, wi * 128:(wi + 1) * 128], srcT[:, kc, 0:N], kc == 0, kc == 7, r=[wM_t, tsrc], w=[tpa])
                    tt('dve', sg[:, 0:N], sg[:, 0:N], pa[:, 0:N], ALU.mult, r=[tsg, tpa], w=[tsg])
                    res.append((sg, tsg))
                tt('pool', merged[:, m, 0:N], res[0][0][:, 0:N], res[1][0][:, 0:N], ALU.add, r=[res[0][1], res[1][1]], w=[t_mg])
            if c == 0:
                load_gb(l, 0)
            for half in range(2):
                wO, wO_t = wmat(w_out[l], half * 512, 512)
                for u, j in enumerate(tiles):
                    n = tile_rows(j)
                    po, tpo = psum()
                    for kc in range(8):
                        mm(po[0:n, :], merged[:, kc, u * 128:u * 128 + n], wO[:, kc, :], kc == 0, kc == 7, r=[wO_t, t_mg], w=[tpo])
                    xs = x_tok[0:n, j, half * 512:(half + 1) * 512]
                    stt(xs, xs, ALPHA, po[0:n, :], ALU.mult, ALU.add, r=[t_x[j], tpo], w=[t_x[j]])
            for j in tiles:
                layer_norm(j, l, 0, t_gbt)

        P.barrier()
        AR.reset()
        xT_all = AR.alloc([128, 8, TALL], BF16)
        t_xTa = Tok()
        HT = AR.alloc([128, 4, TALL], BF16)
        t_HT = [[Tok() for _ in range(4)] for _ in range(5)]
        sgm_r = AR.rot(2, [128, 512], F32)
        lg_r = AR.rot(2, [128, 80], F32)
        transpose_tiles(xT_all, t_xTa, list(range(NT)), 0)
        load_gb(l, 1)
        dma(brt[:, 0:4], b_rg[l:l + 1, :].to_broadcast([128, 4]), w=[t_brt])
        dma(brt[:, 4:36], b_re[l:l + 1, :].to_broadcast([128, 32]), w=[t_brt])
        wR, wR_t = wmulti([(w_rg[l], 0, 4), (w_re[l], 0, 32)])
        for j in range(NT):
            n = tile_rows(j)
            c0 = j * 128
            pr, tpr = psum()
            for kc in range(8):
                mm(pr[0:n, 0:36], xT_all[:, kc, c0:c0 + n], wR[:, kc, :], kc == 0, kc == 7, r=[wR_t, t_xTa], w=[tpr])
            lg, tlg = lg_r.next()
            L = lg[0:n, :]
            tt('dve', L[:, 0:36], pr[0:n, 0:36], brt[0:n, :], ALU.add, r=[tpr, t_brt], w=[tlg])
            s_, st_ = small.next()
            S = s_[0:n, :]
            P.op('dve', lambda e, S=S, L=L: e.reduce_max(out=S[:, 0:1], in_=L[:, 0:4], axis=mybir.AxisListType.X), r=[tlg], w=[st_])
            ts('dve', S[:, 1:2], S[:, 0:1], -1.0, ALU.mult, r=[st_], w=[st_])
            P.op('act', lambda e, S=S, L=L: e.activation(out=L[:, 36:40], in_=L[:, 0:4], func=AF.Exp, bias=S[:, 1:2], scale=1.0, accum_out=S[:, 2:3]), r=[tlg, st_], w=[tlg, st_])
            ts('dve', L[:, 40:44], L[:, 0:4], S[:, 0:1], ALU.is_equal, r=[tlg, st_], w=[tlg])
            ts('dve', L[:, 40:44], L[:, 40:44], BIG, ALU.mult, r=[tlg], w=[tlg], s2=-BIG, op1=ALU.add)
            tt('dve', L[:, 44:76].rearrange("p (g i) -> p g i", g=4), L[:, 4:36].rearrange("p (g i) -> p g i", g=4), L[:, 40:44].unsqueeze(2).to_broadcast([n, 4, 8]), ALU.add, r=[tlg], w=[tlg])
            P.op('dve', lambda e, S=S, L=L: e.max(out=S[:, 8:16], in_=L[:, 44:76]), r=[tlg], w=[st_])
            ts('dve', S[:, 3:4], S[:, 8:9], -1.0, ALU.mult, r=[st_], w=[st_])
            ts('dve', L[:, 4:36], L[:, 44:76], S[:, 9:10], ALU.is_ge, r=[tlg, st_], w=[tlg])
            act(L[:, 44:76], L[:, 44:76], AF.Exp, r=[tlg, st_], w=[tlg], bias=S[:, 3:4])
            stt(L[:, 44:76], L[:, 44:76], 1.0, L[:, 4:36], ALU.mult, ALU.mult, r=[tlg], w=[tlg, st_], accum=S[:, 4:5])
            tt('dve', S[:, 5:6], S[:, 4:5], S[:, 2:3], ALU.mult, r=[st_], w=[st_])
            P.op('dve', lambda e, S=S: e.reciprocal(out=S[:, 6:7], in_=S[:, 5:6]), r=[st_], w=[st_])
            ts('dve', gate_full[0:n, j, :], L[:, 44:76], S[:, 6:7], ALU.mult, r=[tlg, st_], w=[t_gate[j]])
            P.op('act', lambda e, n=n, j=j: e.mul(out=x_tok[0:n, j, :], in_=x_tok[0:n, j, :], mul=ALPHA), r=[t_x[j], t_xTa], w=[t_x[j]])
        mchunks = [(cc * 512, 512) for cc in range(4)] + [(2048, 64)]
        for ex in range(32):
            wG, wG_t = wmat(w_eg[l, ex], 0, 512)
            wU, wU_t = wmat(w_eu[l, ex], 0, 512)
            wD, wD_t = wmat(w_ed[l, ex], 0, 1024, kcs=4)
            for ci, (c0, N) in enumerate(mchunks):
                for fc in range(4):
                    pg, tpg = psum()
                    pu, tpu = psum()
                    for kc in range(8):
                        mm(pg[:, 0:N], wG[:, kc, fc * 128:(fc + 1) * 128], xT_all[:, kc, c0:c0 + N], kc == 0, kc == 7, r=[wG_t, t_xTa], w=[tpg])
                    for kc in range(8):
                        mm(pu[:, 0:N], wU[:, kc, fc * 128:(fc + 1) * 128], xT_all[:, kc, c0:c0 + N], kc == 0, kc == 7, r=[wU_t, t_xTa], w=[tpu])
                    sg, tsg = sgm_r.next()
                    act(sg[:, 0:N], pg[:, 0:N], AF.Silu, r=[tpg], w=[tsg])
                    tt('dve', HT[:, fc, c0:c0 + N], sg[:, 0:N], pu[:, 0:N], ALU.mult, r=[tsg, tpu], w=[t_HT[ci][fc]])
            for j in range(NT):
                n = tile_rows(j)
                ci = min(j // 4, 4)
                for half in range(2):
                    py, tpy = psum()
                    for fc in range(4):
                        mm(py[0:n, :], HT[:, fc, j * 128:j * 128 + n], wD[:, fc, half * 512:(half + 1) * 512], fc == 0, fc == 3, r=[wD_t, t_HT[ci][fc]], w=[tpy])
                    xs = x_tok[0:n, j, half * 512:(half + 1) * 512]
                    stt(xs, py[0:n, :], gate_full[0:n, j, ex:ex + 1], xs, ALU.mult, ALU.add, r=[tpy, t_gate[j], t_x[j]], w=[t_x[j]])
        for j in range(NT):
            layer_norm(j, l, 1, t_gbt)

        P.barrier()
        AR.reset()
        xT_all = AR.alloc([128, 8, TALL], BF16)
        t_xTa = Tok()
        pT_all = AR.alloc([128, 2, TALL], BF16)
        t_pT = Tok()
        ptile_r = AR.rot(2, [128, 256], F32)
        sgp_r = AR.rot(2, [128, 512], F32)
        transpose_tiles(xT_all, t_xTa, list(range(NT)), 0)
        load_gb(l, 2)
        for j in range(NT):
            n = tile_rows(j)
            c0 = j * 128
            pt, tpt = ptile_r.next()
            dma(pt[0:n, :], pin[l, c0:c0 + n, :], w=[tpt])
            pb, ptk = psum()
            for q in range(2):
                tr(pb[:, q * 128:q * 128 + n], pt[0:n, q * 128:(q + 1) * 128], ident_f[0:n, 0:n], r=[tpt, t_c], w=[ptk])
            cp('act', pT_all[:, :, c0:c0 + n], pb[:, 0:256].rearrange("p (a b) -> p a b", a=2)[:, :, 0:n], r=[ptk], w=[t_pT])
            P.op('act', lambda e, n=n, j=j: e.mul(out=x_tok[0:n, j, :], in_=x_tok[0:n, j, :], mul=ALPHA), r=[t_x[j], t_xTa], w=[t_x[j]])
        for half in range(2):
            wPG, wPG_t = wmat(w_pg[l], half * 512, 512)
            wPP, wPP_t = wmat(w_pp[l], half * 512, 512, kcs=2)
            for j in range(NT):
                n = tile_rows(j)
                c0 = j * 128
                p1, tp1 = psum()
                for kc in range(8):
                    mm(p1[0:n, :], xT_all[:, kc, c0:c0 + n], wPG[:, kc, :], kc == 0, kc == 7, r=[wPG_t, t_xTa], w=[tp1])
                p2, tp2 = psum()
                for kc in range(2):
                    mm(p2[0:n, :], pT_all[:, kc, c0:c0 + n], wPP[:, kc, :], kc == 0, kc == 1, r=[wPP_t, t_pT], w=[tp2])
                sg, tsg = sgp_r.next()
                act(sg[0:n, :], p1[0:n, :], AF.Sigmoid, r=[tp1], w=[tsg])
                tt('dve', sg[0:n, :], sg[0:n, :], p2[0:n, :], ALU.mult, r=[tsg, tp2], w=[tsg])
                xs = x_tok[0:n, j, half * 512:(half + 1) * 512]
                tt('pool', xs, xs, sg[0:n, :], ALU.add, r=[t_x[j], tsg], w=[t_x[j]])
        for j in range(NT):
            layer_norm(j, l, 2, t_gbt)
        P.barrier()

    dma(o_y[0:1024, :].rearrange("(j p) d -> p j d", p=128), x_tok[:, 0:8, :], r=t_x[0:8])
    dma(o_y[1024:2048, :].rearrange("(j p) d -> p j d", p=128), x_tok[:, 8:16, :], r=t_x[8:16])
    dma(o_y[2048:2112, :], x_tok[0:64, 16, :], r=[t_x[16]])
    P.emit()
    return nc, P


_CACHE = {}


def kernel(x_prompt, x_sample, state_mlstm_C, state_mlstm_n, state_mlstm_m, state_swa_k, state_swa_v,
           p_prompt, p_sample, w_in, b_in, mh_gain, w_a, w_b, w_out, rel_table, w_sink, ln_g, ln_b,
           w_rg, b_rg, w_re, b_re, w_eg, w_eu, w_ed, w_pg, w_pp, _depth=4):
    f = lambda a: np.ascontiguousarray(np.asarray(a, dtype=np.float32))
    if _depth not in _CACHE:
        _CACHE[_depth] = build(_depth)
    nc, P = _CACHE[_depth]
    consts = make_consts()
    shared = dict(w_in=f(w_in), b_in=f(b_in), mh_gain=f(mh_gain), w_a=f(w_a), w_b=f(w_b), w_out=f(w_out),
                  rel_table=f(rel_table), w_sink=f(w_sink), ln_g=f(ln_g), ln_b=f(ln_b), w_rg=f(w_rg), b_rg=f(b_rg),
                  w_re=f(w_re), b_re=f(b_re), w_eg=f(w_eg), w_eu=f(w_eu), w_ed=f(w_ed), w_pg=f(w_pg), w_pp=f(w_pp))
    shared.update(consts)
    x_prompt = f(x_prompt); x_sample = f(x_sample); p_prompt = f(p_prompt); p_sample = f(p_sample)
    sCa = f(state_mlstm_C); sna = f(state_mlstm_n); sma = f(state_mlstm_m); ska = f(state_swa_k); sva = f(state_swa_v)
    in_maps = []
    for c in range(8):
        sl = slice(16 * c, 16 * c + 16)
        m = dict(shared)
        m['xin'] = np.ascontiguousarray(np.concatenate([x_prompt[c], x_sample[sl].reshape(64, D)], 0))
        m['pin'] = np.ascontiguousarray(np.concatenate([p_prompt[:, c], p_sample[:, sl].reshape(4, 64, 256)], 1))
        m['sC'] = np.ascontiguousarray(sCa[:, sl])
        m['sn'] = np.ascontiguousarray(sna[:, sl])
        m['sm'] = np.ascontiguousarray(sma[:, sl].reshape(4, 64))
        m['sk'] = np.ascontiguousarray(ska[:, sl].reshape(4, 16, 128, 256))
        m['sv'] = np.ascontiguousarray(sva[:, sl].reshape(4, 16, 128, 256))
        in_maps.append(m)
    res = run_bass_kernel_spmd(nc, in_maps, core_ids=list(range(8)))
    R = res.results
    y_p = np.stack([R[c]['o_y'][:2048] for c in range(8)], 0)
    y_s = np.concatenate([R[c]['o_y'][2048:].reshape(16, 4, D) for c in range(8)], 0)
    C_p = np.stack([R[c]['o_Cp'] for c in range(8)], 1)
    n_p = np.stack([R[c]['o_np'] for c in range(8)], 1)
    m_p = np.stack([R[c]['o_mp'] for c in range(8)], 1)
    k_p = np.stack([R[c]['o_kp'].reshape(4, 128, 4, 64) for c in range(8)], 1)
    v_p = np.stack([R[c]['o_vp'].reshape(4, 128, 4, 64) for c in range(8)], 1)
    C_s = np.concatenate([R[c]['o_Cs'] for c in range(8)], 1)
    n_s = np.concatenate([R[c]['o_ns'] for c in range(8)], 1)
    m_s = np.concatenate([R[c]['o_ms'].reshape(4, 16, 4) for c in range(8)], 1)
    k_s = np.concatenate([R[c]['o_ks'].reshape(4, 16, 128, 4, 64) for c in range(8)], 1)
    v_s = np.concatenate([R[c]['o_vs'].reshape(4, 16, 128, 4, 64) for c in range(8)], 1)
    outs = (y_p, y_s, C_p, n_p, m_p, k_p, v_p, C_s, n_s, m_s, k_s, v_s)
    return tuple(np.ascontiguousarray(o, dtype=np.float32) for o in outs)
```

```python
import contextlib
import numpy as np
import concourse.bass as bass
import concourse.mybir as mybir
from concourse.bass_utils import run_bass_kernel_spmd

F32 = mybir.dt.float32
BF16 = mybir.dt.bfloat16
AF = mybir.ActivationFunctionType
ALU = mybir.AluOpType

ENGS = ['pe', 'act', 'dve', 'pool', 'sp']
N_DMA_SEM = 8
SAME_ENGINE_SYNC = True

D = 1024
NT = 17
TP = 2048
TS = 64
TALL = TP + TS
DIN = 6664
O_MQ, O_MK, O_MV, O_OG, O_IG, O_FG, O_SQ, O_SK, O_SV, O_GA, O_GB = 0, 512, 1024, 2048, 3072, 3076, 3080, 4104, 4360, 4616, 5640
ALPHA = float((2 * 4) ** 0.25)
EPS = 1e-5
BIG = 30000.0
FW = 400
CH = 256
TPC = CH // 128


class Tok:
    __slots__ = ('w', 'rs', 'const')

    def __init__(self, const=False):
        self.w = None
        self.rs = []
        self.const = const


class Op:
    __slots__ = ('eng', 'fn', 'deps', 'sig', 'count', 'is_dma', 'dma_id', 'waits')


class Prog:
    def __init__(self, nc):
        self.nc = nc
        self.ops = {e: [] for e in ENGS}
        self.n_dma = 0
        self.dmas = []
        self.stack = contextlib.ExitStack()
        self._n = 0

    def sb(self, shape, dtype):
        self._n += 1
        return self.stack.enter_context(self.nc.sbuf_tensor(f"sb{self._n}", list(shape), dtype))

    def ps(self, shape, dtype=F32):
        self._n += 1
        return self.stack.enter_context(self.nc.psum_tensor(f"ps{self._n}", list(shape), dtype))

    def op(self, eng, fn, r=(), w=()):
        o = Op()
        o.eng = eng
        o.fn = fn
        o.deps = set()
        o.sig = False
        o.is_dma = False
        o.count = 0
        for t in r:
            if t.w is not None:
                o.deps.add(t.w)
        for t in w:
            if t.w is not None:
                o.deps.add(t.w)
            for x in t.rs:
                o.deps.add(x)
        for t in r:
            if not t.const:
                t.rs.append(o)
        for t in w:
            t.w = o
            t.rs = []
        o.deps.discard(o)
        self.ops[eng].append(o)
        return o

    def dma(self, fn, r=(), w=()):
        o = self.op('sp', fn, r, w)
        o.is_dma = True
        o.dma_id = self.n_dma
        if self.n_dma >= N_DMA_SEM:
            o.deps.add(self.dmas[self.n_dma - N_DMA_SEM])
        self.n_dma += 1
        self.dmas.append(o)
        return o

    def barrier(self):
        last = []
        for e in ENGS:
            if e == 'sp':
                continue
            if self.ops[e]:
                last.append(self.ops[e][-1])
        last += self.dmas[-N_DMA_SEM:]
        for e in ENGS:
            o = self.op(e, None)
            for d in last:
                if d is not o:
                    o.deps.add(d)

    def emit(self):
        nc = self.nc

        def skip(d, o):
            return d.eng == o.eng and (d.eng in ('pe', 'sp') or not SAME_ENGINE_SYNC)

        for e in ENGS:
            for o in self.ops[e]:
                for d in o.deps:
                    if d.is_dma or skip(d, o):
                        continue
                    d.sig = True
        for e in ENGS:
            c = 0
            for o in self.ops[e]:
                if o.sig and not o.is_dma and o.fn is not None:
                    c += 1
                o.count = c
        sems = {e: self.stack.enter_context(nc.semaphore(f"s_{e}")) for e in ENGS}
        dsems = [self.stack.enter_context(nc.semaphore(f"s_dma{i}")) for i in range(N_DMA_SEM)]

        def dma_target(d):
            return dsems[d.dma_id % N_DMA_SEM], 16 * (d.dma_id // N_DMA_SEM + 1)

        nwaits = 0
        for e in ENGS:
            waited = {}
            for o in self.ops[e]:
                need = {}
                for d in o.deps:
                    if d.is_dma:
                        s, v = dma_target(d)
                    else:
                        if skip(d, o):
                            continue
                        s, v = sems[d.eng], d.count
                    if need.get(s, 0) < v:
                        need[s] = v
                o.waits = []
                for s, v in need.items():
                    if waited.get(s, 0) < v:
                        waited[s] = v
                        o.waits.append((s, v))
                        nwaits += 1
        self.stats = {e: len(self.ops[e]) for e in ENGS}
        self.stats['waits'] = nwaits
        final_dma = {}
        for d in self.dmas:
            s, v = dma_target(d)
            final_dma[s] = max(final_dma.get(s, 0), v)

        def replay(ename, eng):
            for o in self.ops[ename]:
                for s, v in o.waits:
                    eng.wait_ge(s, v)
                if o.fn is None:
                    continue
                ins = o.fn(eng)
                if o.is_dma:
                    s, v = dma_target(o)
                    ins.then_inc(s, 16)
                elif o.sig:
                    ins.then_inc(sems[ename], 1)
            if ename == 'sp':
                for s, v in final_dma.items():
                    eng.wait_ge(s, v)

        with nc.Block() as block:
            @block.tensor
            def _(eng):
                replay('pe', eng)

            @block.scalar
            def _(eng):
                replay('act', eng)

            @block.vector
            def _(eng):
                replay('dve', eng)

            @block.gpsimd
            def _(eng):
                replay('pool', eng)

            @block.sync
            def _(eng):
                replay('sp', eng)
        self.stack.close()


def rel_bucket_np(dist):
    n = np.maximum(dist, 0)
    max_exact = 16
    large = max_exact + (np.log(np.maximum(n, 1) / max_exact) / np.log(128 / max_exact) * (32 - max_exact)).astype(np.int32)
    large = np.minimum(large, 31)
    return np.where(n < max_exact, n, large).astype(np.int32)


def make_consts():
    c = {}
    c['c_ident'] = np.eye(128, dtype=np.float32)
    s = np.arange(128)[:, None]
    l = np.arange(128)[None, :]
    c['c_maskbig'] = np.where(s > l, BIG, 0.0).astype(np.float32)
    i = np.arange(FW)
    dist = i - 144
    valid = (dist >= 0) & (dist <= 128)
    oh = np.zeros((32, FW), np.float32)
    b = rel_bucket_np(dist)
    oh[b[valid], i[valid]] = 1.0
    c['c_ohd'] = oh
    c['c_maskvec'] = np.where(valid, 0.0, -BIG).astype(np.float32)[None, :]
    sel = np.zeros((4, 4, 128), np.float32)
    for h in range(4):
        sel[h, h, :] = 1.0
    c['c_sel4'] = sel.reshape(4, 512)
    return c


def build(depth=4):
    nc = bass.Bass("TRN2", target_bir_lowering=False)
    P = Prog(nc)

    def din(name, shape):
        return nc.dram_tensor(name, list(shape), F32, kind="ExternalInput").ap()

    def dout(name, shape):
        return nc.dram_tensor(name, list(shape), F32, kind="ExternalOutput").ap()

    xin = din("xin", [TALL, D])
    pin = din("pin", [4, TALL, 256])
    sC = din("sC", [4, 16, 4, 128, 256])
    sn = din("sn", [4, 16, 4, 128])
    sm = din("sm", [4, 64])
    sk = din("sk", [4, 16, 128, 256])
    sv = din("sv", [4, 16, 128, 256])
    w_in = din("w_in", [4, D, DIN])
    b_in = din("b_in", [4, DIN])
    mh_gain = din("mh_gain", [4, D])
    w_a = din("w_a", [4, D, D])
    w_b = din("w_b", [4, D, D])
    w_out = din("w_out", [4, D, D])
    rel_table = din("rel_table", [32, 16])
    w_sink = din("w_sink", [4, 16])
    ln_g = din("ln_g", [4, 3, D])
    ln_b = din("ln_b", [4, 3, D])
    w_rg = din("w_rg", [4, D, 4])
    b_rg = din("b_rg", [4, 4])
    w_re = din("w_re", [4, D, 32])
    b_re = din("b_re", [4, 32])
    w_eg = din("w_eg", [4, 32, D, 512])
    w_eu = din("w_eu", [4, 32, D, 512])
    w_ed = din("w_ed", [4, 32, 512, D])
    w_pg = din("w_pg", [4, D, D])
    w_pp = din("w_pp", [4, 256, D])
    c_ident = din("c_ident", [128, 128])
    c_maskbig = din("c_maskbig", [128, 128])
    c_ohd = din("c_ohd", [32, FW])
    c_maskvec = din("c_maskvec", [1, FW])
    c_sel4 = din("c_sel4", [4, 512])

    o_y = dout("o_y", [TALL, D])
    o_Cp = dout("o_Cp", [4, 4, 128, 256])
    o_np = dout("o_np", [4, 4, 128])
    o_mp = dout("o_mp", [4, 4])
    o_kp = dout("o_kp", [4, 128, 256])
    o_vp = dout("o_vp", [4, 128, 256])
    o_Cs = dout("o_Cs", [4, 16, 4, 128, 256])
    o_ns = dout("o_ns", [4, 16, 4, 128])
    o_ms = dout("o_ms", [4, 64])
    o_ks = dout("o_ks", [4, 16, 128, 256])
    o_vs = dout("o_vs", [4, 16, 128, 256])
    scratch = nc.dram_tensor("scratch", [16, 128, FW], F32, kind="Internal").ap()

    def mm(out, lhsT, rhs, start, stop, r, w):
        P.op('pe', lambda e: e.matmul(out, lhsT=lhsT, rhs=rhs, start=start, stop=stop), r=r, w=w)

    def tr(out, in_, ident, r, w):
        P.op('pe', lambda e: e.transpose(out=out, in_=in_, identity=ident), r=r, w=w)

    def act(out, in_, func, r, w, bias=None, scale=1.0):
        if bias is None:
            P.op('act', lambda e: e.activation(out=out, in_=in_, func=func, scale=scale), r=r, w=w)
        else:
            P.op('act', lambda e: e.activation(out=out, in_=in_, func=func, bias=bias, scale=scale), r=r, w=w)

    def tt(eng, out, in0, in1, op, r, w):
        P.op(eng, lambda e: e.tensor_tensor(out=out, in0=in0, in1=in1, op=op), r=r, w=w)

    def ts(eng, out, in0, s1, op0, r, w, s2=None, op1=None):
        if op1 is None:
            P.op(eng, lambda e: e.tensor_scalar(out=out, in0=in0, scalar1=s1, scalar2=None, op0=op0), r=r, w=w)
        else:
            P.op(eng, lambda e: e.tensor_scalar(out=out, in0=in0, scalar1=s1, scalar2=s2, op0=op0, op1=op1), r=r, w=w)

    def stt(out, in0, scalar, in1, op0, op1, r, w, accum=None):
        if accum is None:
            P.op('dve', lambda e: e.scalar_tensor_tensor(out=out, in0=in0, scalar=scalar, in1=in1, op0=op0, op1=op1), r=r, w=w)
        else:
            P.op('dve', lambda e: e.scalar_tensor_tensor(out=out, in0=in0, scalar=scalar, in1=in1, op0=op0, op1=op1, accum_out=accum), r=r, w=w)

    def cp(eng, out, in_, r, w):
        if eng == 'act':
            P.op('act', lambda e: e.copy(out=out, in_=in_), r=r, w=w)
        else:
            P.op(eng, lambda e: e.tensor_copy(out=out, in_=in_), r=r, w=w)

    def dma(out, in_, r=(), w=(), slow=False):
        if slow:
            P.dma(lambda e: e.dma_start(out=out, in_=in_, allow_slow_non_contiguous=True), r=r, w=w)
        else:
            P.dma(lambda e: e.dma_start(out=out, in_=in_), r=r, w=w)

    def memset(eng, ap, val, w):
        P.op(eng, lambda e: e.memset(ap, val), w=w)

    class Rot:
        def __init__(self, bufs):
            self.bufs = bufs
            self.toks = [Tok() for _ in bufs]
            self.i = 0

        def next(self):
            k = self.i % len(self.bufs)
            self.i += 1
            return self.bufs[k], self.toks[k]

    banks = Rot([P.ps([128, 512], F32) for _ in range(8)])

    def psum():
        b, t = banks.next()
        return b, t

    x_tok = P.sb([128, NT, D], F32)
    t_x = [Tok() for _ in range(NT)]
    ident_f = P.sb([128, 128], F32)
    ident_b = P.sb([128, 128], BF16)
    ones_b = P.sb([128, 128], BF16)
    maskbig = P.sb([128, 128], F32)
    sel4 = P.sb([4, 4, 128], F32)
    EB = P.sb([128, 16, 2, 128], BF16)
    mhalf = P.sb([128, 1], F32)
    bcolh = P.sb([128, 16], F32)
    t_bcolh = Tok()
    t_c = Tok()
    NSTG = 3
    stg = Rot([P.sb([128, 1024], F32) for _ in range(NSTG)])
    NRING = 4
    ring = Rot([P.sb([128, 4096], BF16) for _ in range(NRING)])
    gate_full = P.sb([128, NT, 32], F32)
    t_gate = [Tok() for _ in range(NT)]
    gbt = P.sb([128, 2, D], F32)
    t_gbt = Tok()
    bcol = P.sb([128, 48], F32)
    t_bcol = Tok()
    esink = P.sb([128, 16], F32)
    t_esink = Tok()
    brt = P.sb([128, 36], F32)
    t_brt = Tok()
    small = Rot([P.sb([128, 16], F32) for _ in range(6)])
    ARENA_W = 18700
    arena = P.sb([128, ARENA_W], F32)

    class Arena:
        def __init__(self):
            self.off = 0

        def reset(self):
            self.off = 0

        def alloc(self, shape, dtype):
            n = int(np.prod(shape[1:]))
            nw = n if dtype == F32 else (n + 1) // 2
            assert self.off + nw <= ARENA_W, (self.off, nw)
            v = arena[0:shape[0], self.off:self.off + nw]
            self.off += nw
            if dtype != F32:
                v = v.bitcast(dtype)
                if n % 2:
                    v = v[:, 0:n]
            if len(shape) > 2:
                names = " ".join(f"d{i}" for i in range(1, len(shape)))
                v = v.rearrange(f"p ({names}) -> p {names}", **{f"d{i}": shape[i] for i in range(1, len(shape) - 1)})
            return v

        def rot(self, n, shape, dtype):
            return Rot([self.alloc(shape, dtype) for _ in range(n)])

    AR = Arena()

    def wblock(srcs):
        slot, tok = ring.next()
        views = []
        off = 0
        for s in srcs:
            shp = list(s.shape)
            n = int(np.prod(shp[1:]))
            assert n <= 1024
            st, stok = stg.next()
            sv_ = st[0:shp[0], 0:n]
            dv = slot[0:shp[0], off:off + n]
            if len(shp) == 3:
                sv_ = sv_.rearrange("p (a b) -> p a b", a=shp[1])
                dv = dv.rearrange("p (a b) -> p a b", a=shp[1])
            dma(sv_, s, w=[stok])
            cast(dv, sv_, r=[stok], w=[tok])
            views.append(dv)
            off += n
        return views, tok

    cast_state = {'i': 0, 'pat': ['act', 'dve', 'act', 'dve', 'pool']}

    def cast(dst, src, r, w):
        pat = cast_state['pat']
        eng = pat[cast_state['i'] % len(pat)]
        cast_state['i'] += 1
        cp(eng, dst, src, r=r, w=w)

    def wcols(w2d, c0, ncols, kcs=8):
        v = w2d[:, c0:c0 + ncols].rearrange("(kc p) n -> p kc n", p=128)
        per = max(1, 1024 // ncols)
        return [v[:, k0:min(k0 + per, kcs), :] for k0 in range(0, kcs, per)]

    def join_views(views):
        return views

    def wmat(w2d, c0, ncols, kcs=8):
        slot, tok = ring.next()
        v = w2d[:, c0:c0 + ncols].rearrange("(kc p) n -> p kc n", p=128)
        per = max(1, 1024 // ncols)
        full = slot[:, 0:kcs * ncols].rearrange("p (a b) -> p a b", a=kcs)
        for k0 in range(0, kcs, per):
            k1 = min(k0 + per, kcs)
            st, stok = stg.next()
            sv_ = st[:, 0:(k1 - k0) * ncols].rearrange("p (a b) -> p a b", a=k1 - k0)
            dma(sv_, v[:, k0:k1, :], w=[stok])
            cast(full[:, k0:k1, :], sv_, r=[stok], w=[tok])
        return full, tok

    def wmulti(parts):
        slot, tok = ring.next()
        tot = sum(p[2] for p in parts)
        assert 8 * tot <= 4096
        full = slot[:, 0:8 * tot].rearrange("p (a b) -> p a b", a=8)
        o = 0
        for (w2d, c0, ncols) in parts:
            v = w2d[:, c0:c0 + ncols].rearrange("(kc p) n -> p kc n", p=128)
            per = max(1, 1024 // ncols)
            for k0 in range(0, 8, per):
                k1 = min(k0 + per, 8)
                st, stok = stg.next()
                sv_ = st[:, 0:(k1 - k0) * ncols].rearrange("p (a b) -> p a b", a=k1 - k0)
                dma(sv_, v[:, k0:k1, :], w=[stok])
                cast(full[:, k0:k1, o:o + ncols], sv_, r=[stok], w=[tok])
            o += ncols
        return full, tok

    PD = 2

    class WStream:
        def __init__(self, thunks):
            self.thunks = thunks
            self.res = []
            self.consumed = 0

        def next(self):
            i = self.consumed
            self.consumed += 1
            lim = min(len(self.thunks), i + 1 + PD)
            while len(self.res) < lim:
                self.res.append(self.thunks[len(self.res)]())
            return self.res[i]

    dma(ident_f[:], c_ident, w=[t_c])
    dma(maskbig[:], c_maskbig, w=[t_c])
    dma(sel4[:].rearrange("p a b -> p (a b)"), c_sel4, w=[t_c])
    cp('dve', ident_b[:], ident_f[:], r=[t_c], w=[t_c])
    memset('dve', ones_b[:], 1.0, w=[t_c])
    memset('dve', mhalf[:], -0.5, w=[t_c])
    AR.reset()
    rt = AR.alloc([32, 16], F32)
    ohd = AR.alloc([32, FW], F32)
    mvec = AR.alloc([1, FW], F32)
    one1 = AR.alloc([1, 16], F32)
    fsb = AR.alloc([16, FW], F32)
    t_tmp = Tok()
    dma(rt, rel_table, w=[t_tmp])
    dma(ohd, c_ohd, w=[t_tmp])
    dma(mvec, c_maskvec, w=[t_tmp])
    memset('dve', one1, 1.0, w=[t_tmp])
    pb, pt_ = psum()
    mm(pb[0:16, 0:FW], rt, ohd, True, False, r=[t_tmp], w=[pt_])
    mm(pb[0:16, 0:FW], one1, mvec, False, True, r=[t_tmp], w=[pt_])
    t_f = Tok()
    cp('dve', fsb, pb[0:16, 0:FW], r=[pt_], w=[t_f])
    t_scr = Tok()
    dma(scratch, fsb.unsqueeze(1).to_broadcast([16, 128, FW]), r=[t_f], w=[t_scr])
    btmp = AR.rot(3, [128, 128], F32)
    for h in range(16):
        for kind, cc in ((0, 144), (1, 272)):
            base = scratch[h, 0, cc:cc + 128]
            src = bass.AP(tensor=base.tensor, offset=base.offset, ap=[[FW - 1, 128], [1, 128]])
            bt_, btk = btmp.next()
            dma(bt_, src, r=[t_scr], w=[btk])
            act(EB[:, h, kind, :], bt_, AF.Exp, r=[btk], w=[t_c])
    t_c.const = True
    P.barrier()

    dma(x_tok[:, 0:8, :], xin[0:1024, :].rearrange("(j p) d -> p j d", p=128), w=t_x[0:8])
    dma(x_tok[:, 8:16, :], xin[1024:2048, :].rearrange("(j p) d -> p j d", p=128), w=t_x[8:16])
    dma(x_tok[0:64, 16, :], xin[2048:2112, :], w=[t_x[16]])

    def tile_rows(j):
        return 64 if j == 16 else 128

    def transpose_tiles(dst, t_dst, tiles, col0):
        k = 0
        for j in tiles:
            n = tile_rows(j)
            c = col0 + (j - tiles[0]) * 128
            for half in range(2):
                pb, ptk = psum()
                for q in range(4):
                    kc = half * 4 + q
                    tr(pb[:, q * 128:q * 128 + n], x_tok[0:n, j, kc * 128:(kc + 1) * 128], ident_f[0:n, 0:n], r=[t_x[j], t_c], w=[ptk])
                src = pb[:, :].rearrange("p (a b) -> p a b", a=4)[:, :, 0:n]
                cp('act' if k % 2 == 0 else 'dve', dst[:, half * 4:half * 4 + 4, c:c + n], src, r=[ptk], w=[t_dst])
                k += 1

    def layer_norm(j, l, idx, t_g):
        n = tile_rows(j)
        xs = x_tok[0:n, j, :]
        s_, st_ = small.next()
        P.op('dve', lambda e: e.bn_stats(out=s_[0:n, 0:6], in_=x_tok[0:n, j, 0:512]), r=[t_x[j]], w=[st_])
        P.op('dve', lambda e: e.bn_stats(out=s_[0:n, 6:12], in_=x_tok[0:n, j, 512:1024]), r=[t_x[j]], w=[st_])
        P.op('dve', lambda e: e.bn_aggr(out=s_[0:n, 12:14], in_=s_[0:n, 0:12]), r=[st_], w=[st_])
        ts('dve', s_[0:n, 14:15], s_[0:n, 13:14], EPS, ALU.add, r=[st_], w=[st_])
        tt('pool', s_[0:n, 15:16], s_[0:n, 14:15], mhalf[0:n, 0:1], ALU.pow, r=[st_, t_c], w=[st_])
        ts('dve', s_[0:n, 14:15], s_[0:n, 12:13], s_[0:n, 15:16], ALU.mult, r=[st_], w=[st_], s2=-1.0, op1=ALU.mult)
        act(xs, xs, AF.Identity, r=[t_x[j], st_], w=[t_x[j]], bias=s_[0:n, 14:15], scale=s_[0:n, 15:16])
        tt('dve', xs, xs, gbt[0:n, 0, :], ALU.mult, r=[t_x[j], t_g], w=[t_x[j]])
        tt('pool', xs, xs, gbt[0:n, 1, :], ALU.add, r=[t_x[j], t_g], w=[t_x[j]])

    def load_gb(l, idx):
        dma(gbt[:, 0, :], ln_g[l, idx:idx + 1, :].to_broadcast([128, D]), w=[t_gbt])
        dma(gbt[:, 1, :], ln_b[l, idx:idx + 1, :].to_broadcast([128, D]), w=[t_gbt])

    SL_MQ, SL_MK, SL_IG, SL_FG, SL_SQ, SL_SK, SL_GA, SL_GB = 0, 4, 8, 9, 10, 26, 30, 38

    def load_bcol(l):
        def col(slot, c0, n):
            dma(bcol[0:n, slot:slot + 1], b_in[l, c0:c0 + n].unsqueeze(1), w=[t_bcol])
        for h in range(4):
            col(SL_MQ + h, O_MQ + h * 128, 128)
            col(SL_MK + h, O_MK + h * 128, 128)
        col(SL_IG, O_IG, 4)
        col(SL_FG, O_FG, 4)
        for hh in range(16):
            col(SL_SQ + hh, O_SQ + hh * 64, 64)
        for g in range(4):
            col(SL_SK + g, O_SK + g * 64, 64)
        for m in range(8):
            col(SL_GA + m, O_GA + m * 128, 128)
            col(SL_GB + m, O_GB + m * 128, 128)

    for l in range(depth):
        AR.reset()
        load_bcol(l)
        ts('dve', bcolh[:, :], bcol[:, SL_GA:SL_GA + 16], 0.5, ALU.mult, r=[t_bcol], w=[t_bcolh])
        dma(esink[:], w_sink[l:l + 1, :].to_broadcast([128, 16]), w=[t_esink])
        act(esink[:], esink[:], AF.Exp, r=[t_esink], w=[t_esink])
        gain_r = AR.rot(2, [128, 256], F32)
        xT_c = AR.alloc([128, 8, CH], BF16)
        t_xT = Tok()
        yaT = AR.alloc([128, 8, CH], BF16)
        t_yaT = Tok()
        ybT = AR.alloc([128, 8, CH], BF16)
        t_ybT = Tok()
        merged = AR.alloc([128, 8, CH], BF16)
        t_mg = Tok()
        fgr = AR.alloc([4, CH], F32)
        Brow = AR.alloc([4, CH], F32)
        Urow = AR.alloc([4, CH], F32)
        Arow = AR.alloc([4, CH], F32)
        onec = AR.alloc([4, 2], F32)
        t_rows = Tok()
        memset('dve', onec, 1.0, w=[t_rows])
        ones_row = onec[:, 0:1].to_broadcast([4, CH])
        carry = AR.alloc([4, 2], F32)
        t_carry = Tok()
        cols = AR.alloc([128, TPC, 8], F32)
        t_cols = Tok()
        cols_s = AR.alloc([4, 16, 8], F32)
        m0row = AR.alloc([4, 16], F32)
        m0rep = AR.alloc([128, 64], F32)
        t_m0 = Tok()
        msrow = AR.alloc([4, 16], F32)
        acs_c = AR.alloc([128, 4], F32)
        t_acs = Tok()
        memset('dve', acs_c, 0.0, w=[t_acs])
        ArepU = AR.rot(1, [128, CH], F32)
        qT_r = AR.rot(1, [128, CH], BF16)
        kT_r = AR.rot(1, [128, CH], BF16)
        bt_r = AR.rot(2, [128, 512], F32)
        vaug_r = AR.rot(2, [128, 257], BF16)
        for vb_ in vaug_r.bufs:
            memset('pool', vb_[:, 256:257], 1.0, w=[Tok()])
        ogt_r = AR.rot(2, [128, 256], F32)
        AM_r = AR.rot(2, [128, 128], F32)
        WT_r = AR.rot(2, [128, 128], F32)
        Wi_r = AR.rot(2, [128, 128], F32)
        STw_r = AR.rot(2, [128, 128], BF16)
        qw_r = AR.rot(2, [128, 128], BF16)
        kw_r = AR.rot(2, [128, 128], BF16)
        ya_r = AR.rot(2, [128, 256], F32)
        C_f = AR.alloc([128, 4, 257], F32)
        C_b = AR.alloc([128, 4, 257], BF16)
        t_C = [Tok() for _ in range(4)]
        for h in range(4):
            memset('pool', C_f[:, h, :], 0.0, w=[t_C[h]])
            memset('pool', C_b[:, h, :], 0.0, w=[t_C[h]])
        Cs_f = AR.rot(2, [128, 257], F32)
        Cs_b = AR.rot(2, [128, 257], BF16)
        Cs_o = AR.rot(2, [128, 257], F32)
        qTg = AR.alloc([64, 4, CH], BF16)
        t_qTg = Tok()
        kTg = AR.alloc([64, 4, CH + 128], BF16)
        t_kTg = [Tok() for _ in range(4)]
        vtd = AR.alloc([128, TPC + 1, 512], BF16)
        t_vtd = [Tok() for _ in range(TPC + 1)]
        T512 = AR.rot(4, [128, 512], F32)
        kvo_r = E_r = rd_r = sg_r = T512
        PT_r = AR.rot(3, [128, 512], BF16)
        kbuf_r = AR.rot(2, [128, 64], F32)
        vbuf_r = AR.rot(2, [128, 64], F32)
        kbT_r = AR.rot(2, [64, 1, 128], BF16)
        vbd_r = AR.rot(2, [128, 256], BF16)

        def mlstm_core(n, h, kTv, qTv, ArU, Ucol, emcol, Acs, Ace, vaug, gs, t_in, Cb_ap, Cf_ap, Cout_ap, t_Cst, t_Cout, yaT_dst):
            pS, tS = psum()
            mm(pS[0:n, 0:n], kTv, qTv, True, True, r=t_in, w=[tS])
            AM, tAM = AM_r.next()
            tt('pool', AM[0:n, 0:n], ArU[0:n, :], maskbig[0:n, 0:n], ALU.add, r=t_in + [t_c], w=[tAM])
            WT, tWT = WT_r.next()
            act(WT[0:n, 0:n], AM[0:n, 0:n], AF.Exp, r=[tAM] + t_in, w=[tWT], bias=Ucol, scale=-1.0)
            STw, tST = STw_r.next()
            tt('dve', STw[0:n, 0:n], pS[0:n, 0:n], WT[0:n, 0:n], ALU.mult, r=[tS, tWT], w=[tST])
            Wi, tWi = Wi_r.next()
            act(Wi[:, 0:n], ArU, AF.Exp, r=t_in, w=[tWi], bias=Acs, scale=-1.0)
            qw, tqw = qw_r.next()
            tt('pool', qw[:, 0:n], qTv, Wi[:, 0:n], ALU.mult, r=t_in + [tWi], w=[tqw])
            pN, tN = psum()
            mm(pN[0:n, 0:257], STw[0:n, 0:n], vaug, True, False, r=[tST] + t_in, w=[tN])
            mm(pN[0:n, 0:257], qw[:, 0:n], Cb_ap, False, True, r=[tqw, t_Cst], w=[tN])
            s_, st_ = small.next()
            act(s_[0:n, 0:1], pN[0:n, 256:257], AF.Abs, r=[tN], w=[st_])
            ts('dve', s_[0:n, 1:2], s_[0:n, 0:1], emcol, ALU.max, r=[st_] + t_in, w=[st_])
            P.op('dve', lambda e: e.bn_stats(out=s_[0:n, 2:8], in_=pN[0:n, 0:256]), r=[tN], w=[st_])
            P.op('dve', lambda e: e.bn_aggr(out=s_[0:n, 8:10], in_=s_[0:n, 2:8]), r=[st_], w=[st_])
            ts('dve', s_[0:n, 10:11], s_[0:n, 1:2], s_[0:n, 1:2], ALU.mult, r=[st_], w=[st_], s2=EPS, op1=ALU.mult)
            tt('dve', s_[0:n, 10:11], s_[0:n, 10:11], s_[0:n, 9:10], ALU.add, r=[st_], w=[st_])
            tt('pool', s_[0:n, 11:12], s_[0:n, 10:11], mhalf[0:n, 0:1], ALU.pow, r=[st_, t_c], w=[st_])
            ya, tya = ya_r.next()
            ts('dve', ya[0:n, :], pN[0:n, 0:256], s_[0:n, 8:9], ALU.subtract, r=[tN, st_], w=[tya], s2=s_[0:n, 11:12], op1=ALU.mult)
            tt('pool', ya[0:n, :], ya[0:n, :], gs, ALU.mult, r=[tya] + t_in, w=[tya])
            pY, tY = psum()
            for vc in range(2):
                tr(pY[:, vc * 128:vc * 128 + n], ya[0:n, vc * 128:(vc + 1) * 128], ident_f[0:n, 0:n], r=[tya, t_c], w=[tY])
            cp('act', yaT_dst, pY[:, 0:256].rearrange("p (a b) -> p a b", a=2)[:, :, 0:n], r=[tY], w=[t_yaT])
            s2_, st2 = small.next()
            ts('dve', s2_[0:n, 0:1], Ucol, Ace[0:n, :], ALU.subtract, r=t_in, w=[st2])
            act(s2_[0:n, 0:1], s2_[0:n, 0:1], AF.Exp, r=[st2], w=[st2])
            tt('dve', s2_[:, 1:2], Acs, Ace, ALU.subtract, r=t_in, w=[st2])
            act(s2_[:, 1:2], s2_[:, 1:2], AF.Exp, r=[st2], w=[st2])
            pK, tK = psum()
            mm(pK[0:n, 0:128], kTv, ident_b[:], True, True, r=t_in + [t_c], w=[tK])
            kw, tkw = kw_r.next()
            ts('dve', kw[0:n, :], pK[0:n, 0:128], s2_[0:n, 0:1], ALU.mult, r=[tK, st2], w=[tkw])
            pU, tU = psum()
            mm(pU[:, 0:257], kw[0:n, :], vaug, True, True, r=[tkw] + t_in, w=[tU])
            stt(Cout_ap, Cf_ap, s2_[:, 1:2], pU[:, 0:257], ALU.mult, ALU.add, r=[t_Cst, st2, tU], w=[t_Cout])

        NPC = TP // CH
        chunks = [(c, CH) for c in range(NPC)] + [(NPC, 64)]
        th = []
        for (c_, N_) in chunks:
            s_chunk = (c_ == NPC)
            th.append(lambda l=l: wblock([w_in[l][:, O_IG:O_IG + 8].rearrange("(kc p) n -> p kc n", p=128)]))
            for h_ in range(4):
                th.append(lambda l=l, h_=h_: wmulti([(w_in[l], O_MQ + h_ * 128, 128), (w_in[l], O_MK + h_ * 128, 128)]))
                th.append(lambda l=l, h_=h_: wmulti([(w_in[l], O_MV + h_ * 256, 256), (w_in[l], O_OG + h_ * 256, 256)]))
            if not s_chunk:
                th.append(lambda l=l: wmulti([(w_in[l], O_SK, 256), (w_in[l], O_SV, 256)]))
            for g_ in range(4):
                th.append(lambda l=l, g_=g_: wmulti([(w_in[l], O_SQ + g_ * 256, 256), (w_in[l], O_SK + g_ * 64, 64)]))
                if s_chunk:
                    th.append(lambda l=l: wmulti([(w_in[l], O_SK, 256), (w_in[l], O_SV, 256)]))
            for m_ in range(8):
                th.append(lambda l=l, m_=m_: wmulti([(w_a[l], m_ * 128, 128), (w_b[l], m_ * 128, 128), (w_in[l], O_GA + m_ * 128, 128), (w_in[l], O_GB + m_ * 128, 128)]))
            for half_ in range(2):
                th.append(lambda l=l, half_=half_: wmat(w_out[l], half_ * 512, 512))
        WS = WStream(th)
        for (c, N) in chunks:
            is_s = (c == NPC)
            tiles = [16] if is_s else list(range(TPC * c, TPC * c + TPC))
            transpose_tiles(xT_c, t_xT, tiles, 0)
            wg_v, wg_t = WS.next()
            wg = wg_v[0]
            pI, tI = psum()
            pF, tF = psum()
            for kc in range(8):
                mm(pI[0:4, 0:N], wg[:, kc, 0:4], xT_c[:, kc, 0:N], kc == 0, kc == 7, r=[wg_t, t_xT], w=[tI])
            for kc in range(8):
                mm(pF[0:4, 0:N], wg[:, kc, 4:8], xT_c[:, kc, 0:N], kc == 0, kc == 7, r=[wg_t, t_xT], w=[tF])
            act(Urow[:, 0:N], pI[0:4, 0:N], AF.Identity, r=[tI, t_bcol], w=[t_rows], bias=bcol[0:4, SL_IG:SL_IG + 1])
            act(fgr[:, 0:N], pF[0:4, 0:N], AF.Identity, r=[tF, t_bcol], w=[t_rows], bias=bcol[0:4, SL_FG:SL_FG + 1])
            act(fgr[:, 0:N], fgr[:, 0:N], AF.Exp, r=[t_rows], w=[t_rows], scale=-1.0)
            act(fgr[:, 0:N], fgr[:, 0:N], AF.Ln, r=[t_rows], w=[t_rows], bias=1.0)
            if not is_s:
                if c == 0:
                    P.op('dve', lambda e: e.tensor_tensor_scan(out=Brow[:, :], data0=ones_row[:, :], data1=fgr[:, :], initial=0.0, op0=ALU.mult, op1=ALU.subtract), r=[t_rows], w=[t_rows])
                else:
                    P.op('dve', lambda e: e.tensor_tensor_scan(out=Brow[:, :], data0=ones_row[:, :], data1=fgr[:, :], initial=carry[:, 0:1], op0=ALU.mult, op1=ALU.subtract), r=[t_rows, t_carry], w=[t_rows])
                tt('dve', Urow[:, :], Urow[:, :], Brow[:, :], ALU.subtract, r=[t_rows], w=[t_rows])
                if c == 0:
                    P.op('dve', lambda e: e.tensor_tensor_scan(out=Arow[:, :], data0=Urow[:, :], data1=Urow[:, :], initial=0.0, op0=ALU.max, op1=ALU.max), r=[t_rows], w=[t_rows])
                else:
                    P.op('dve', lambda e: e.tensor_tensor_scan(out=Arow[:, :], data0=Urow[:, :], data1=Urow[:, :], initial=carry[:, 1:2], op0=ALU.max, op1=ALU.max), r=[t_rows, t_carry], w=[t_rows])
                cp('dve', carry[:, 0:1], Brow[:, CH - 1:CH], r=[t_rows], w=[t_carry])
                cp('dve', carry[:, 1:2], Arow[:, CH - 1:CH], r=[t_rows], w=[t_carry])
            else:
                dma(m0row, sm[l].rearrange("(s h) -> h s", h=4), w=[t_m0], slow=True)
                dma(m0rep, sm[l:l + 1, :].to_broadcast([128, 64]), w=[t_m0])
                f3 = fgr[:, 0:64].rearrange("p (s t) -> p s t", t=4)
                B3 = Brow[:, 0:64].rearrange("p (s t) -> p s t", t=4)
                U3 = Urow[:, 0:64].rearrange("p (s t) -> p s t", t=4)
                A3 = Arow[:, 0:64].rearrange("p (s t) -> p s t", t=4)
                ts('dve', B3[:, :, 0], f3[:, :, 0], -1.0, ALU.mult, r=[t_rows], w=[t_rows])
                for t in range(1, 4):
                    tt('dve', B3[:, :, t], B3[:, :, t - 1], f3[:, :, t], ALU.subtract, r=[t_rows], w=[t_rows])
                tt('dve', Urow[:, 0:64], Urow[:, 0:64], Brow[:, 0:64], ALU.subtract, r=[t_rows], w=[t_rows])
                tt('dve', A3[:, :, 0], U3[:, :, 0], m0row[:, :], ALU.max, r=[t_rows, t_m0], w=[t_rows])
                for t in range(1, 4):
                    tt('dve', A3[:, :, t], A3[:, :, t - 1], U3[:, :, t], ALU.max, r=[t_rows], w=[t_rows])
                tt('dve', msrow[:, :], A3[:, :, 3], B3[:, :, 3], ALU.add, r=[t_rows], w=[t_m0])
                dma(o_ms[l].rearrange("(s h) -> h s", h=4), msrow[:, :], r=[t_m0], slow=True)
            stt(fgr[:, 0:N], Arow[:, 0:N], -1.0, Brow[:, 0:N], ALU.mult, ALU.subtract, r=[t_rows], w=[t_rows])
            pC, tC = psum()
            if not is_s:
                for jj in range(TPC):
                    tr(pC[:, jj * 8:jj * 8 + 4], Urow[:, jj * 128:(jj + 1) * 128], ident_f[0:4, 0:4], r=[t_rows, t_c], w=[tC])
                    tr(pC[:, jj * 8 + 4:jj * 8 + 8], fgr[:, jj * 128:(jj + 1) * 128], ident_f[0:4, 0:4], r=[t_rows, t_c], w=[tC])
                cp('dve', cols[:, :, 0:4], pC[:, 0:8 * TPC].rearrange("p (a b) -> p a b", a=TPC)[:, :, 0:4], r=[tC], w=[t_cols])
                act(cols[:, :, 4:8], pC[:, 0:8 * TPC].rearrange("p (a b) -> p a b", a=TPC)[:, :, 4:8], AF.Exp, r=[tC], w=[t_cols])
            else:
                for s in range(16):
                    tr(pC[0:4, s * 8:s * 8 + 4], Urow[:, s * 4:(s + 1) * 4], ident_f[0:4, 0:4], r=[t_rows, t_c], w=[tC])
                    tr(pC[0:4, s * 8 + 4:s * 8 + 8], fgr[:, s * 4:(s + 1) * 4], ident_f[0:4, 0:4], r=[t_rows, t_c], w=[tC])
                cp('dve', cols_s[:, :, 0:4], pC[0:4, 0:128].rearrange("p (a b) -> p a b", a=16)[:, :, 0:4], r=[tC], w=[t_cols])
                act(cols_s[:, :, 4:8], pC[0:4, 0:128].rearrange("p (a b) -> p a b", a=16)[:, :, 4:8], AF.Exp, r=[tC], w=[t_cols])

            for h in range(4):
                ArU, tAr = ArepU.next()
                pA, tA = psum()
                mm(pA[:, 0:N], sel4[:, h, :], Arow[:, 0:N], True, True, r=[t_c, t_rows], w=[tA])
                cp('act', ArU[:, 0:N], pA[:, 0:N], r=[tA], w=[tAr])
                wA, wA_t = WS.next()
                wB, wB_t = WS.next()
                qT, tq = qT_r.next()
                kT, tk = kT_r.next()
                pq, tpq = psum()
                for kc in range(8):
                    mm(pq[:, 0:N], wA[:, kc, 0:128], xT_c[:, kc, 0:N], kc == 0, kc == 7, r=[wA_t, t_xT], w=[tpq])
                ts('dve', qT[:, 0:N], pq[:, 0:N], bcol[:, SL_MQ + h:SL_MQ + h + 1], ALU.add, r=[tpq, t_bcol], w=[tq], s2=float(128 ** -0.5), op1=ALU.mult)
                pk, tpk = psum()
                for kc in range(8):
                    mm(pk[:, 0:N], wA[:, kc, 128:256], xT_c[:, kc, 0:N], kc == 0, kc == 7, r=[wA_t, t_xT], w=[tpk])
                act(kT[:, 0:N], pk[:, 0:N], AF.Identity, r=[tpk, t_bcol], w=[tk], bias=bcol[:, SL_MK + h:SL_MK + h + 1])
                gain, t_gain = gain_r.next()
                dma(gain[:, :], mh_gain[l:l + 1, h * 256:(h + 1) * 256].to_broadcast([128, 256]), w=[t_gain])
                bt_, tbt = bt_r.next()
                dma(bt_[:, 0:256], b_in[l:l + 1, O_MV + h * 256:O_MV + (h + 1) * 256].to_broadcast([128, 256]), w=[tbt])
                dma(bt_[:, 256:512], b_in[l:l + 1, O_OG + h * 256:O_OG + (h + 1) * 256].to_broadcast([128, 256]), w=[tbt])
                units = [(jj, 128, jj * 128) for jj in range(TPC)] if not is_s else [(s, 4, s * 4) for s in range(16)]
                def issue_c0(u_, h=h):
                    cf_, tcf_ = Cs_f.next()
                    cb_, tcb_ = Cs_b.next()
                    dma(cf_[:, 0:256], sC[l, u_, h], w=[tcf_])
                    dma(cf_[:, 256:257], sn[l, u_, h, :].unsqueeze(1), w=[tcf_])
                    cp('pool', cb_[:, :], cf_[:, :], r=[tcf_], w=[tcb_])
                    return cf_, tcf_, cb_, tcb_
                nxt_c0 = issue_c0(0) if is_s else None
                for (u, n, c0) in units:
                    cs = slice(c0, c0 + n)
                    if is_s:
                        cur_c0 = nxt_c0
                        nxt_c0 = issue_c0(u + 1) if u + 1 < 16 else None
                    pv, tpv = psum()
                    for kc in range(8):
                        mm(pv[0:n, :], xT_c[:, kc, cs], wB[:, kc, :], kc == 0, kc == 7, r=[wB_t, t_xT], w=[tpv])
                    va, tva = vaug_r.next()
                    tt('dve', va[0:n, 0:256], pv[0:n, 0:256], bt_[0:n, 0:256], ALU.add, r=[tpv, tbt], w=[tva])
                    og, tog = ogt_r.next()
                    tt('dve', og[0:n, :], pv[0:n, 256:512], bt_[0:n, 256:512], ALU.add, r=[tpv, tbt], w=[tog])
                    act(og[0:n, :], og[0:n, :], AF.Tanh, r=[tog], w=[tog], scale=0.5)
                    act(og[0:n, :], og[0:n, :], AF.Identity, r=[tog], w=[tog], bias=0.5, scale=0.5)
                    tt('pool', og[0:n, :], og[0:n, :], gain[0:n, :], ALU.mult, r=[tog, t_gain], w=[tog])
                    t_in = [tq, tk, tAr, t_cols, tva, tog, t_acs, t_m0]
                    yaT_dst = yaT[:, 2 * h:2 * h + 2, cs]
                    if not is_s:
                        Acs = acs_c[:, h:h + 1] if u == 0 else ArU[:, c0 - 1:c0]
                        Ace = ArU[:, c0 + n - 1:c0 + n]
                        mlstm_core(n, h, kT[:, cs], qT[:, cs], ArU[:, cs], cols[:, u, h:h + 1], cols[:, u, 4 + h:5 + h],
                                   Acs, Ace, va[0:n, :], og[0:n, :], t_in, C_b[:, h, :], C_f[:, h, :], C_f[:, h, :], t_C[h], t_C[h], yaT_dst)
                        cp('act', C_b[:, h, :], C_f[:, h, :], r=[t_C[h]], w=[t_C[h]])
                    else:
                        cf, tcf, cb, tcb = cur_c0
                        co, tco = Cs_o.next()
                        Acs = m0rep[:, u * 4 + h:u * 4 + h + 1]
                        Ace = ArU[:, c0 + n - 1:c0 + n]
                        mlstm_core(n, h, kT[:, cs], qT[:, cs], ArU[:, cs], cols_s[0:4, u, h:h + 1], cols_s[0:4, u, 4 + h:5 + h],
                                   Acs, Ace, va[0:n, :], og[0:n, :], t_in + [tcb], cb[:, :], cf[:, :], co[:, :], tcf, tco, yaT_dst)
                        dma(o_Cs[l, u, h], co[:, 0:256], r=[tco])
                        dma(o_ns[l, u, h, :].unsqueeze(1), co[:, 256:257], r=[tco])
                if not is_s:
                    cp('dve', acs_c[:, h:h + 1], ArU[:, CH - 1:CH], r=[tAr], w=[t_acs])
                    if c == NPC - 1:
                        dma(o_Cp[l, h], C_f[:, h, 0:256], r=[t_C[h]])
                        dma(o_np[l, h, :].unsqueeze(1), C_f[:, h, 256:257], r=[t_C[h]])
            if c == NPC - 1:
                s_, st_ = small.next()
                tt('dve', s_[0:4, 0:1], carry[:, 0:1], carry[:, 1:2], ALU.add, r=[t_carry], w=[st_])
                dma(o_mp[l, :].unsqueeze(1), s_[0:4, 0:1], r=[st_])

            if not is_s:
                wKV, wKV_t = WS.next()
            btk_, tbtk = bt_r.next()
            dma(btk_[:, :], b_in[l:l + 1, O_SK:O_SK + 512].to_broadcast([128, 512]), w=[tbtk])
            if not is_s:
                for jj in range(TPC):
                    j = TPC * c + jj
                    pkv, tpkv = psum()
                    for kc in range(8):
                        mm(pkv[:, :], xT_c[:, kc, jj * 128:(jj + 1) * 128], wKV[:, kc, :], kc == 0, kc == 7, r=[wKV_t, t_xT], w=[tpkv])
                    dst = vtd[:, jj + 1, :].rearrange("p (g u d) -> p g u d", g=4, u=2)
                    src = pkv[:, 256:512].rearrange("p (g d) -> p g d", g=4).unsqueeze(2).to_broadcast([128, 4, 2, 64])
                    bsrc = btk_[:, 256:512].rearrange("p (g d) -> p g d", g=4).unsqueeze(2).to_broadcast([128, 4, 2, 64])
                    tt('dve', dst, src, bsrc, ALU.add, r=[tpkv, tbtk], w=[t_vtd[jj + 1]])
                    if j == 15:
                        kvo, tkvo = kvo_r.next()
                        tt('dve', kvo[:, :], pkv[:, :], btk_[:, :], ALU.add, r=[tpkv, tbtk], w=[tkvo])
                        dma(o_kp[l], kvo[:, 0:256], r=[tkvo])
                        dma(o_vp[l], kvo[:, 256:512], r=[tkvo])
            else:
                dma(o_ks[l, :, 0:124, :], sk[l, :, 4:128, :])
                dma(o_vs[l, :, 0:124, :], sv[l, :, 4:128, :])
            for g in range(4):
                wS, wS_t = WS.next()
                if is_s:
                    wKV, wKV_t = WS.next()
                for i in range(4):
                    pq, tpq = psum()
                    for kc in range(8):
                        mm(pq[0:64, 0:N], wS[:, kc, i * 64:(i + 1) * 64], xT_c[:, kc, 0:N], kc == 0, kc == 7, r=[wS_t, t_xT], w=[tpq])
                    act(qTg[:, i, 0:N], pq[0:64, 0:N], AF.Identity, r=[tpq, t_bcol], w=[t_qTg], bias=bcol[0:64, SL_SQ + 4 * g + i:SL_SQ + 4 * g + i + 1])
                pk, tpk = psum()
                for kc in range(8):
                    mm(pk[0:64, 0:N], wS[:, kc, 256:320], xT_c[:, kc, 0:N], kc == 0, kc == 7, r=[wS_t, t_xT], w=[tpk])
                act(kTg[:, g, 128:128 + N], pk[0:64, 0:N], AF.Identity, r=[tpk, t_bcol], w=[t_kTg[g]], bias=bcol[0:64, SL_SK + g:SL_SK + g + 1])
                if not is_s:
                    for jj in range(TPC):
                        j = TPC * c + jj
                        cs = slice(jj * 128, (jj + 1) * 128)
                        pNm, tNm = psum()
                        pDn, tDn = psum()
                        kts = ([(1, jj)] if j > 0 else []) + [(0, jj + 1)]
                        for ki, (kind, slot) in enumerate(kts):
                            pS, tS = psum()
                            mm(pS[:, :].rearrange("p (a b) -> p a b", a=4), kTg[:, g, slot * 128:(slot + 1) * 128], qTg[:, :, cs], True, True, r=[t_kTg[g], t_qTg], w=[tS])
                            E_, tE = E_r.next()
                            act(E_[:, :], pS[:, :], AF.Exp, r=[tS], w=[tE], scale=0.125)
                            PT, tPT = PT_r.next()
                            tt('dve', PT[:, :].rearrange("p (a b) -> p a b", a=4), E_[:, :].rearrange("p (a b) -> p a b", a=4), EB[:, 4 * g:4 * g + 4, kind, :], ALU.mult, r=[tE, t_c], w=[tPT])
                            first = ki == 0
                            last = ki == len(kts) - 1
                            mm(pNm[:, :], vtd[:, slot, g * 128:(g + 1) * 128], PT[:, :], first, last, r=[t_vtd[slot], tPT], w=[tNm])
                            mm(pDn[:, :], ones_b[:, :], PT[:, :], first, last, r=[tPT, t_c], w=[tDn])
                        rd, trd = rd_r.next()
                        for i in range(4):
                            ts('dve', rd[:, i * 128:(i + 1) * 128], pDn[:, i * 128:(i + 1) * 128], esink[:, 4 * g + i:4 * g + i + 1], ALU.add, r=[tDn, t_esink], w=[trd])
                        P.op('dve', lambda e, rd=rd: e.reciprocal(out=rd[:, :], in_=rd[:, :]), r=[trd], w=[trd])
                        for par in range(2):
                            ps_ = slice(par * 64, (par + 1) * 64)
                            nv = pNm[ps_, :].rearrange("p (a b) -> p a b", a=4)[:, par::2, :]
                            rv = rd[ps_, :].rearrange("p (a b) -> p a b", a=4)[:, par::2, :]
                            tt('dve', ybT[ps_, 2 * g:2 * g + 2, cs], nv, rv, ALU.mult, r=[tNm, trd], w=[t_ybT])
                else:
                    def issue_kv(s_, g=g):
                        kb, tkb = kbuf_r.next()
                        vb, tvb = vbuf_r.next()
                        dma(kb[:, :], sk[l, s_, :, g * 64:(g + 1) * 64], w=[tkb])
                        dma(vb[:, :], sv[l, s_, :, g * 64:(g + 1) * 64], w=[tvb])
                        pT_, tT_ = psum()
                        tr(pT_[0:64, 0:128], kb[:, :], ident_f[:, :], r=[tkb, t_c], w=[tT_])
                        kbT_, tkbT_ = kbT_r.next()
                        cp('act', kbT_[:, 0, :], pT_[0:64, 0:128], r=[tT_], w=[tkbT_])
                        vbd_, tvbd_ = vbd_r.next()
                        cp('pool', vbd_[:, 0:128].rearrange("p (u d) -> p u d", u=2), vb[:, :].unsqueeze(1).to_broadcast([128, 2, 64]), r=[tvb], w=[tvbd_])
                        return kbT_, tkbT_, vbd_, tvbd_
                    nxt_kv = issue_kv(0)
                    for s in range(16):
                        cs = slice(4 * s, 4 * s + 4)
                        kbT, tkbT, vbd, tvbd = nxt_kv
                        nxt_kv = issue_kv(s + 1) if s + 1 < 16 else None
                        pkv, tpkv = psum()
                        for kc in range(8):
                            mm(pkv[0:4, :], xT_c[:, kc, cs], wKV[:, kc, :], kc == 0, kc == 7, r=[wKV_t, t_xT], w=[tpkv])
                        kvo, tkvo = kvo_r.next()
                        tt('dve', kvo[0:4, :], pkv[0:4, :], btk_[0:4, :], ALU.add, r=[tpkv, tbtk], w=[tkvo])
                        if g == 0:
                            dma(o_ks[l, s, 124:128, :], kvo[0:4, 0:256], r=[tkvo])
                            dma(o_vs[l, s, 124:128, :], kvo[0:4, 256:512], r=[tkvo])
                        cp('pool', vbd[0:4, 128:256].rearrange("p (u d) -> p u d", u=2), kvo[0:4, 256 + g * 64:256 + (g + 1) * 64].unsqueeze(1).to_broadcast([4, 2, 64]), r=[tkvo], w=[tvbd])
                        pS1, tS1 = psum()
                        mm(pS1[:, 0:16].rearrange("p (a b) -> p a b", a=4), kbT[:, 0, :], qTg[:, :, cs], True, True, r=[tkbT, t_qTg], w=[tS1])
                        pS2, tS2 = psum()
                        mm(pS2[0:4, 0:16].rearrange("p (a b) -> p a b", a=4), kTg[:, g, 128 + 4 * s:128 + 4 * s + 4], qTg[:, :, cs], True, True, r=[t_kTg[g], t_qTg], w=[tS2])
                        E_, tE = E_r.next()
                        act(E_[:, 0:16], pS1[:, 0:16], AF.Exp, r=[tS1], w=[tE], scale=0.125)
                        act(E_[0:4, 16:32], pS2[0:4, 0:16], AF.Exp, r=[tS2], w=[tE], scale=0.125)
                        PT, tPT = PT_r.next()
                        tt('dve', PT[:, 0:16].rearrange("p (a b) -> p a b", a=4), E_[:, 0:16].rearrange("p (a b) -> p a b", a=4), EB[:, 4 * g:4 * g + 4, 1, 0:4], ALU.mult, r=[tE, t_c], w=[tPT])
                        tt('dve', PT[0:4, 16:32].rearrange("p (a b) -> p a b", a=4), E_[0:4, 16:32].rearrange("p (a b) -> p a b", a=4), EB[0:4, 4 * g:4 * g + 4, 0, 0:4], ALU.mult, r=[tE, t_c], w=[tPT])
                        pNm, tNm = psum()
                        pDn, tDn = psum()
                        mm(pNm[:, 0:16], vbd[:, 0:128], PT[:, 0:16], True, False, r=[tvbd, tPT], w=[tNm])
                        mm(pNm[:, 0:16], vbd[0:4, 128:256], PT[0:4, 16:32], False, True, r=[tvbd, tPT], w=[tNm])
                        mm(pDn[:, 0:16], ones_b[:, :], PT[:, 0:16], True, False, r=[tPT, t_c], w=[tDn])
                        mm(pDn[:, 0:16], ones_b[0:4, :], PT[0:4, 16:32], False, True, r=[tPT, t_c], w=[tDn])
                        rd, trd = rd_r.next()
                        for i in range(4):
                            ts('dve', rd[:, i * 4:(i + 1) * 4], pDn[:, i * 4:(i + 1) * 4], esink[:, 4 * g + i:4 * g + i + 1], ALU.add, r=[tDn, t_esink], w=[trd])
                        P.op('dve', lambda e, rd=rd: e.reciprocal(out=rd[:, 0:16], in_=rd[:, 0:16]), r=[trd], w=[trd])
                        for par in range(2):
                            ps_ = slice(par * 64, (par + 1) * 64)
                            nv = pNm[ps_, 0:16].rearrange("p (a b) -> p a b", a=4)[:, par::2, :]
                            rv = rd[ps_, 0:16].rearrange("p (a b) -> p a b", a=4)[:, par::2, :]
                            tt('dve', ybT[ps_, 2 * g:2 * g + 2, cs], nv, rv, ALU.mult, r=[tNm, trd], w=[t_ybT])
                if not is_s:
                    cp('pool', kTg[:, g, 0:128], kTg[:, g, CH:CH + 128], r=[t_kTg[g]], w=[t_kTg[g]])
            if not is_s:
                cp('pool', vtd[:, 0, :], vtd[:, TPC, :], r=[t_vtd[TPC]], w=[t_vtd[0]])

            for m in range(8):
                wM, wM_t = WS.next()
                res = []
                for (wi, gi, srcT, tsrc, slot) in ((0, 2, yaT, t_yaT, SL_GA + m), (1, 3, ybT, t_ybT, SL_GB + m)):
                    pg, tpg = psum()
                    for kc in range(8):
                        mm(pg[:, 0:N], wM[:, kc, gi * 128:(gi + 1) * 128], xT_c[:, kc, 0:N], kc == 0, kc == 7, r=[wM_t, t_xT], w=[tpg])
                    sg, tsg = sg_r.next()
                    act(sg[:, 0:N], pg[:, 0:N], AF.Tanh, r=[tpg, t_bcolh], w=[tsg], bias=bcolh[:, slot - SL_GA:slot - SL_GA + 1], scale=0.5)
                    act(sg[:, 0:N], sg[:, 0:N], AF.Identity, r=[tsg], w=[tsg], bias=0.5, scale=0.5)
                    pa, tpa = psum()
                    for kc in range(8):
                        mm(pa[:, 0:N], wM[:, kc, wi * 128:(wi + 1) * 128], srcT[:, kc, 0:N], kc == 0, kc == 7, r=[wM_t, tsrc], w=[tpa])
                    tt('dve', sg[:, 0:N], sg[:, 0:N], pa[:, 0:N], ALU.mult, r=[tsg, tpa], w=[tsg])
                    res.append((sg, tsg))
                tt('pool', merged[:, m, 0:N], res[0][0][:, 0:N], res[1][0][:, 0:N], ALU.add, r=[res[0][1], res[1][1]], w=[t_mg])
            if c == 0:
                load_gb(l, 0)
            for half in range(2):
                wO, wO_t = WS.next()
                for u, j in enumerate(tiles):
                    n = tile_rows(j)
                    po, tpo = psum()
                    for kc in range(8):
                        mm(po[0:n, :], merged[:, kc, u * 128:u * 128 + n], wO[:, kc, :], kc == 0, kc == 7, r=[wO_t, t_mg], w=[tpo])
                    xs = x_tok[0:n, j, half * 512:(half + 1) * 512]
                    stt(xs, xs, ALPHA, po[0:n, :], ALU.mult, ALU.add, r=[t_x[j], tpo], w=[t_x[j]])
            for j in tiles:
                layer_norm(j, l, 0, t_gbt)

        P.barrier()
        AR.reset()
        xT_all = AR.alloc([128, 8, TALL], BF16)
        t_xTa = Tok()
        HT = AR.alloc([128, 4, TALL], BF16)
        t_HT = [[Tok() for _ in range(4)] for _ in range(5)]
        sgm_r = AR.rot(2, [128, 512], F32)
        lg_r = AR.rot(2, [128, 80], F32)
        transpose_tiles(xT_all, t_xTa, list(range(NT)), 0)
        load_gb(l, 1)
        dma(brt[:, 0:4], b_rg[l:l + 1, :].to_broadcast([128, 4]), w=[t_brt])
        dma(brt[:, 4:36], b_re[l:l + 1, :].to_broadcast([128, 32]), w=[t_brt])
        th = [lambda l=l: wmulti([(w_rg[l], 0, 4), (w_re[l], 0, 32)])]
        for ex_ in range(32):
            th.append(lambda l=l, ex_=ex_: wmat(w_eg[l, ex_], 0, 512))
            th.append(lambda l=l, ex_=ex_: wmat(w_eu[l, ex_], 0, 512))
            th.append(lambda l=l, ex_=ex_: wmat(w_ed[l, ex_], 0, 1024, kcs=4))
        WS = WStream(th)
        wR, wR_t = WS.next()
        for j in range(NT):
            n = tile_rows(j)
            c0 = j * 128
            pr, tpr = psum()
            for kc in range(8):
                mm(pr[0:n, 0:36], xT_all[:, kc, c0:c0 + n], wR[:, kc, :], kc == 0, kc == 7, r=[wR_t, t_xTa], w=[tpr])
            lg, tlg = lg_r.next()
            L = lg[0:n, :]
            tt('dve', L[:, 0:36], pr[0:n, 0:36], brt[0:n, :], ALU.add, r=[tpr, t_brt], w=[tlg])
            s_, st_ = small.next()
            S = s_[0:n, :]
            P.op('dve', lambda e, S=S, L=L: e.reduce_max(out=S[:, 0:1], in_=L[:, 0:4], axis=mybir.AxisListType.X), r=[tlg], w=[st_])
            ts('dve', S[:, 1:2], S[:, 0:1], -1.0, ALU.mult, r=[st_], w=[st_])
            P.op('act', lambda e, S=S, L=L: e.activation(out=L[:, 36:40], in_=L[:, 0:4], func=AF.Exp, bias=S[:, 1:2], scale=1.0, accum_out=S[:, 2:3]), r=[tlg, st_], w=[tlg, st_])
            ts('dve', L[:, 40:44], L[:, 0:4], S[:, 0:1], ALU.is_equal, r=[tlg, st_], w=[tlg])
            ts('dve', L[:, 40:44], L[:, 40:44], BIG, ALU.mult, r=[tlg], w=[tlg], s2=-BIG, op1=ALU.add)
            tt('dve', L[:, 44:76].rearrange("p (g i) -> p g i", g=4), L[:, 4:36].rearrange("p (g i) -> p g i", g=4), L[:, 40:44].unsqueeze(2).to_broadcast([n, 4, 8]), ALU.add, r=[tlg], w=[tlg])
            P.op('dve', lambda e, S=S, L=L: e.max(out=S[:, 8:16], in_=L[:, 44:76]), r=[tlg], w=[st_])
            ts('dve', S[:, 3:4], S[:, 8:9], -1.0, ALU.mult, r=[st_], w=[st_])
            ts('dve', L[:, 4:36], L[:, 44:76], S[:, 9:10], ALU.is_ge, r=[tlg, st_], w=[tlg])
            act(L[:, 44:76], L[:, 44:76], AF.Exp, r=[tlg, st_], w=[tlg], bias=S[:, 3:4])
            stt(L[:, 44:76], L[:, 44:76], 1.0, L[:, 4:36], ALU.mult, ALU.mult, r=[tlg], w=[tlg, st_], accum=S[:, 4:5])
            tt('dve', S[:, 5:6], S[:, 4:5], S[:, 2:3], ALU.mult, r=[st_], w=[st_])
            P.op('dve', lambda e, S=S: e.reciprocal(out=S[:, 6:7], in_=S[:, 5:6]), r=[st_], w=[st_])
            ts('dve', gate_full[0:n, j, :], L[:, 44:76], S[:, 6:7], ALU.mult, r=[tlg, st_], w=[t_gate[j]])
            P.op('act', lambda e, n=n, j=j: e.mul(out=x_tok[0:n, j, :], in_=x_tok[0:n, j, :], mul=ALPHA), r=[t_x[j], t_xTa], w=[t_x[j]])
        mchunks = [(cc * 512, 512) for cc in range(4)] + [(2048, 64)]
        for ex in range(32):
            wG, wG_t = WS.next()
            wU, wU_t = WS.next()
            for ci, (c0, N) in enumerate(mchunks):
                for fc in range(4):
                    pg, tpg = psum()
                    pu, tpu = psum()
                    for kc in range(8):
                        mm(pg[:, 0:N], wG[:, kc, fc * 128:(fc + 1) * 128], xT_all[:, kc, c0:c0 + N], kc == 0, kc == 7, r=[wG_t, t_xTa], w=[tpg])
                    for kc in range(8):
                        mm(pu[:, 0:N], wU[:, kc, fc * 128:(fc + 1) * 128], xT_all[:, kc, c0:c0 + N], kc == 0, kc == 7, r=[wU_t, t_xTa], w=[tpu])
                    sg, tsg = sgm_r.next()
                    act(sg[:, 0:N], pg[:, 0:N], AF.Silu, r=[tpg], w=[tsg])
                    tt('dve', HT[:, fc, c0:c0 + N], sg[:, 0:N], pu[:, 0:N], ALU.mult, r=[tsg, tpu], w=[t_HT[ci][fc]])
            wD, wD_t = WS.next()
            for j in range(NT):
                n = tile_rows(j)
                ci = min(j // 4, 4)
                for half in range(2):
                    py, tpy = psum()
                    for fc in range(4):
                        mm(py[0:n, :], HT[:, fc, j * 128:j * 128 + n], wD[:, fc, half * 512:(half + 1) * 512], fc == 0, fc == 3, r=[wD_t, t_HT[ci][fc]], w=[tpy])
                    xs = x_tok[0:n, j, half * 512:(half + 1) * 512]
                    stt(xs, py[0:n, :], gate_full[0:n, j, ex:ex + 1], xs, ALU.mult, ALU.add, r=[tpy, t_gate[j], t_x[j]], w=[t_x[j]])
        for j in range(NT):
            layer_norm(j, l, 1, t_gbt)

        P.barrier()
        AR.reset()
        xT_all = AR.alloc([128, 8, TALL], BF16)
        t_xTa = Tok()
        pT_all = AR.alloc([128, 2, TALL], BF16)
        t_pT = Tok()
        ptile_r = AR.rot(2, [128, 256], F32)
        sgp_r = AR.rot(2, [128, 512], F32)
        transpose_tiles(xT_all, t_xTa, list(range(NT)), 0)
        load_gb(l, 2)
        for j in range(NT):
            n = tile_rows(j)
            c0 = j * 128
            pt, tpt = ptile_r.next()
            dma(pt[0:n, :], pin[l, c0:c0 + n, :], w=[tpt])
            pb, ptk = psum()
            for q in range(2):
                tr(pb[:, q * 128:q * 128 + n], pt[0:n, q * 128:(q + 1) * 128], ident_f[0:n, 0:n], r=[tpt, t_c], w=[ptk])
            cp('act', pT_all[:, :, c0:c0 + n], pb[:, 0:256].rearrange("p (a b) -> p a b", a=2)[:, :, 0:n], r=[ptk], w=[t_pT])
            P.op('act', lambda e, n=n, j=j: e.mul(out=x_tok[0:n, j, :], in_=x_tok[0:n, j, :], mul=ALPHA), r=[t_x[j], t_xTa], w=[t_x[j]])
        th = []
        for half_ in range(2):
            th.append(lambda l=l, half_=half_: wmat(w_pg[l], half_ * 512, 512))
            th.append(lambda l=l, half_=half_: wmat(w_pp[l], half_ * 512, 512, kcs=2))
        WS = WStream(th)
        for half in range(2):
            wPG, wPG_t = WS.next()
            wPP, wPP_t = WS.next()
            for j in range(NT):
                n = tile_rows(j)
                c0 = j * 128
                p1, tp1 = psum()
                for kc in range(8):
                    mm(p1[0:n, :], xT_all[:, kc, c0:c0 + n], wPG[:, kc, :], kc == 0, kc == 7, r=[wPG_t, t_xTa], w=[tp1])
                p2, tp2 = psum()
                for kc in range(2):
                    mm(p2[0:n, :], pT_all[:, kc, c0:c0 + n], wPP[:, kc, :], kc == 0, kc == 1, r=[wPP_t, t_pT], w=[tp2])
                sg, tsg = sgp_r.next()
                act(sg[0:n, :], p1[0:n, :], AF.Sigmoid, r=[tp1], w=[tsg])
                tt('dve', sg[0:n, :], sg[0:n, :], p2[0:n, :], ALU.mult, r=[tsg, tp2], w=[tsg])
                xs = x_tok[0:n, j, half * 512:(half + 1) * 512]
                tt('pool', xs, xs, sg[0:n, :], ALU.add, r=[t_x[j], tsg], w=[t_x[j]])
        for j in range(NT):
            layer_norm(j, l, 2, t_gbt)
        P.barrier()

    dma(o_y[0:1024, :].rearrange("(j p) d -> p j d", p=128), x_tok[:, 0:8, :], r=t_x[0:8])
    dma(o_y[1024:2048, :].rearrange("(j p) d -> p j d", p=128), x_tok[:, 8:16, :], r=t_x[8:16])
    dma(o_y[2048:2112, :], x_tok[0:64, 16, :], r=[t_x[16]])
    P.emit()
    return nc, P


_CACHE = {}


def kernel(x_prompt, x_sample, state_mlstm_C, state_mlstm_n, state_mlstm_m, state_swa_k, state_swa_v,
           p_prompt, p_sample, w_in, b_in, mh_gain, w_a, w_b, w_out, rel_table, w_sink, ln_g, ln_b,
           w_rg, b_rg, w_re, b_re, w_eg, w_eu, w_ed, w_pg, w_pp, _depth=4):
    f = lambda a: np.ascontiguousarray(np.asarray(a, dtype=np.float32))
    if _depth not in _CACHE:
        _CACHE[_depth] = build(_depth)
    nc, P = _CACHE[_depth]
    consts = make_consts()
    shared = dict(w_in=f(w_in), b_in=f(b_in), mh_gain=f(mh_gain), w_a=f(w_a), w_b=f(w_b), w_out=f(w_out),
                  rel_table=f(rel_table), w_sink=f(w_sink), ln_g=f(ln_g), ln_b=f(ln_b), w_rg=f(w_rg), b_rg=f(b_rg),
                  w_re=f(w_re), b_re=f(b_re), w_eg=f(w_eg), w_eu=f(w_eu), w_ed=f(w_ed), w_pg=f(w_pg), w_pp=f(w_pp))
    shared.update(consts)
    x_prompt = f(x_prompt); x_sample = f(x_sample); p_prompt = f(p_prompt); p_sample = f(p_sample)
    sCa = f(state_mlstm_C); sna = f(state_mlstm_n); sma = f(state_mlstm_m); ska = f(state_swa_k); sva = f(state_swa_v)
    in_maps = []
    for c in range(8):
        sl = slice(16 * c, 16 * c + 16)
        m = dict(shared)
        m['xin'] = np.ascontiguousarray(np.concatenate([x_prompt[c], x_sample[sl].reshape(64, D)], 0))
        m['pin'] = np.ascontiguousarray(np.concatenate([p_prompt[:, c], p_sample[:, sl].reshape(4, 64, 256)], 1))
        m['sC'] = np.ascontiguousarray(sCa[:, sl])
        m['sn'] = np.ascontiguousarray(sna[:, sl])
        m['sm'] = np.ascontiguousarray(sma[:, sl].reshape(4, 64))
        m['sk'] = np.ascontiguousarray(ska[:, sl].reshape(4, 16, 128, 256))
        m['sv'] = np.ascontiguousarray(sva[:, sl].reshape(4, 16, 128, 256))
        in_maps.append(m)
    res = run_bass_kernel_spmd(nc, in_maps, core_ids=list(range(8)))
    R = res.results
    y_p = np.stack([R[c]['o_y'][:2048] for c in range(8)], 0)
    y_s = np.concatenate([R[c]['o_y'][2048:].reshape(16, 4, D) for c in range(8)], 0)
    C_p = np.stack([R[c]['o_Cp'] for c in range(8)], 1)
    n_p = np.stack([R[c]['o_np'] for c in range(8)], 1)
    m_p = np.stack([R[c]['o_mp'] for c in range(8)], 1)
    k_p = np.stack([R[c]['o_kp'].reshape(4, 128, 4, 64) for c in range(8)], 1)
    v_p = np.stack([R[c]['o_vp'].reshape(4, 128, 4, 64) for c in range(8)], 1)
    C_s = np.concatenate([R[c]['o_Cs'] for c in range(8)], 1)
    n_s = np.concatenate([R[c]['o_ns'] for c in range(8)], 1)
    m_s = np.concatenate([R[c]['o_ms'].reshape(4, 16, 4) for c in range(8)], 1)
    k_s = np.concatenate([R[c]['o_ks'].reshape(4, 16, 128, 4, 64) for c in range(8)], 1)
    v_s = np.concatenate([R[c]['o_vs'].reshape(4, 16, 128, 4, 64) for c in range(8)], 1)
    outs = (y_p, y_s, C_p, n_p, m_p, k_p, v_p, C_s, n_s, m_s, k_s, v_s)
    return tuple(np.ascontiguousarray(o, dtype=np.float32) for o in outs)
```

```python
import contextlib
import numpy as np
import concourse.bass as bass
import concourse.mybir as mybir
from concourse.bass_utils import run_bass_kernel_spmd

F32 = mybir.dt.float32
BF16 = mybir.dt.bfloat16
AF = mybir.ActivationFunctionType
ALU = mybir.AluOpType

ENGS = ['pe', 'act', 'dve', 'pool', 'sp']
N_DMA_SEM = 8
SAME_ENGINE_SYNC = True

D = 1024
NT = 17
TP = 2048
TS = 64
TALL = TP + TS
DIN = 6664
O_MQ, O_MK, O_MV, O_OG, O_IG, O_FG, O_SQ, O_SK, O_SV, O_GA, O_GB = 0, 512, 1024, 2048, 3072, 3076, 3080, 4104, 4360, 4616, 5640
ALPHA = float((2 * 4) ** 0.25)
EPS = 1e-5
BIG = 30000.0
FW = 400
CH = 256
TPC = CH // 128


class Tok:
    __slots__ = ('w', 'rs', 'const')

    def __init__(self, const=False):
        self.w = None
        self.rs = []
        self.const = const


class Op:
    __slots__ = ('eng', 'fn', 'deps', 'sig', 'count', 'is_dma', 'dma_id', 'waits')


class Prog:
    def __init__(self, nc):
        self.nc = nc
        self.ops = {e: [] for e in ENGS}
        self.n_dma = 0
        self.dmas = []
        self.stack = contextlib.ExitStack()
        self._n = 0

    def sb(self, shape, dtype):
        self._n += 1
        return self.stack.enter_context(self.nc.sbuf_tensor(f"sb{self._n}", list(shape), dtype))

    def ps(self, shape, dtype=F32):
        self._n += 1
        return self.stack.enter_context(self.nc.psum_tensor(f"ps{self._n}", list(shape), dtype))

    def op(self, eng, fn, r=(), w=()):
        o = Op()
        o.eng = eng
        o.fn = fn
        o.deps = set()
        o.sig = False
        o.is_dma = False
        o.count = 0
        for t in r:
            if t.w is not None:
                o.deps.add(t.w)
        for t in w:
            if t.w is not None:
                o.deps.add(t.w)
            for x in t.rs:
                o.deps.add(x)
        for t in r:
            if not t.const:
                t.rs.append(o)
        for t in w:
            t.w = o
            t.rs = []
        o.deps.discard(o)
        self.ops[eng].append(o)
        return o

    def dma(self, fn, r=(), w=()):
        o = self.op('sp', fn, r, w)
        o.is_dma = True
        o.dma_id = self.n_dma
        if self.n_dma >= N_DMA_SEM:
            o.deps.add(self.dmas[self.n_dma - N_DMA_SEM])
        self.n_dma += 1
        self.dmas.append(o)
        return o

    def barrier(self):
        last = []
        for e in ENGS:
            if e == 'sp':
                continue
            if self.ops[e]:
                last.append(self.ops[e][-1])
        last += self.dmas[-N_DMA_SEM:]
        for e in ENGS:
            o = self.op(e, None)
            for d in last:
                if d is not o:
                    o.deps.add(d)

    def emit(self):
        nc = self.nc

        def skip(d, o):
            return d.eng == o.eng and (d.eng in ('pe', 'sp') or not SAME_ENGINE_SYNC)

        for e in ENGS:
            for o in self.ops[e]:
                for d in o.deps:
                    if d.is_dma or skip(d, o):
                        continue
                    d.sig = True
        for e in ENGS:
            c = 0
            for o in self.ops[e]:
                if o.sig and not o.is_dma and o.fn is not None:
                    c += 1
                o.count = c
        sems = {e: self.stack.enter_context(nc.semaphore(f"s_{e}")) for e in ENGS}
        dsems = [self.stack.enter_context(nc.semaphore(f"s_dma{i}")) for i in range(N_DMA_SEM)]

        def dma_target(d):
            return dsems[d.dma_id % N_DMA_SEM], 16 * (d.dma_id // N_DMA_SEM + 1)

        nwaits = 0
        for e in ENGS:
            waited = {}
            for o in self.ops[e]:
                need = {}
                for d in o.deps:
                    if d.is_dma:
                        s, v = dma_target(d)
                    else:
                        if skip(d, o):
                            continue
                        s, v = sems[d.eng], d.count
                    if need.get(s, 0) < v:
                        need[s] = v
                o.waits = []
                for s, v in need.items():
                    if waited.get(s, 0) < v:
                        waited[s] = v
                        o.waits.append((s, v))
                        nwaits += 1
        self.stats = {e: len(self.ops[e]) for e in ENGS}
        self.stats['waits'] = nwaits
        final_dma = {}
        for d in self.dmas:
            s, v = dma_target(d)
            final_dma[s] = max(final_dma.get(s, 0), v)

        def replay(ename, eng):
            for o in self.ops[ename]:
                for s, v in o.waits:
                    eng.wait_ge(s, v)
                if o.fn is None:
                    continue
                ins = o.fn(eng)
                if o.is_dma:
                    s, v = dma_target(o)
                    ins.then_inc(s, 16)
                elif o.sig:
                    ins.then_inc(sems[ename], 1)
            if ename == 'sp':
                for s, v in final_dma.items():
                    eng.wait_ge(s, v)

        with nc.Block() as block:
            @block.tensor
            def _(eng):
                replay('pe', eng)

            @block.scalar
            def _(eng):
                replay('act', eng)

            @block.vector
            def _(eng):
                replay('dve', eng)

            @block.gpsimd
            def _(eng):
                replay('pool', eng)

            @block.sync
            def _(eng):
                replay('sp', eng)
        self.stack.close()


def rel_bucket_np(dist):
    n = np.maximum(dist, 0)
    max_exact = 16
    large = max_exact + (np.log(np.maximum(n, 1) / max_exact) / np.log(128 / max_exact) * (32 - max_exact)).astype(np.int32)
    large = np.minimum(large, 31)
    return np.where(n < max_exact, n, large).astype(np.int32)


def make_consts():
    c = {}
    c['c_ident'] = np.eye(128, dtype=np.float32)
    s = np.arange(128)[:, None]
    l = np.arange(128)[None, :]
    c['c_maskbig'] = np.where(s > l, BIG, 0.0).astype(np.float32)
    i = np.arange(FW)
    dist = i - 144
    valid = (dist >= 0) & (dist <= 128)
    oh = np.zeros((32, FW), np.float32)
    b = rel_bucket_np(dist)
    oh[b[valid], i[valid]] = 1.0
    c['c_ohd'] = oh
    c['c_maskvec'] = np.where(valid, 0.0, -BIG).astype(np.float32)[None, :]
    sel = np.zeros((4, 4, 128), np.float32)
    for h in range(4):
        sel[h, h, :] = 1.0
    c['c_sel4'] = sel.reshape(4, 512)
    return c


def build(depth=4):
    nc = bass.Bass("TRN2", target_bir_lowering=False)
    P = Prog(nc)

    def din(name, shape):
        return nc.dram_tensor(name, list(shape), F32, kind="ExternalInput").ap()

    def dout(name, shape):
        return nc.dram_tensor(name, list(shape), F32, kind="ExternalOutput").ap()

    xin = din("xin", [TALL, D])
    pin = din("pin", [4, TALL, 256])
    sC = din("sC", [4, 16, 4, 128, 256])
    sn = din("sn", [4, 16, 4, 128])
    sm = din("sm", [4, 64])
    sk = din("sk", [4, 16, 128, 256])
    sv = din("sv", [4, 16, 128, 256])
    w_in = din("w_in", [4, D, DIN])
    b_in = din("b_in", [4, DIN])
    mh_gain = din("mh_gain", [4, D])
    w_a = din("w_a", [4, D, D])
    w_b = din("w_b", [4, D, D])
    w_out = din("w_out", [4, D, D])
    rel_table = din("rel_table", [32, 16])
    w_sink = din("w_sink", [4, 16])
    ln_g = din("ln_g", [4, 3, D])
    ln_b = din("ln_b", [4, 3, D])
    w_rg = din("w_rg", [4, D, 4])
    b_rg = din("b_rg", [4, 4])
    w_re = din("w_re", [4, D, 32])
    b_re = din("b_re", [4, 32])
    w_eg = din("w_eg", [4, 32, D, 512])
    w_eu = din("w_eu", [4, 32, D, 512])
    w_ed = din("w_ed", [4, 32, 512, D])
    w_pg = din("w_pg", [4, D, D])
    w_pp = din("w_pp", [4, 256, D])
    c_ident = din("c_ident", [128, 128])
    c_maskbig = din("c_maskbig", [128, 128])
    c_ohd = din("c_ohd", [32, FW])
    c_maskvec = din("c_maskvec", [1, FW])
    c_sel4 = din("c_sel4", [4, 512])

    o_y = dout("o_y", [TALL, D])
    o_Cp = dout("o_Cp", [4, 4, 128, 256])
    o_np = dout("o_np", [4, 4, 128])
    o_mp = dout("o_mp", [4, 4])
    o_kp = dout("o_kp", [4, 128, 256])
    o_vp = dout("o_vp", [4, 128, 256])
    o_Cs = dout("o_Cs", [4, 16, 4, 128, 256])
    o_ns = dout("o_ns", [4, 16, 4, 128])
    o_ms = dout("o_ms", [4, 64])
    o_ks = dout("o_ks", [4, 16, 128, 256])
    o_vs = dout("o_vs", [4, 16, 128, 256])
    scratch = nc.dram_tensor("scratch", [16, 128, FW], F32, kind="Internal").ap()

    def mm(out, lhsT, rhs, start, stop, r, w):
        P.op('pe', lambda e: e.matmul(out, lhsT=lhsT, rhs=rhs, start=start, stop=stop), r=r, w=w)

    def tr(out, in_, ident, r, w):
        P.op('pe', lambda e: e.transpose(out=out, in_=in_, identity=ident), r=r, w=w)

    def act(out, in_, func, r, w, bias=None, scale=1.0):
        if bias is None:
            P.op('act', lambda e: e.activation(out=out, in_=in_, func=func, scale=scale), r=r, w=w)
        else:
            P.op('act', lambda e: e.activation(out=out, in_=in_, func=func, bias=bias, scale=scale), r=r, w=w)

    def tt(eng, out, in0, in1, op, r, w):
        P.op(eng, lambda e: e.tensor_tensor(out=out, in0=in0, in1=in1, op=op), r=r, w=w)

    def ts(eng, out, in0, s1, op0, r, w, s2=None, op1=None):
        if op1 is None:
            P.op(eng, lambda e: e.tensor_scalar(out=out, in0=in0, scalar1=s1, scalar2=None, op0=op0), r=r, w=w)
        else:
            P.op(eng, lambda e: e.tensor_scalar(out=out, in0=in0, scalar1=s1, scalar2=s2, op0=op0, op1=op1), r=r, w=w)

    def stt(out, in0, scalar, in1, op0, op1, r, w, accum=None):
        if accum is None:
            P.op('dve', lambda e: e.scalar_tensor_tensor(out=out, in0=in0, scalar=scalar, in1=in1, op0=op0, op1=op1), r=r, w=w)
        else:
            P.op('dve', lambda e: e.scalar_tensor_tensor(out=out, in0=in0, scalar=scalar, in1=in1, op0=op0, op1=op1, accum_out=accum), r=r, w=w)

    def cp(eng, out, in_, r, w):
        if eng == 'act':
            P.op('act', lambda e: e.copy(out=out, in_=in_), r=r, w=w)
        else:
            P.op(eng, lambda e: e.tensor_copy(out=out, in_=in_), r=r, w=w)

    def dma(out, in_, r=(), w=(), slow=False):
        if slow:
            P.dma(lambda e: e.dma_start(out=out, in_=in_, allow_slow_non_contiguous=True), r=r, w=w)
        else:
            P.dma(lambda e: e.dma_start(out=out, in_=in_), r=r, w=w)

    def memset(eng, ap, val, w):
        P.op(eng, lambda e: e.memset(ap, val), w=w)

    class Rot:
        def __init__(self, bufs):
            self.bufs = bufs
            self.toks = [Tok() for _ in bufs]
            self.i = 0

        def next(self):
            k = self.i % len(self.bufs)
            self.i += 1
            return self.bufs[k], self.toks[k]

    banks = Rot([P.ps([128, 512], F32) for _ in range(8)])

    def psum():
        b, t = banks.next()
        return b, t

    x_tok = P.sb([128, NT, D], F32)
    t_x = [Tok() for _ in range(NT)]
    ident_f = P.sb([128, 128], F32)
    ident_b = P.sb([128, 128], BF16)
    ones_b = P.sb([128, 128], BF16)
    maskbig = P.sb([128, 128], F32)
    sel4 = P.sb([4, 4, 128], F32)
    EB = P.sb([128, 16, 2, 128], BF16)
    mhalf = P.sb([128, 1], F32)
    bcolh = P.sb([128, 16], F32)
    t_bcolh = Tok()
    t_c = Tok()
    NSTG = 3
    stg = Rot([P.sb([128, 1024], F32) for _ in range(NSTG)])
    NRING = 4
    ring = Rot([P.sb([128, 4096], BF16) for _ in range(NRING)])
    gate_full = P.sb([128, NT, 32], F32)
    t_gate = [Tok() for _ in range(NT)]
    gbt = P.sb([128, 2, D], F32)
    t_gbt = Tok()
    bcol = P.sb([128, 48], F32)
    t_bcol = Tok()
    esink = P.sb([128, 16], F32)
    t_esink = Tok()
    brt = P.sb([128, 36], F32)
    t_brt = Tok()
    small = Rot([P.sb([128, 16], F32) for _ in range(6)])
    ARENA_W = 18700
    arena = P.sb([128, ARENA_W], F32)

    class Arena:
        def __init__(self):
            self.off = 0

        def reset(self):
            self.off = 0

        def alloc(self, shape, dtype):
            n = int(np.prod(shape[1:]))
            nw = n if dtype == F32 else (n + 1) // 2
            assert self.off + nw <= ARENA_W, (self.off, nw)
            v = arena[0:shape[0], self.off:self.off + nw]
            self.off += nw
            if dtype != F32:
                v = v.bitcast(dtype)
                if n % 2:
                    v = v[:, 0:n]
            if len(shape) > 2:
                names = " ".join(f"d{i}" for i in range(1, len(shape)))
                v = v.rearrange(f"p ({names}) -> p {names}", **{f"d{i}": shape[i] for i in range(1, len(shape) - 1)})
            return v

        def rot(self, n, shape, dtype):
            return Rot([self.alloc(shape, dtype) for _ in range(n)])

    AR = Arena()

    def wblock(srcs):
        slot, tok = ring.next()
        views = []
        off = 0
        for s in srcs:
            shp = list(s.shape)
            n = int(np.prod(shp[1:]))
            assert n <= 1024
            st, stok = stg.next()
            sv_ = st[0:shp[0], 0:n]
            dv = slot[0:shp[0], off:off + n]
            if len(shp) == 3:
                sv_ = sv_.rearrange("p (a b) -> p a b", a=shp[1])
                dv = dv.rearrange("p (a b) -> p a b", a=shp[1])
            dma(sv_, s, w=[stok])
            cast(dv, sv_, r=[stok], w=[tok])
            views.append(dv)
            off += n
        return views, tok

    cast_state = {'i': 0, 'pat': ['act', 'dve', 'act', 'dve', 'pool']}

    def cast(dst, src, r, w):
        pat = cast_state['pat']
        eng = pat[cast_state['i'] % len(pat)]
        cast_state['i'] += 1
        cp(eng, dst, src, r=r, w=w)

    def wcols(w2d, c0, ncols, kcs=8):
        v = w2d[:, c0:c0 + ncols].rearrange("(kc p) n -> p kc n", p=128)
        per = max(1, 1024 // ncols)
        return [v[:, k0:min(k0 + per, kcs), :] for k0 in range(0, kcs, per)]

    def join_views(views):
        return views

    def wmat(w2d, c0, ncols, kcs=8):
        slot, tok = ring.next()
        v = w2d[:, c0:c0 + ncols].rearrange("(kc p) n -> p kc n", p=128)
        per = max(1, 1024 // ncols)
        full = slot[:, 0:kcs * ncols].rearrange("p (a b) -> p a b", a=kcs)
        for k0 in range(0, kcs, per):
            k1 = min(k0 + per, kcs)
            st, stok = stg.next()
            sv_ = st[:, 0:(k1 - k0) * ncols].rearrange("p (a b) -> p a b", a=k1 - k0)
            dma(sv_, v[:, k0:k1, :], w=[stok])
            cast(full[:, k0:k1, :], sv_, r=[stok], w=[tok])
        return full, tok

    def wmulti(parts):
        slot, tok = ring.next()
        tot = sum(p[2] for p in parts)
        assert 8 * tot <= 4096
        full = slot[:, 0:8 * tot].rearrange("p (a b) -> p a b", a=8)
        o = 0
        for (w2d, c0, ncols) in parts:
            v = w2d[:, c0:c0 + ncols].rearrange("(kc p) n -> p kc n", p=128)
            per = max(1, 1024 // ncols)
            for k0 in range(0, 8, per):
                k1 = min(k0 + per, 8)
                st, stok = stg.next()
                sv_ = st[:, 0:(k1 - k0) * ncols].rearrange("p (a b) -> p a b", a=k1 - k0)
                dma(sv_, v[:, k0:k1, :], w=[stok])
                cast(full[:, k0:k1, o:o + ncols], sv_, r=[stok], w=[tok])
            o += ncols
        return full, tok

    PD = 2

    class WStream:
        def __init__(self, thunks):
            self.thunks = thunks
            self.res = []
            self.consumed = 0

        def next(self):
            i = self.consumed
            self.consumed += 1
            lim = min(len(self.thunks), i + 1 + PD)
            while len(self.res) < lim:
                self.res.append(self.thunks[len(self.res)]())
            return self.res[i]

    dma(ident_f[:], c_ident, w=[t_c])
    dma(maskbig[:], c_maskbig, w=[t_c])
    dma(sel4[:].rearrange("p a b -> p (a b)"), c_sel4, w=[t_c])
    cp('dve', ident_b[:], ident_f[:], r=[t_c], w=[t_c])
    memset('dve', ones_b[:], 1.0, w=[t_c])
    memset('dve', mhalf[:], -0.5, w=[t_c])
    AR.reset()
    rt = AR.alloc([32, 16], F32)
    ohd = AR.alloc([32, FW], F32)
    mvec = AR.alloc([1, FW], F32)
    one1 = AR.alloc([1, 16], F32)
    fsb = AR.alloc([16, FW], F32)
    t_tmp = Tok()
    dma(rt, rel_table, w=[t_tmp])
    dma(ohd, c_ohd, w=[t_tmp])
    dma(mvec, c_maskvec, w=[t_tmp])
    memset('dve', one1, 1.0, w=[t_tmp])
    pb, pt_ = psum()
    mm(pb[0:16, 0:FW], rt, ohd, True, False, r=[t_tmp], w=[pt_])
    mm(pb[0:16, 0:FW], one1, mvec, False, True, r=[t_tmp], w=[pt_])
    t_f = Tok()
    cp('dve', fsb, pb[0:16, 0:FW], r=[pt_], w=[t_f])
    t_scr = Tok()
    dma(scratch, fsb.unsqueeze(1).to_broadcast([16, 128, FW]), r=[t_f], w=[t_scr])
    btmp = AR.rot(3, [128, 128], F32)
    for h in range(16):
        for kind, cc in ((0, 144), (1, 272)):
            base = scratch[h, 0, cc:cc + 128]
            src = bass.AP(tensor=base.tensor, offset=base.offset, ap=[[FW - 1, 128], [1, 128]])
            bt_, btk = btmp.next()
            dma(bt_, src, r=[t_scr], w=[btk])
            act(EB[:, h, kind, :], bt_, AF.Exp, r=[btk], w=[t_c])
    t_c.const = True
    P.barrier()

    dma(x_tok[:, 0:8, :], xin[0:1024, :].rearrange("(j p) d -> p j d", p=128), w=t_x[0:8])
    dma(x_tok[:, 8:16, :], xin[1024:2048, :].rearrange("(j p) d -> p j d", p=128), w=t_x[8:16])
    dma(x_tok[0:64, 16, :], xin[2048:2112, :], w=[t_x[16]])

    def tile_rows(j):
        return 64 if j == 16 else 128

    def transpose_tiles(dst, t_dst, tiles, col0):
        k = 0
        for j in tiles:
            n = tile_rows(j)
            c = col0 + (j - tiles[0]) * 128
            for half in range(2):
                pb, ptk = psum()
                for q in range(4):
                    kc = half * 4 + q
                    tr(pb[:, q * 128:q * 128 + n], x_tok[0:n, j, kc * 128:(kc + 1) * 128], ident_f[0:n, 0:n], r=[t_x[j], t_c], w=[ptk])
                src = pb[:, :].rearrange("p (a b) -> p a b", a=4)[:, :, 0:n]
                cp('act' if k % 2 == 0 else 'dve', dst[:, half * 4:half * 4 + 4, c:c + n], src, r=[ptk], w=[t_dst])
                k += 1

    def layer_norm(j, l, idx, t_g):
        n = tile_rows(j)
        xs = x_tok[0:n, j, :]
        s_, st_ = small.next()
        P.op('dve', lambda e: e.bn_stats(out=s_[0:n, 0:6], in_=x_tok[0:n, j, 0:512]), r=[t_x[j]], w=[st_])
        P.op('dve', lambda e: e.bn_stats(out=s_[0:n, 6:12], in_=x_tok[0:n, j, 512:1024]), r=[t_x[j]], w=[st_])
        P.op('dve', lambda e: e.bn_aggr(out=s_[0:n, 12:14], in_=s_[0:n, 0:12]), r=[st_], w=[st_])
        ts('dve', s_[0:n, 14:15], s_[0:n, 13:14], EPS, ALU.add, r=[st_], w=[st_])
        tt('pool', s_[0:n, 15:16], s_[0:n, 14:15], mhalf[0:n, 0:1], ALU.pow, r=[st_, t_c], w=[st_])
        ts('dve', s_[0:n, 14:15], s_[0:n, 12:13], s_[0:n, 15:16], ALU.mult, r=[st_], w=[st_], s2=-1.0, op1=ALU.mult)
        act(xs, xs, AF.Identity, r=[t_x[j], st_], w=[t_x[j]], bias=s_[0:n, 14:15], scale=s_[0:n, 15:16])
        tt('dve', xs, xs, gbt[0:n, 0, :], ALU.mult, r=[t_x[j], t_g], w=[t_x[j]])
        tt('dve', xs, xs, gbt[0:n, 1, :], ALU.add, r=[t_x[j], t_g], w=[t_x[j]])

    def load_gb(l, idx):
        dma(gbt[:, 0, :], ln_g[l, idx:idx + 1, :].to_broadcast([128, D]), w=[t_gbt])
        dma(gbt[:, 1, :], ln_b[l, idx:idx + 1, :].to_broadcast([128, D]), w=[t_gbt])

    SL_MQ, SL_MK, SL_IG, SL_FG, SL_SQ, SL_SK, SL_GA, SL_GB = 0, 4, 8, 9, 10, 26, 30, 38

    def load_bcol(l):
        def col(slot, c0, n):
            dma(bcol[0:n, slot:slot + 1], b_in[l, c0:c0 + n].unsqueeze(1), w=[t_bcol])
        for h in range(4):
            col(SL_MQ + h, O_MQ + h * 128, 128)
            col(SL_MK + h, O_MK + h * 128, 128)
        col(SL_IG, O_IG, 4)
        col(SL_FG, O_FG, 4)
        for hh in range(16):
            col(SL_SQ + hh, O_SQ + hh * 64, 64)
        for g in range(4):
            col(SL_SK + g, O_SK + g * 64, 64)
        for m in range(8):
            col(SL_GA + m, O_GA + m * 128, 128)
            col(SL_GB + m, O_GB + m * 128, 128)

    for l in range(depth):
        AR.reset()
        load_bcol(l)
        ts('dve', bcolh[:, :], bcol[:, SL_GA:SL_GA + 16], 0.5, ALU.mult, r=[t_bcol], w=[t_bcolh])
        dma(esink[:], w_sink[l:l + 1, :].to_broadcast([128, 16]), w=[t_esink])
        act(esink[:], esink[:], AF.Exp, r=[t_esink], w=[t_esink])
        gain_r = AR.rot(2, [128, 256], F32)
        xT_c = AR.alloc([128, 8, CH], BF16)
        t_xT = Tok()
        yaT = AR.alloc([128, 8, CH], BF16)
        t_yaT = Tok()
        ybT = AR.alloc([128, 8, CH], BF16)
        t_ybT = Tok()
        merged = AR.alloc([128, 8, CH], BF16)
        t_mg = Tok()
        fgr = AR.alloc([4, CH], F32)
        Brow = AR.alloc([4, CH], F32)
        Urow = AR.alloc([4, CH], F32)
        Arow = AR.alloc([4, CH], F32)
        onec = AR.alloc([4, 2], F32)
        t_rows = Tok()
        memset('dve', onec, 1.0, w=[t_rows])
        ones_row = onec[:, 0:1].to_broadcast([4, CH])
        carry = AR.alloc([4, 2], F32)
        t_carry = Tok()
        cols = AR.alloc([128, TPC, 8], F32)
        t_cols = Tok()
        cols_s = AR.alloc([4, 16, 8], F32)
        m0row = AR.alloc([4, 16], F32)
        m0rep = AR.alloc([128, 64], F32)
        t_m0 = Tok()
        msrow = AR.alloc([4, 16], F32)
        acs_c = AR.alloc([128, 4], F32)
        t_acs = Tok()
        memset('dve', acs_c, 0.0, w=[t_acs])
        ArepU = AR.rot(1, [128, CH], F32)
        qT_r = AR.rot(1, [128, CH], BF16)
        kT_r = AR.rot(1, [128, CH], BF16)
        bt_r = AR.rot(1, [128, 512], F32)
        hb_r = AR.rot(2, [128, 512], F32)
        vaug_r = AR.rot(2, [128, 257], BF16)
        for vb_ in vaug_r.bufs:
            memset('pool', vb_[:, 256:257], 1.0, w=[Tok()])
        ogt_r = AR.rot(2, [128, 256], F32)
        AM_r = AR.rot(2, [128, 128], F32)
        WT_r = AR.rot(2, [128, 128], F32)
        Wi_r = AR.rot(2, [128, 128], F32)
        STw_r = AR.rot(2, [128, 128], BF16)
        qw_r = AR.rot(2, [128, 128], BF16)
        kw_r = AR.rot(2, [128, 128], BF16)
        ya_r = AR.rot(2, [128, 256], F32)
        C_f = AR.alloc([128, 4, 257], F32)
        C_b = AR.alloc([128, 4, 257], BF16)
        t_C = [Tok() for _ in range(4)]
        for h in range(4):
            memset('pool', C_f[:, h, :], 0.0, w=[t_C[h]])
            memset('pool', C_b[:, h, :], 0.0, w=[t_C[h]])
        Cs_f = AR.rot(2, [128, 257], F32)
        Cs_b = AR.rot(2, [128, 257], BF16)
        Cs_o = AR.rot(2, [128, 257], F32)
        qTg = AR.alloc([64, 4, CH], BF16)
        t_qTg = Tok()
        kTg = AR.alloc([64, 4, CH + 128], BF16)
        t_kTg = [Tok() for _ in range(4)]
        vtd = AR.alloc([128, TPC + 1, 512], BF16)
        t_vtd = [Tok() for _ in range(TPC + 1)]
        T512 = AR.rot(4, [128, 512], F32)
        kvo_r = E_r = rd_r = sg_r = T512
        PT_r = AR.rot(3, [128, 512], BF16)
        kbuf_r = AR.rot(2, [128, 64], F32)
        vbuf_r = AR.rot(2, [128, 64], F32)
        kbT_r = AR.rot(2, [64, 1, 128], BF16)
        vbd_r = AR.rot(2, [128, 256], BF16)

        def mlstm_core(n, h, kTv, qTv, ArU, Ucol, emcol, Acs, Ace, vaug, gs, t_in, Cb_ap, Cf_ap, Cout_ap, t_Cst, t_Cout, yaT_dst):
            pS, tS = psum()
            mm(pS[0:n, 0:n], kTv, qTv, True, True, r=t_in, w=[tS])
            AM, tAM = AM_r.next()
            tt('dve', AM[0:n, 0:n], ArU[0:n, :], maskbig[0:n, 0:n], ALU.add, r=t_in + [t_c], w=[tAM])
            WT, tWT = WT_r.next()
            act(WT[0:n, 0:n], AM[0:n, 0:n], AF.Exp, r=[tAM] + t_in, w=[tWT], bias=Ucol, scale=-1.0)
            STw, tST = STw_r.next()
            tt('dve', STw[0:n, 0:n], pS[0:n, 0:n], WT[0:n, 0:n], ALU.mult, r=[tS, tWT], w=[tST])
            Wi, tWi = Wi_r.next()
            act(Wi[:, 0:n], ArU, AF.Exp, r=t_in, w=[tWi], bias=Acs, scale=-1.0)
            qw, tqw = qw_r.next()
            tt('dve', qw[:, 0:n], qTv, Wi[:, 0:n], ALU.mult, r=t_in + [tWi], w=[tqw])
            pN, tN = psum()
            mm(pN[0:n, 0:257], STw[0:n, 0:n], vaug, True, False, r=[tST] + t_in, w=[tN])
            mm(pN[0:n, 0:257], qw[:, 0:n], Cb_ap, False, True, r=[tqw, t_Cst], w=[tN])
            s_, st_ = small.next()
            act(s_[0:n, 0:1], pN[0:n, 256:257], AF.Abs, r=[tN], w=[st_])
            ts('dve', s_[0:n, 1:2], s_[0:n, 0:1], emcol, ALU.max, r=[st_] + t_in, w=[st_])
            P.op('dve', lambda e: e.bn_stats(out=s_[0:n, 2:8], in_=pN[0:n, 0:256]), r=[tN], w=[st_])
            P.op('dve', lambda e: e.bn_aggr(out=s_[0:n, 8:10], in_=s_[0:n, 2:8]), r=[st_], w=[st_])
            ts('dve', s_[0:n, 10:11], s_[0:n, 1:2], s_[0:n, 1:2], ALU.mult, r=[st_], w=[st_], s2=EPS, op1=ALU.mult)
            tt('dve', s_[0:n, 10:11], s_[0:n, 10:11], s_[0:n, 9:10], ALU.add, r=[st_], w=[st_])
            tt('pool', s_[0:n, 11:12], s_[0:n, 10:11], mhalf[0:n, 0:1], ALU.pow, r=[st_, t_c], w=[st_])
            ya, tya = ya_r.next()
            ts('dve', ya[0:n, :], pN[0:n, 0:256], s_[0:n, 8:9], ALU.subtract, r=[tN, st_], w=[tya], s2=s_[0:n, 11:12], op1=ALU.mult)
            tt('dve', ya[0:n, :], ya[0:n, :], gs, ALU.mult, r=[tya] + t_in, w=[tya])
            pY, tY = psum()
            for vc in range(2):
                tr(pY[:, vc * 128:vc * 128 + n], ya[0:n, vc * 128:(vc + 1) * 128], ident_f[0:n, 0:n], r=[tya, t_c], w=[tY])
            cp('act', yaT_dst, pY[:, 0:256].rearrange("p (a b) -> p a b", a=2)[:, :, 0:n], r=[tY], w=[t_yaT])
            s2_, st2 = small.next()
            ts('dve', s2_[0:n, 0:1], Ucol, Ace[0:n, :], ALU.subtract, r=t_in, w=[st2])
            act(s2_[0:n, 0:1], s2_[0:n, 0:1], AF.Exp, r=[st2], w=[st2])
            tt('dve', s2_[:, 1:2], Acs, Ace, ALU.subtract, r=t_in, w=[st2])
            act(s2_[:, 1:2], s2_[:, 1:2], AF.Exp, r=[st2], w=[st2])
            pK, tK = psum()
            mm(pK[0:n, 0:128], kTv, ident_b[:], True, True, r=t_in + [t_c], w=[tK])
            kw, tkw = kw_r.next()
            ts('dve', kw[0:n, :], pK[0:n, 0:128], s2_[0:n, 0:1], ALU.mult, r=[tK, st2], w=[tkw])
            pU, tU = psum()
            mm(pU[:, 0:257], kw[0:n, :], vaug, True, True, r=[tkw] + t_in, w=[tU])
            stt(Cout_ap, Cf_ap, s2_[:, 1:2], pU[:, 0:257], ALU.mult, ALU.add, r=[t_Cst, st2, tU], w=[t_Cout])

        NPC = TP // CH
        chunks = [(c, CH) for c in range(NPC)] + [(NPC, 64)]
        th = []
        for (c_, N_) in chunks:
            s_chunk = (c_ == NPC)
            th.append(lambda l=l: wblock([w_in[l][:, O_IG:O_IG + 8].rearrange("(kc p) n -> p kc n", p=128)]))
            for h_ in range(4):
                th.append(lambda l=l, h_=h_: wmulti([(w_in[l], O_MQ + h_ * 128, 128), (w_in[l], O_MK + h_ * 128, 128)]))
                th.append(lambda l=l, h_=h_: wmulti([(w_in[l], O_MV + h_ * 256, 256), (w_in[l], O_OG + h_ * 256, 256)]))
            if not s_chunk:
                th.append(lambda l=l: wmulti([(w_in[l], O_SK, 256), (w_in[l], O_SV, 256)]))
            for g_ in range(4):
                th.append(lambda l=l, g_=g_: wmulti([(w_in[l], O_SQ + g_ * 256, 256), (w_in[l], O_SK + g_ * 64, 64)]))
                if s_chunk:
                    th.append(lambda l=l: wmulti([(w_in[l], O_SK, 256), (w_in[l], O_SV, 256)]))
            for m_ in range(8):
                th.append(lambda l=l, m_=m_: wmulti([(w_a[l], m_ * 128, 128), (w_b[l], m_ * 128, 128), (w_in[l], O_GA + m_ * 128, 128), (w_in[l], O_GB + m_ * 128, 128)]))
            for half_ in range(2):
                th.append(lambda l=l, half_=half_: wmat(w_out[l], half_ * 512, 512))
        WS = WStream(th)
        for (c, N) in chunks:
            is_s = (c == NPC)
            tiles = [16] if is_s else list(range(TPC * c, TPC * c + TPC))
            transpose_tiles(xT_c, t_xT, tiles, 0)
            wg_v, wg_t = WS.next()
            wg = wg_v[0]
            pI, tI = psum()
            pF, tF = psum()
            for kc in range(8):
                mm(pI[0:4, 0:N], wg[:, kc, 0:4], xT_c[:, kc, 0:N], kc == 0, kc == 7, r=[wg_t, t_xT], w=[tI])
            for kc in range(8):
                mm(pF[0:4, 0:N], wg[:, kc, 4:8], xT_c[:, kc, 0:N], kc == 0, kc == 7, r=[wg_t, t_xT], w=[tF])
            act(Urow[:, 0:N], pI[0:4, 0:N], AF.Identity, r=[tI, t_bcol], w=[t_rows], bias=bcol[0:4, SL_IG:SL_IG + 1])
            act(fgr[:, 0:N], pF[0:4, 0:N], AF.Identity, r=[tF, t_bcol], w=[t_rows], bias=bcol[0:4, SL_FG:SL_FG + 1])
            act(fgr[:, 0:N], fgr[:, 0:N], AF.Exp, r=[t_rows], w=[t_rows], scale=-1.0)
            act(fgr[:, 0:N], fgr[:, 0:N], AF.Ln, r=[t_rows], w=[t_rows], bias=1.0)
            if not is_s:
                if c == 0:
                    P.op('dve', lambda e: e.tensor_tensor_scan(out=Brow[:, :], data0=ones_row[:, :], data1=fgr[:, :], initial=0.0, op0=ALU.mult, op1=ALU.subtract), r=[t_rows], w=[t_rows])
                else:
                    P.op('dve', lambda e: e.tensor_tensor_scan(out=Brow[:, :], data0=ones_row[:, :], data1=fgr[:, :], initial=carry[:, 0:1], op0=ALU.mult, op1=ALU.subtract), r=[t_rows, t_carry], w=[t_rows])
                tt('dve', Urow[:, :], Urow[:, :], Brow[:, :], ALU.subtract, r=[t_rows], w=[t_rows])
                if c == 0:
                    P.op('dve', lambda e: e.tensor_tensor_scan(out=Arow[:, :], data0=Urow[:, :], data1=Urow[:, :], initial=0.0, op0=ALU.max, op1=ALU.max), r=[t_rows], w=[t_rows])
                else:
                    P.op('dve', lambda e: e.tensor_tensor_scan(out=Arow[:, :], data0=Urow[:, :], data1=Urow[:, :], initial=carry[:, 1:2], op0=ALU.max, op1=ALU.max), r=[t_rows, t_carry], w=[t_rows])
                cp('dve', carry[:, 0:1], Brow[:, CH - 1:CH], r=[t_rows], w=[t_carry])
                cp('dve', carry[:, 1:2], Arow[:, CH - 1:CH], r=[t_rows], w=[t_carry])
            else:
                dma(m0row, sm[l].rearrange("(s h) -> h s", h=4), w=[t_m0], slow=True)
                dma(m0rep, sm[l:l + 1, :].to_broadcast([128, 64]), w=[t_m0])
                f3 = fgr[:, 0:64].rearrange("p (s t) -> p s t", t=4)
                B3 = Brow[:, 0:64].rearrange("p (s t) -> p s t", t=4)
                U3 = Urow[:, 0:64].rearrange("p (s t) -> p s t", t=4)
                A3 = Arow[:, 0:64].rearrange("p (s t) -> p s t", t=4)
                ts('dve', B3[:, :, 0], f3[:, :, 0], -1.0, ALU.mult, r=[t_rows], w=[t_rows])
                for t in range(1, 4):
                    tt('dve', B3[:, :, t], B3[:, :, t - 1], f3[:, :, t], ALU.subtract, r=[t_rows], w=[t_rows])
                tt('dve', Urow[:, 0:64], Urow[:, 0:64], Brow[:, 0:64], ALU.subtract, r=[t_rows], w=[t_rows])
                tt('dve', A3[:, :, 0], U3[:, :, 0], m0row[:, :], ALU.max, r=[t_rows, t_m0], w=[t_rows])
                for t in range(1, 4):
                    tt('dve', A3[:, :, t], A3[:, :, t - 1], U3[:, :, t], ALU.max, r=[t_rows], w=[t_rows])
                tt('dve', msrow[:, :], A3[:, :, 3], B3[:, :, 3], ALU.add, r=[t_rows], w=[t_m0])
                dma(o_ms[l].rearrange("(s h) -> h s", h=4), msrow[:, :], r=[t_m0], slow=True)
            stt(fgr[:, 0:N], Arow[:, 0:N], -1.0, Brow[:, 0:N], ALU.mult, ALU.subtract, r=[t_rows], w=[t_rows])
            pC, tC = psum()
            if not is_s:
                for jj in range(TPC):
                    tr(pC[:, jj * 8:jj * 8 + 4], Urow[:, jj * 128:(jj + 1) * 128], ident_f[0:4, 0:4], r=[t_rows, t_c], w=[tC])
                    tr(pC[:, jj * 8 + 4:jj * 8 + 8], fgr[:, jj * 128:(jj + 1) * 128], ident_f[0:4, 0:4], r=[t_rows, t_c], w=[tC])
                cp('dve', cols[:, :, 0:4], pC[:, 0:8 * TPC].rearrange("p (a b) -> p a b", a=TPC)[:, :, 0:4], r=[tC], w=[t_cols])
                act(cols[:, :, 4:8], pC[:, 0:8 * TPC].rearrange("p (a b) -> p a b", a=TPC)[:, :, 4:8], AF.Exp, r=[tC], w=[t_cols])
            else:
                for s in range(16):
                    tr(pC[0:4, s * 8:s * 8 + 4], Urow[:, s * 4:(s + 1) * 4], ident_f[0:4, 0:4], r=[t_rows, t_c], w=[tC])
                    tr(pC[0:4, s * 8 + 4:s * 8 + 8], fgr[:, s * 4:(s + 1) * 4], ident_f[0:4, 0:4], r=[t_rows, t_c], w=[tC])
                cp('dve', cols_s[:, :, 0:4], pC[0:4, 0:128].rearrange("p (a b) -> p a b", a=16)[:, :, 0:4], r=[tC], w=[t_cols])
                act(cols_s[:, :, 4:8], pC[0:4, 0:128].rearrange("p (a b) -> p a b", a=16)[:, :, 4:8], AF.Exp, r=[tC], w=[t_cols])

            for h in range(4):
                ArU, tAr = ArepU.next()
                pA, tA = psum()
                mm(pA[:, 0:N], sel4[:, h, :], Arow[:, 0:N], True, True, r=[t_c, t_rows], w=[tA])
                cp('act', ArU[:, 0:N], pA[:, 0:N], r=[tA], w=[tAr])
                wA, wA_t = WS.next()
                wB, wB_t = WS.next()
                qT, tq = qT_r.next()
                kT, tk = kT_r.next()
                pq, tpq = psum()
                for kc in range(8):
                    mm(pq[:, 0:N], wA[:, kc, 0:128], xT_c[:, kc, 0:N], kc == 0, kc == 7, r=[wA_t, t_xT], w=[tpq])
                ts('dve', qT[:, 0:N], pq[:, 0:N], bcol[:, SL_MQ + h:SL_MQ + h + 1], ALU.add, r=[tpq, t_bcol], w=[tq], s2=float(128 ** -0.5), op1=ALU.mult)
                pk, tpk = psum()
                for kc in range(8):
                    mm(pk[:, 0:N], wA[:, kc, 128:256], xT_c[:, kc, 0:N], kc == 0, kc == 7, r=[wA_t, t_xT], w=[tpk])
                act(kT[:, 0:N], pk[:, 0:N], AF.Identity, r=[tpk, t_bcol], w=[tk], bias=bcol[:, SL_MK + h:SL_MK + h + 1])
                def issue_hb(h_):
                    g_, tg_ = gain_r.next()
                    dma(g_[:, :], mh_gain[l:l + 1, h_ * 256:(h_ + 1) * 256].to_broadcast([128, 256]), w=[tg_])
                    b_, tb_ = hb_r.next()
                    dma(b_[:, 0:256], b_in[l:l + 1, O_MV + h_ * 256:O_MV + (h_ + 1) * 256].to_broadcast([128, 256]), w=[tb_])
                    dma(b_[:, 256:512], b_in[l:l + 1, O_OG + h_ * 256:O_OG + (h_ + 1) * 256].to_broadcast([128, 256]), w=[tb_])
                    return g_, tg_, b_, tb_
                if h == 0:
                    nxt_hb = issue_hb(0)
                gain, t_gain, bt_, tbt = nxt_hb
                if h + 1 < 4:
                    nxt_hb = issue_hb(h + 1)
                units = [(jj, 128, jj * 128) for jj in range(TPC)] if not is_s else [(s, 4, s * 4) for s in range(16)]
                def issue_c0(u_, h=h):
                    cf_, tcf_ = Cs_f.next()
                    cb_, tcb_ = Cs_b.next()
                    dma(cf_[:, 0:256], sC[l, u_, h], w=[tcf_])
                    dma(cf_[:, 256:257], sn[l, u_, h, :].unsqueeze(1), w=[tcf_])
                    cp('pool', cb_[:, :], cf_[:, :], r=[tcf_], w=[tcb_])
                    return cf_, tcf_, cb_, tcb_
                nxt_c0 = issue_c0(0) if is_s else None
                for (u, n, c0) in units:
                    cs = slice(c0, c0 + n)
                    if is_s:
                        cur_c0 = nxt_c0
                        nxt_c0 = issue_c0(u + 1) if u + 1 < 16 else None
                    pv, tpv = psum()
                    for kc in range(8):
                        mm(pv[0:n, :], xT_c[:, kc, cs], wB[:, kc, :], kc == 0, kc == 7, r=[wB_t, t_xT], w=[tpv])
                    va, tva = vaug_r.next()
                    tt('dve', va[0:n, 0:256], pv[0:n, 0:256], bt_[0:n, 0:256], ALU.add, r=[tpv, tbt], w=[tva])
                    og, tog = ogt_r.next()
                    tt('dve', og[0:n, :], pv[0:n, 256:512], bt_[0:n, 256:512], ALU.add, r=[tpv, tbt], w=[tog])
                    act(og[0:n, :], og[0:n, :], AF.Tanh, r=[tog], w=[tog], scale=0.5)
                    act(og[0:n, :], og[0:n, :], AF.Identity, r=[tog], w=[tog], bias=0.5, scale=0.5)
                    tt('dve', og[0:n, :], og[0:n, :], gain[0:n, :], ALU.mult, r=[tog, t_gain], w=[tog])
                    t_in = [tq, tk, tAr, t_cols, tva, tog, t_acs, t_m0]
                    yaT_dst = yaT[:, 2 * h:2 * h + 2, cs]
                    if not is_s:
                        Acs = acs_c[:, h:h + 1] if u == 0 else ArU[:, c0 - 1:c0]
                        Ace = ArU[:, c0 + n - 1:c0 + n]
                        mlstm_core(n, h, kT[:, cs], qT[:, cs], ArU[:, cs], cols[:, u, h:h + 1], cols[:, u, 4 + h:5 + h],
                                   Acs, Ace, va[0:n, :], og[0:n, :], t_in, C_b[:, h, :], C_f[:, h, :], C_f[:, h, :], t_C[h], t_C[h], yaT_dst)
                        cp('act', C_b[:, h, :], C_f[:, h, :], r=[t_C[h]], w=[t_C[h]])
                    else:
                        cf, tcf, cb, tcb = cur_c0
                        co, tco = Cs_o.next()
                        Acs = m0rep[:, u * 4 + h:u * 4 + h + 1]
                        Ace = ArU[:, c0 + n - 1:c0 + n]
                        mlstm_core(n, h, kT[:, cs], qT[:, cs], ArU[:, cs], cols_s[0:4, u, h:h + 1], cols_s[0:4, u, 4 + h:5 + h],
                                   Acs, Ace, va[0:n, :], og[0:n, :], t_in + [tcb], cb[:, :], cf[:, :], co[:, :], tcf, tco, yaT_dst)
                        dma(o_Cs[l, u, h], co[:, 0:256], r=[tco])
                        dma(o_ns[l, u, h, :].unsqueeze(1), co[:, 256:257], r=[tco])
                if not is_s:
                    cp('dve', acs_c[:, h:h + 1], ArU[:, CH - 1:CH], r=[tAr], w=[t_acs])
                    if c == NPC - 1:
                        dma(o_Cp[l, h], C_f[:, h, 0:256], r=[t_C[h]])
                        dma(o_np[l, h, :].unsqueeze(1), C_f[:, h, 256:257], r=[t_C[h]])
            if c == NPC - 1:
                s_, st_ = small.next()
                tt('dve', s_[0:4, 0:1], carry[:, 0:1], carry[:, 1:2], ALU.add, r=[t_carry], w=[st_])
                dma(o_mp[l, :].unsqueeze(1), s_[0:4, 0:1], r=[st_])

            if not is_s:
                wKV, wKV_t = WS.next()
            btk_, tbtk = bt_r.next()
            dma(btk_[:, :], b_in[l:l + 1, O_SK:O_SK + 512].to_broadcast([128, 512]), w=[tbtk])
            if not is_s:
                for jj in range(TPC):
                    j = TPC * c + jj
                    pkv, tpkv = psum()
                    for kc in range(8):
                        mm(pkv[:, :], xT_c[:, kc, jj * 128:(jj + 1) * 128], wKV[:, kc, :], kc == 0, kc == 7, r=[wKV_t, t_xT], w=[tpkv])
                    dst = vtd[:, jj + 1, :].rearrange("p (g u d) -> p g u d", g=4, u=2)
                    src = pkv[:, 256:512].rearrange("p (g d) -> p g d", g=4).unsqueeze(2).to_broadcast([128, 4, 2, 64])
                    bsrc = btk_[:, 256:512].rearrange("p (g d) -> p g d", g=4).unsqueeze(2).to_broadcast([128, 4, 2, 64])
                    tt('dve', dst, src, bsrc, ALU.add, r=[tpkv, tbtk], w=[t_vtd[jj + 1]])
                    if j == 15:
                        kvo, tkvo = kvo_r.next()
                        tt('dve', kvo[:, :], pkv[:, :], btk_[:, :], ALU.add, r=[tpkv, tbtk], w=[tkvo])
                        dma(o_kp[l], kvo[:, 0:256], r=[tkvo])
                        dma(o_vp[l], kvo[:, 256:512], r=[tkvo])
            else:
                dma(o_ks[l, :, 0:124, :], sk[l, :, 4:128, :])
                dma(o_vs[l, :, 0:124, :], sv[l, :, 4:128, :])
            for g in range(4):
                wS, wS_t = WS.next()
                if is_s:
                    wKV, wKV_t = WS.next()
                for i in range(4):
                    pq, tpq = psum()
                    for kc in range(8):
                        mm(pq[0:64, 0:N], wS[:, kc, i * 64:(i + 1) * 64], xT_c[:, kc, 0:N], kc == 0, kc == 7, r=[wS_t, t_xT], w=[tpq])
                    act(qTg[:, i, 0:N], pq[0:64, 0:N], AF.Identity, r=[tpq, t_bcol], w=[t_qTg], bias=bcol[0:64, SL_SQ + 4 * g + i:SL_SQ + 4 * g + i + 1])
                pk, tpk = psum()
                for kc in range(8):
                    mm(pk[0:64, 0:N], wS[:, kc, 256:320], xT_c[:, kc, 0:N], kc == 0, kc == 7, r=[wS_t, t_xT], w=[tpk])
                act(kTg[:, g, 128:128 + N], pk[0:64, 0:N], AF.Identity, r=[tpk, t_bcol], w=[t_kTg[g]], bias=bcol[0:64, SL_SK + g:SL_SK + g + 1])
                if not is_s:
                    for jj in range(TPC):
                        j = TPC * c + jj
                        cs = slice(jj * 128, (jj + 1) * 128)
                        pNm, tNm = psum()
                        pDn, tDn = psum()
                        kts = ([(1, jj)] if j > 0 else []) + [(0, jj + 1)]
                        for ki, (kind, slot) in enumerate(kts):
                            pS, tS = psum()
                            mm(pS[:, :].rearrange("p (a b) -> p a b", a=4), kTg[:, g, slot * 128:(slot + 1) * 128], qTg[:, :, cs], True, True, r=[t_kTg[g], t_qTg], w=[tS])
                            E_, tE = E_r.next()
                            act(E_[:, :], pS[:, :], AF.Exp, r=[tS], w=[tE], scale=0.125)
                            PT, tPT = PT_r.next()
                            tt('dve', PT[:, :].rearrange("p (a b) -> p a b", a=4), E_[:, :].rearrange("p (a b) -> p a b", a=4), EB[:, 4 * g:4 * g + 4, kind, :], ALU.mult, r=[tE, t_c], w=[tPT])
                            first = ki == 0
                            last = ki == len(kts) - 1
                            mm(pNm[:, :], vtd[:, slot, g * 128:(g + 1) * 128], PT[:, :], first, last, r=[t_vtd[slot], tPT], w=[tNm])
                            mm(pDn[:, :], ones_b[:, :], PT[:, :], first, last, r=[tPT, t_c], w=[tDn])
                        rd, trd = rd_r.next()
                        for i in range(4):
                            ts('dve', rd[:, i * 128:(i + 1) * 128], pDn[:, i * 128:(i + 1) * 128], esink[:, 4 * g + i:4 * g + i + 1], ALU.add, r=[tDn, t_esink], w=[trd])
                        P.op('dve', lambda e, rd=rd: e.reciprocal(out=rd[:, :], in_=rd[:, :]), r=[trd], w=[trd])
                        for par in range(2):
                            ps_ = slice(par * 64, (par + 1) * 64)
                            nv = pNm[ps_, :].rearrange("p (a b) -> p a b", a=4)[:, par::2, :]
                            rv = rd[ps_, :].rearrange("p (a b) -> p a b", a=4)[:, par::2, :]
                            tt('dve', ybT[ps_, 2 * g:2 * g + 2, cs], nv, rv, ALU.mult, r=[tNm, trd], w=[t_ybT])
                else:
                    def issue_kv(s_, g=g):
                        kb, tkb = kbuf_r.next()
                        vb, tvb = vbuf_r.next()
                        dma(kb[:, :], sk[l, s_, :, g * 64:(g + 1) * 64], w=[tkb])
                        dma(vb[:, :], sv[l, s_, :, g * 64:(g + 1) * 64], w=[tvb])
                        pT_, tT_ = psum()
                        tr(pT_[0:64, 0:128], kb[:, :], ident_f[:, :], r=[tkb, t_c], w=[tT_])
                        kbT_, tkbT_ = kbT_r.next()
                        cp('act', kbT_[:, 0, :], pT_[0:64, 0:128], r=[tT_], w=[tkbT_])
                        vbd_, tvbd_ = vbd_r.next()
                        cp('pool', vbd_[:, 0:128].rearrange("p (u d) -> p u d", u=2), vb[:, :].unsqueeze(1).to_broadcast([128, 2, 64]), r=[tvb], w=[tvbd_])
                        return kbT_, tkbT_, vbd_, tvbd_
                    nxt_kv = issue_kv(0)
                    for s in range(16):
                        cs = slice(4 * s, 4 * s + 4)
                        kbT, tkbT, vbd, tvbd = nxt_kv
                        nxt_kv = issue_kv(s + 1) if s + 1 < 16 else None
                        pkv, tpkv = psum()
                        for kc in range(8):
                            mm(pkv[0:4, :], xT_c[:, kc, cs], wKV[:, kc, :], kc == 0, kc == 7, r=[wKV_t, t_xT], w=[tpkv])
                        kvo, tkvo = kvo_r.next()
                        tt('dve', kvo[0:4, :], pkv[0:4, :], btk_[0:4, :], ALU.add, r=[tpkv, tbtk], w=[tkvo])
                        if g == 0:
                            dma(o_ks[l, s, 124:128, :], kvo[0:4, 0:256], r=[tkvo])
                            dma(o_vs[l, s, 124:128, :], kvo[0:4, 256:512], r=[tkvo])
                        cp('pool', vbd[0:4, 128:256].rearrange("p (u d) -> p u d", u=2), kvo[0:4, 256 + g * 64:256 + (g + 1) * 64].unsqueeze(1).to_broadcast([4, 2, 64]), r=[tkvo], w=[tvbd])
                        pS1, tS1 = psum()
                        mm(pS1[:, 0:16].rearrange("p (a b) -> p a b", a=4), kbT[:, 0, :], qTg[:, :, cs], True, True, r=[tkbT, t_qTg], w=[tS1])
                        pS2, tS2 = psum()
                        mm(pS2[0:4, 0:16].rearrange("p (a b) -> p a b", a=4), kTg[:, g, 128 + 4 * s:128 + 4 * s + 4], qTg[:, :, cs], True, True, r=[t_kTg[g], t_qTg], w=[tS2])
                        E_, tE = E_r.next()
                        act(E_[:, 0:16], pS1[:, 0:16], AF.Exp, r=[tS1], w=[tE], scale=0.125)
                        act(E_[0:4, 16:32], pS2[0:4, 0:16], AF.Exp, r=[tS2], w=[tE], scale=0.125)
                        PT, tPT = PT_r.next()
                        tt('dve', PT[:, 0:16].rearrange("p (a b) -> p a b", a=4), E_[:, 0:16].rearrange("p (a b) -> p a b", a=4), EB[:, 4 * g:4 * g + 4, 1, 0:4], ALU.mult, r=[tE, t_c], w=[tPT])
                        tt('dve', PT[0:4, 16:32].rearrange("p (a b) -> p a b", a=4), E_[0:4, 16:32].rearrange("p (a b) -> p a b", a=4), EB[0:4, 4 * g:4 * g + 4, 0, 0:4], ALU.mult, r=[tE, t_c], w=[tPT])
                        pNm, tNm = psum()
                        pDn, tDn = psum()
                        mm(pNm[:, 0:16], vbd[:, 0:128], PT[:, 0:16], True, False, r=[tvbd, tPT], w=[tNm])
                        mm(pNm[:, 0:16], vbd[0:4, 128:256], PT[0:4, 16:32], False, True, r=[tvbd, tPT], w=[tNm])
                        mm(pDn[:, 0:16], ones_b[:, :], PT[:, 0:16], True, False, r=[tPT, t_c], w=[tDn])
                        mm(pDn[:, 0:16], ones_b[0:4, :], PT[0:4, 16:32], False, True, r=[tPT, t_c], w=[tDn])
                        rd, trd = rd_r.next()
                        for i in range(4):
                            ts('dve', rd[:, i * 4:(i + 1) * 4], pDn[:, i * 4:(i + 1) * 4], esink[:, 4 * g + i:4 * g + i + 1], ALU.add, r=[tDn, t_esink], w=[trd])
                        P.op('dve', lambda e, rd=rd: e.reciprocal(out=rd[:, 0:16], in_=rd[:, 0:16]), r=[trd], w=[trd])
                        for par in range(2):
                            ps_ = slice(par * 64, (par + 1) * 64)
                            nv = pNm[ps_, 0:16].rearrange("p (a b) -> p a b", a=4)[:, par::2, :]
                            rv = rd[ps_, 0:16].rearrange("p (a b) -> p a b", a=4)[:, par::2, :]
                            tt('dve', ybT[ps_, 2 * g:2 * g + 2, cs], nv, rv, ALU.mult, r=[tNm, trd], w=[t_ybT])
                if not is_s:
                    cp('pool', kTg[:, g, 0:128], kTg[:, g, CH:CH + 128], r=[t_kTg[g]], w=[t_kTg[g]])
            if not is_s:
                cp('pool', vtd[:, 0, :], vtd[:, TPC, :], r=[t_vtd[TPC]], w=[t_vtd[0]])

            for m in range(8):
                wM, wM_t = WS.next()
                res = []
                for (wi, gi, srcT, tsrc, slot) in ((0, 2, yaT, t_yaT, SL_GA + m), (1, 3, ybT, t_ybT, SL_GB + m)):
                    pg, tpg = psum()
                    for kc in range(8):
                        mm(pg[:, 0:N], wM[:, kc, gi * 128:(gi + 1) * 128], xT_c[:, kc, 0:N], kc == 0, kc == 7, r=[wM_t, t_xT], w=[tpg])
                    sg, tsg = sg_r.next()
                    act(sg[:, 0:N], pg[:, 0:N], AF.Tanh, r=[tpg, t_bcolh], w=[tsg], bias=bcolh[:, slot - SL_GA:slot - SL_GA + 1], scale=0.5)
                    act(sg[:, 0:N], sg[:, 0:N], AF.Identity, r=[tsg], w=[tsg], bias=0.5, scale=0.5)
                    pa, tpa = psum()
                    for kc in range(8):
                        mm(pa[:, 0:N], wM[:, kc, wi * 128:(wi + 1) * 128], srcT[:, kc, 0:N], kc == 0, kc == 7, r=[wM_t, tsrc], w=[tpa])
                    tt('dve', sg[:, 0:N], sg[:, 0:N], pa[:, 0:N], ALU.mult, r=[tsg, tpa], w=[tsg])
                    res.append((sg, tsg))
                tt('dve', merged[:, m, 0:N], res[0][0][:, 0:N], res[1][0][:, 0:N], ALU.add, r=[res[0][1], res[1][1]], w=[t_mg])
            if c == 0:
                load_gb(l, 0)
            for half in range(2):
                wO, wO_t = WS.next()
                for u, j in enumerate(tiles):
                    n = tile_rows(j)
                    po, tpo = psum()
                    for kc in range(8):
                        mm(po[0:n, :], merged[:, kc, u * 128:u * 128 + n], wO[:, kc, :], kc == 0, kc == 7, r=[wO_t, t_mg], w=[tpo])
                    xs = x_tok[0:n, j, half * 512:(half + 1) * 512]
                    stt(xs, xs, ALPHA, po[0:n, :], ALU.mult, ALU.add, r=[t_x[j], tpo], w=[t_x[j]])
            for j in tiles:
                layer_norm(j, l, 0, t_gbt)

        P.barrier()
        AR.reset()
        xT_all = AR.alloc([128, 8, TALL], BF16)
        t_xTa = Tok()
        HT = AR.alloc([128, 4, TALL], BF16)
        t_HT = [[Tok() for _ in range(4)] for _ in range(5)]
        sgm_r = AR.rot(2, [128, 512], F32)
        lg_r = AR.rot(2, [128, 80], F32)
        transpose_tiles(xT_all, t_xTa, list(range(NT)), 0)
        load_gb(l, 1)
        dma(brt[:, 0:4], b_rg[l:l + 1, :].to_broadcast([128, 4]), w=[t_brt])
        dma(brt[:, 4:36], b_re[l:l + 1, :].to_broadcast([128, 32]), w=[t_brt])
        th = [lambda l=l: wmulti([(w_rg[l], 0, 4), (w_re[l], 0, 32)])]
        for ex_ in range(32):
            th.append(lambda l=l, ex_=ex_: wmat(w_eg[l, ex_], 0, 512))
            th.append(lambda l=l, ex_=ex_: wmat(w_eu[l, ex_], 0, 512))
            th.append(lambda l=l, ex_=ex_: wmat(w_ed[l, ex_], 0, 1024, kcs=4))
        WS = WStream(th)
        wR, wR_t = WS.next()
        for j in range(NT):
            n = tile_rows(j)
            c0 = j * 128
            pr, tpr = psum()
            for kc in range(8):
                mm(pr[0:n, 0:36], xT_all[:, kc, c0:c0 + n], wR[:, kc, :], kc == 0, kc == 7, r=[wR_t, t_xTa], w=[tpr])
            lg, tlg = lg_r.next()
            L = lg[0:n, :]
            tt('dve', L[:, 0:36], pr[0:n, 0:36], brt[0:n, :], ALU.add, r=[tpr, t_brt], w=[tlg])
            s_, st_ = small.next()
            S = s_[0:n, :]
            P.op('dve', lambda e, S=S, L=L: e.reduce_max(out=S[:, 0:1], in_=L[:, 0:4], axis=mybir.AxisListType.X), r=[tlg], w=[st_])
            ts('dve', S[:, 1:2], S[:, 0:1], -1.0, ALU.mult, r=[st_], w=[st_])
            P.op('act', lambda e, S=S, L=L: e.activation(out=L[:, 36:40], in_=L[:, 0:4], func=AF.Exp, bias=S[:, 1:2], scale=1.0, accum_out=S[:, 2:3]), r=[tlg, st_], w=[tlg, st_])
            ts('dve', L[:, 40:44], L[:, 0:4], S[:, 0:1], ALU.is_equal, r=[tlg, st_], w=[tlg])
            ts('dve', L[:, 40:44], L[:, 40:44], BIG, ALU.mult, r=[tlg], w=[tlg], s2=-BIG, op1=ALU.add)
            tt('dve', L[:, 44:76].rearrange("p (g i) -> p g i", g=4), L[:, 4:36].rearrange("p (g i) -> p g i", g=4), L[:, 40:44].unsqueeze(2).to_broadcast([n, 4, 8]), ALU.add, r=[tlg], w=[tlg])
            P.op('dve', lambda e, S=S, L=L: e.max(out=S[:, 8:16], in_=L[:, 44:76]), r=[tlg], w=[st_])
            ts('dve', S[:, 3:4], S[:, 8:9], -1.0, ALU.mult, r=[st_], w=[st_])
            ts('dve', L[:, 4:36], L[:, 44:76], S[:, 9:10], ALU.is_ge, r=[tlg, st_], w=[tlg])
            act(L[:, 44:76], L[:, 44:76], AF.Exp, r=[tlg, st_], w=[tlg], bias=S[:, 3:4])
            stt(L[:, 44:76], L[:, 44:76], 1.0, L[:, 4:36], ALU.mult, ALU.mult, r=[tlg], w=[tlg, st_], accum=S[:, 4:5])
            tt('dve', S[:, 5:6], S[:, 4:5], S[:, 2:3], ALU.mult, r=[st_], w=[st_])
            P.op('dve', lambda e, S=S: e.reciprocal(out=S[:, 6:7], in_=S[:, 5:6]), r=[st_], w=[st_])
            ts('dve', gate_full[0:n, j, :], L[:, 44:76], S[:, 6:7], ALU.mult, r=[tlg, st_], w=[t_gate[j]])
            P.op('act', lambda e, n=n, j=j: e.mul(out=x_tok[0:n, j, :], in_=x_tok[0:n, j, :], mul=ALPHA), r=[t_x[j], t_xTa], w=[t_x[j]])
        mchunks = [(cc * 512, 512) for cc in range(4)] + [(2048, 64)]
        for ex in range(32):
            wG, wG_t = WS.next()
            wU, wU_t = WS.next()
            for ci, (c0, N) in enumerate(mchunks):
                for fc in range(4):
                    pg, tpg = psum()
                    pu, tpu = psum()
                    for kc in range(8):
                        mm(pg[:, 0:N], wG[:, kc, fc * 128:(fc + 1) * 128], xT_all[:, kc, c0:c0 + N], kc == 0, kc == 7, r=[wG_t, t_xTa], w=[tpg])
                    for kc in range(8):
                        mm(pu[:, 0:N], wU[:, kc, fc * 128:(fc + 1) * 128], xT_all[:, kc, c0:c0 + N], kc == 0, kc == 7, r=[wU_t, t_xTa], w=[tpu])
                    sg, tsg = sgm_r.next()
                    act(sg[:, 0:N], pg[:, 0:N], AF.Silu, r=[tpg], w=[tsg])
                    tt('dve', HT[:, fc, c0:c0 + N], sg[:, 0:N], pu[:, 0:N], ALU.mult, r=[tsg, tpu], w=[t_HT[ci][fc]])
            wD, wD_t = WS.next()
            for j in range(NT):
                n = tile_rows(j)
                ci = min(j // 4, 4)
                for half in range(2):
                    py, tpy = psum()
                    for fc in range(4):
                        mm(py[0:n, :], HT[:, fc, j * 128:j * 128 + n], wD[:, fc, half * 512:(half + 1) * 512], fc == 0, fc == 3, r=[wD_t, t_HT[ci][fc]], w=[tpy])
                    xs = x_tok[0:n, j, half * 512:(half + 1) * 512]
                    stt(xs, py[0:n, :], gate_full[0:n, j, ex:ex + 1], xs, ALU.mult, ALU.add, r=[tpy, t_gate[j], t_x[j]], w=[t_x[j]])
        for j in range(NT):
            layer_norm(j, l, 1, t_gbt)

        P.barrier()
        AR.reset()
        xT_all = AR.alloc([128, 8, TALL], BF16)
        t_xTa = Tok()
        pT_all = AR.alloc([128, 2, TALL], BF16)
        t_pT = Tok()
        ptile_r = AR.rot(2, [128, 256], F32)
        sgp_r = AR.rot(2, [128, 512], F32)
        transpose_tiles(xT_all, t_xTa, list(range(NT)), 0)
        load_gb(l, 2)
        for j in range(NT):
            n = tile_rows(j)
            c0 = j * 128
            pt, tpt = ptile_r.next()
            dma(pt[0:n, :], pin[l, c0:c0 + n, :], w=[tpt])
            pb, ptk = psum()
            for q in range(2):
                tr(pb[:, q * 128:q * 128 + n], pt[0:n, q * 128:(q + 1) * 128], ident_f[0:n, 0:n], r=[tpt, t_c], w=[ptk])
            cp('act', pT_all[:, :, c0:c0 + n], pb[:, 0:256].rearrange("p (a b) -> p a b", a=2)[:, :, 0:n], r=[ptk], w=[t_pT])
            P.op('act', lambda e, n=n, j=j: e.mul(out=x_tok[0:n, j, :], in_=x_tok[0:n, j, :], mul=ALPHA), r=[t_x[j], t_xTa], w=[t_x[j]])
        th = []
        for half_ in range(2):
            th.append(lambda l=l, half_=half_: wmat(w_pg[l], half_ * 512, 512))
            th.append(lambda l=l, half_=half_: wmat(w_pp[l], half_ * 512, 512, kcs=2))
        WS = WStream(th)
        for half in range(2):
            wPG, wPG_t = WS.next()
            wPP, wPP_t = WS.next()
            for j in range(NT):
                n = tile_rows(j)
                c0 = j * 128
                p1, tp1 = psum()
                for kc in range(8):
                    mm(p1[0:n, :], xT_all[:, kc, c0:c0 + n], wPG[:, kc, :], kc == 0, kc == 7, r=[wPG_t, t_xTa], w=[tp1])
                p2, tp2 = psum()
                for kc in range(2):
                    mm(p2[0:n, :], pT_all[:, kc, c0:c0 + n], wPP[:, kc, :], kc == 0, kc == 1, r=[wPP_t, t_pT], w=[tp2])
                sg, tsg = sgp_r.next()
                act(sg[0:n, :], p1[0:n, :], AF.Sigmoid, r=[tp1], w=[tsg])
                tt('dve', sg[0:n, :], sg[0:n, :], p2[0:n, :], ALU.mult, r=[tsg, tp2], w=[tsg])
                xs = x_tok[0:n, j, half * 512:(half + 1) * 512]
                tt('dve', xs, xs, sg[0:n, :], ALU.add, r=[t_x[j], tsg], w=[t_x[j]])
        for j in range(NT):
            layer_norm(j, l, 2, t_gbt)
        P.barrier()

    dma(o_y[0:1024, :].rearrange("(j p) d -> p j d", p=128), x_tok[:, 0:8, :], r=t_x[0:8])
    dma(o_y[1024:2048, :].rearrange("(j p) d -> p j d", p=128), x_tok[:, 8:16, :], r=t_x[8:16])
    dma(o_y[2048:2112, :], x_tok[0:64, 16, :], r=[t_x[16]])
    P.emit()
    return nc, P


_CACHE = {}


def kernel(x_prompt, x_sample, state_mlstm_C, state_mlstm_n, state_mlstm_m, state_swa_k, state_swa_v,
           p_prompt, p_sample, w_in, b_in, mh_gain, w_a, w_b, w_out, rel_table, w_sink, ln_g, ln_b,
           w_rg, b_rg, w_re, b_re, w_eg, w_eu, w_ed, w_pg, w_pp, _depth=4):
    f = lambda a: np.ascontiguousarray(np.asarray(a, dtype=np.float32))
    if _depth not in _CACHE:
        _CACHE[_depth] = build(_depth)
    nc, P = _CACHE[_depth]
    consts = make_consts()
    shared = dict(w_in=f(w_in), b_in=f(b_in), mh_gain=f(mh_gain), w_a=f(w_a), w_b=f(w_b), w_out=f(w_out),
                  rel_table=f(rel_table), w_sink=f(w_sink), ln_g=f(ln_g), ln_b=f(ln_b), w_rg=f(w_rg), b_rg=f(b_rg),
                  w_re=f(w_re), b_re=f(b_re), w_eg=f(w_eg), w_eu=f(w_eu), w_ed=f(w_ed), w_pg=f(w_pg), w_pp=f(w_pp))
    shared.update(consts)
    x_prompt = f(x_prompt); x_sample = f(x_sample); p_prompt = f(p_prompt); p_sample = f(p_sample)
    sCa = f(state_mlstm_C); sna = f(state_mlstm_n); sma = f(state_mlstm_m); ska = f(state_swa_k); sva = f(state_swa_v)
    in_maps = []
    for c in range(8):
        sl = slice(16 * c, 16 * c + 16)
        m = dict(shared)
        m['xin'] = np.ascontiguousarray(np.concatenate([x_prompt[c], x_sample[sl].reshape(64, D)], 0))
        m['pin'] = np.ascontiguousarray(np.concatenate([p_prompt[:, c], p_sample[:, sl].reshape(4, 64, 256)], 1))
        m['sC'] = np.ascontiguousarray(sCa[:, sl])
        m['sn'] = np.ascontiguousarray(sna[:, sl])
        m['sm'] = np.ascontiguousarray(sma[:, sl].reshape(4, 64))
        m['sk'] = np.ascontiguousarray(ska[:, sl].reshape(4, 16, 128, 256))
        m['sv'] = np.ascontiguousarray(sva[:, sl].reshape(4, 16, 128, 256))
        in_maps.append(m)
    res = run_bass_kernel_spmd(nc, in_maps, core_ids=list(range(8)))
    R = res.results
    y_p = np.stack([R[c]['o_y'][:2048] for c in range(8)], 0)
    y_s = np.concatenate([R[c]['o_y'][2048:].reshape(16, 4, D) for c in range(8)], 0)
    C_p = np.stack([R[c]['o_Cp'] for c in range(8)], 1)
    n_p = np.stack([R[c]['o_np'] for c in range(8)], 1)
    m_p = np.stack([R[c]['o_mp'] for c in range(8)], 1)
    k_p = np.stack([R[c]['o_kp'].reshape(4, 128, 4, 64) for c in range(8)], 1)
    v_p = np.stack([R[c]['o_vp'].reshape(4, 128, 4, 64) for c in range(8)], 1)
    C_s = np.concatenate([R[c]['o_Cs'] for c in range(8)], 1)
    n_s = np.concatenate([R[c]['o_ns'] for c in range(8)], 1)
    m_s = np.concatenate([R[c]['o_ms'].reshape(4, 16, 4) for c in range(8)], 1)
    k_s = np.concatenate([R[c]['o_ks'].reshape(4, 16, 128, 4, 64) for c in range(8)], 1)
    v_s = np.concatenate([R[c]['o_vs'].reshape(4, 16, 128, 4, 64) for c in range(8)], 1)
    outs = (y_p, y_s, C_p, n_p, m_p, k_p, v_p, C_s, n_s, m_s, k_s, v_s)
    return tuple(np.ascontiguousarray(o, dtype=np.float32) for o in outs)
```

```python
import contextlib
import numpy as np
import concourse.bass as bass
import concourse.mybir as mybir
from concourse.bass_utils import run_bass_kernel_spmd

F32 = mybir.dt.float32
BF16 = mybir.dt.bfloat16
AF = mybir.ActivationFunctionType
ALU = mybir.AluOpType

ENGS = ['pe', 'act', 'dve', 'pool', 'sp']
N_DMA_SEM = 8
SAME_ENGINE_SYNC = True

D = 1024
NT = 17
TP = 2048
TS = 64
TALL = TP + TS
DIN = 6664
O_MQ, O_MK, O_MV, O_OG, O_IG, O_FG, O_SQ, O_SK, O_SV, O_GA, O_GB = 0, 512, 1024, 2048, 3072, 3076, 3080, 4104, 4360, 4616, 5640
ALPHA = float((2 * 4) ** 0.25)
EPS = 1e-5
BIG = 30000.0
FW = 400
CH = 256
TPC = CH // 128


class Tok:
    __slots__ = ('w', 'rs', 'const')

    def __init__(self, const=False):
        self.w = None
        self.rs = []
        self.const = const


class Op:
    __slots__ = ('eng', 'fn', 'deps', 'raw', 'sig', 'count', 'is_dma', 'dma_id', 'waits')


class Prog:
    def __init__(self, nc):
        self.nc = nc
        self.ops = {e: [] for e in ENGS}
        self.n_dma = 0
        self.dmas = []
        self.stack = contextlib.ExitStack()
        self._n = 0

    def sb(self, shape, dtype):
        self._n += 1
        return self.stack.enter_context(self.nc.sbuf_tensor(f"sb{self._n}", list(shape), dtype))

    def ps(self, shape, dtype=F32):
        self._n += 1
        return self.stack.enter_context(self.nc.psum_tensor(f"ps{self._n}", list(shape), dtype))

    def op(self, eng, fn, r=(), w=()):
        o = Op()
        o.eng = eng
        o.fn = fn
        o.deps = set()
        o.raw = set()
        o.sig = False
        o.is_dma = False
        o.count = 0
        for t in r:
            if t.w is not None:
                o.deps.add(t.w)
                o.raw.add(t.w)
        for t in w:
            if t.w is not None:
                o.deps.add(t.w)
            for x in t.rs:
                o.deps.add(x)
        for t in r:
            if not t.const:
                t.rs.append(o)
        for t in w:
            t.w = o
            t.rs = []
        o.deps.discard(o)
        self.ops[eng].append(o)
        return o

    def dma(self, fn, r=(), w=()):
        o = self.op('sp', fn, r, w)
        o.is_dma = True
        o.dma_id = self.n_dma
        if self.n_dma >= N_DMA_SEM:
            o.deps.add(self.dmas[self.n_dma - N_DMA_SEM])
        self.n_dma += 1
        self.dmas.append(o)
        return o

    def barrier(self):
        last = []
        for e in ENGS:
            if e == 'sp':
                continue
            if self.ops[e]:
                last.append(self.ops[e][-1])
        last += self.dmas[-N_DMA_SEM:]
        for e in ENGS:
            o = self.op(e, None)
            for d in last:
                if d is not o:
                    o.deps.add(d)

    def emit(self):
        nc = self.nc

        def skip(d, o):
            return d.eng == o.eng and (d.eng in ('pe', 'sp') or not SAME_ENGINE_SYNC or d not in o.raw)

        for e in ENGS:
            for o in self.ops[e]:
                for d in o.deps:
                    if d.is_dma or skip(d, o):
                        continue
                    d.sig = True
        for e in ENGS:
            c = 0
            for o in self.ops[e]:
                if o.sig and not o.is_dma and o.fn is not None:
                    c += 1
                o.count = c
        sems = {e: self.stack.enter_context(nc.semaphore(f"s_{e}")) for e in ENGS}
        dsems = [self.stack.enter_context(nc.semaphore(f"s_dma{i}")) for i in range(N_DMA_SEM)]

        def dma_target(d):
            return dsems[d.dma_id % N_DMA_SEM], 16 * (d.dma_id // N_DMA_SEM + 1)

        nwaits = 0
        for e in ENGS:
            waited = {}
            for o in self.ops[e]:
                need = {}
                for d in o.deps:
                    if d.is_dma:
                        s, v = dma_target(d)
                    else:
                        if skip(d, o):
                            continue
                        s, v = sems[d.eng], d.count
                    if need.get(s, 0) < v:
                        need[s] = v
                o.waits = []
                for s, v in need.items():
                    if waited.get(s, 0) < v:
                        waited[s] = v
                        o.waits.append((s, v))
                        nwaits += 1
        self.stats = {e: len(self.ops[e]) for e in ENGS}
        self.stats['waits'] = nwaits
        final_dma = {}
        for d in self.dmas:
            s, v = dma_target(d)
            final_dma[s] = max(final_dma.get(s, 0), v)

        def replay(ename, eng):
            for o in self.ops[ename]:
                for s, v in o.waits:
                    eng.wait_ge(s, v)
                if o.fn is None:
                    continue
                ins = o.fn(eng)
                if o.is_dma:
                    s, v = dma_target(o)
                    ins.then_inc(s, 16)
                elif o.sig:
                    ins.then_inc(sems[ename], 1)
            if ename == 'sp':
                for s, v in final_dma.items():
                    eng.wait_ge(s, v)

        with nc.Block() as block:
            @block.tensor
            def _(eng):
                replay('pe', eng)

            @block.scalar
            def _(eng):
                replay('act', eng)

            @block.vector
            def _(eng):
                replay('dve', eng)

            @block.gpsimd
            def _(eng):
                replay('pool', eng)

            @block.sync
            def _(eng):
                replay('sp', eng)
        self.stack.close()


def rel_bucket_np(dist):
    n = np.maximum(dist, 0)
    max_exact = 16
    large = max_exact + (np.log(np.maximum(n, 1) / max_exact) / np.log(128 / max_exact) * (32 - max_exact)).astype(np.int32)
    large = np.minimum(large, 31)
    return np.where(n < max_exact, n, large).astype(np.int32)


def make_consts():
    c = {}
    c['c_ident'] = np.eye(128, dtype=np.float32)
    s = np.arange(128)[:, None]
    l = np.arange(128)[None, :]
    c['c_maskbig'] = np.where(s > l, BIG, 0.0).astype(np.float32)
    i = np.arange(FW)
    dist = i - 144
    valid = (dist >= 0) & (dist <= 128)
    oh = np.zeros((32, FW), np.float32)
    b = rel_bucket_np(dist)
    oh[b[valid], i[valid]] = 1.0
    c['c_ohd'] = oh
    c['c_maskvec'] = np.where(valid, 0.0, -BIG).astype(np.float32)[None, :]
    sel = np.zeros((4, 4, 128), np.float32)
    for h in range(4):
        sel[h, h, :] = 1.0
    c['c_sel4'] = sel.reshape(4, 512)
    return c


def build(depth=4):
    nc = bass.Bass("TRN2", target_bir_lowering=False)
    P = Prog(nc)

    def din(name, shape):
        return nc.dram_tensor(name, list(shape), F32, kind="ExternalInput").ap()

    def dout(name, shape):
        return nc.dram_tensor(name, list(shape), F32, kind="ExternalOutput").ap()

    xin = din("xin", [TALL, D])
    pin = din("pin", [4, TALL, 256])
    sC = din("sC", [4, 16, 4, 128, 256])
    sn = din("sn", [4, 16, 4, 128])
    sm = din("sm", [4, 64])
    sk = din("sk", [4, 16, 128, 256])
    sv = din("sv", [4, 16, 128, 256])
    w_in = din("w_in", [4, D, DIN])
    b_in = din("b_in", [4, DIN])
    mh_gain = din("mh_gain", [4, D])
    w_a = din("w_a", [4, D, D])
    w_b = din("w_b", [4, D, D])
    w_out = din("w_out", [4, D, D])
    rel_table = din("rel_table", [32, 16])
    w_sink = din("w_sink", [4, 16])
    ln_g = din("ln_g", [4, 3, D])
    ln_b = din("ln_b", [4, 3, D])
    w_rg = din("w_rg", [4, D, 4])
    b_rg = din("b_rg", [4, 4])
    w_re = din("w_re", [4, D, 32])
    b_re = din("b_re", [4, 32])
    w_eg = din("w_eg", [4, 32, D, 512])
    w_eu = din("w_eu", [4, 32, D, 512])
    w_ed = din("w_ed", [4, 32, 512, D])
    w_pg = din("w_pg", [4, D, D])
    w_pp = din("w_pp", [4, 256, D])
    c_ident = din("c_ident", [128, 128])
    c_maskbig = din("c_maskbig", [128, 128])
    c_ohd = din("c_ohd", [32, FW])
    c_maskvec = din("c_maskvec", [1, FW])
    c_sel4 = din("c_sel4", [4, 512])

    o_y = dout("o_y", [TALL, D])
    o_Cp = dout("o_Cp", [4, 4, 128, 256])
    o_np = dout("o_np", [4, 4, 128])
    o_mp = dout("o_mp", [4, 4])
    o_kp = dout("o_kp", [4, 128, 256])
    o_vp = dout("o_vp", [4, 128, 256])
    o_Cs = dout("o_Cs", [4, 16, 4, 128, 256])
    o_ns = dout("o_ns", [4, 16, 4, 128])
    o_ms = dout("o_ms", [4, 64])
    o_ks = dout("o_ks", [4, 16, 128, 256])
    o_vs = dout("o_vs", [4, 16, 128, 256])
    scratch = nc.dram_tensor("scratch", [16, 128, FW], F32, kind="Internal").ap()

    def mm(out, lhsT, rhs, start, stop, r, w):
        P.op('pe', lambda e: e.matmul(out, lhsT=lhsT, rhs=rhs, start=start, stop=stop), r=r, w=w)

    def tr(out, in_, ident, r, w):
        P.op('pe', lambda e: e.transpose(out=out, in_=in_, identity=ident), r=r, w=w)

    def act(out, in_, func, r, w, bias=None, scale=1.0):
        if bias is None:
            P.op('act', lambda e: e.activation(out=out, in_=in_, func=func, scale=scale), r=r, w=w)
        else:
            P.op('act', lambda e: e.activation(out=out, in_=in_, func=func, bias=bias, scale=scale), r=r, w=w)

    def tt(eng, out, in0, in1, op, r, w):
        P.op(eng, lambda e: e.tensor_tensor(out=out, in0=in0, in1=in1, op=op), r=r, w=w)

    def ts(eng, out, in0, s1, op0, r, w, s2=None, op1=None):
        if op1 is None:
            P.op(eng, lambda e: e.tensor_scalar(out=out, in0=in0, scalar1=s1, scalar2=None, op0=op0), r=r, w=w)
        else:
            P.op(eng, lambda e: e.tensor_scalar(out=out, in0=in0, scalar1=s1, scalar2=s2, op0=op0, op1=op1), r=r, w=w)

    def stt(out, in0, scalar, in1, op0, op1, r, w, accum=None):
        if accum is None:
            P.op('dve', lambda e: e.scalar_tensor_tensor(out=out, in0=in0, scalar=scalar, in1=in1, op0=op0, op1=op1), r=r, w=w)
        else:
            P.op('dve', lambda e: e.scalar_tensor_tensor(out=out, in0=in0, scalar=scalar, in1=in1, op0=op0, op1=op1, accum_out=accum), r=r, w=w)

    def cp(eng, out, in_, r, w):
        if eng == 'act':
            P.op('act', lambda e: e.copy(out=out, in_=in_), r=r, w=w)
        else:
            P.op(eng, lambda e: e.tensor_copy(out=out, in_=in_), r=r, w=w)

    def dma(out, in_, r=(), w=(), slow=False):
        if slow:
            P.dma(lambda e: e.dma_start(out=out, in_=in_, allow_slow_non_contiguous=True), r=r, w=w)
        else:
            P.dma(lambda e: e.dma_start(out=out, in_=in_), r=r, w=w)

    def memset(eng, ap, val, w):
        P.op(eng, lambda e: e.memset(ap, val), w=w)

    class Rot:
        def __init__(self, bufs):
            self.bufs = bufs
            self.toks = [Tok() for _ in bufs]
            self.i = 0

        def next(self):
            k = self.i % len(self.bufs)
            self.i += 1
            return self.bufs[k], self.toks[k]

    banks = Rot([P.ps([128, 512], F32) for _ in range(8)])

    def psum():
        b, t = banks.next()
        return b, t

    x_tok = P.sb([128, NT, D], F32)
    t_x = [Tok() for _ in range(NT)]
    ident_f = P.sb([128, 128], F32)
    ident_b = P.sb([128, 128], BF16)
    ones_b = P.sb([128, 128], BF16)
    maskbig = P.sb([128, 128], F32)
    sel4 = P.sb([4, 4, 128], F32)
    EB = P.sb([128, 16, 2, 128], BF16)
    mhalf = P.sb([128, 1], F32)
    bcolh = P.sb([128, 16], F32)
    t_bcolh = Tok()
    t_c = Tok()
    NSTG = 3
    stg = Rot([P.sb([128, 1024], F32) for _ in range(NSTG)])
    NRING = 4
    ring = Rot([P.sb([128, 4096], BF16) for _ in range(NRING)])
    gate_full = P.sb([128, NT, 32], F32)
    t_gate = [Tok() for _ in range(NT)]
    gbt = P.sb([128, 2, D], F32)
    t_gbt = Tok()
    bcol = P.sb([128, 48], F32)
    t_bcol = Tok()
    esink = P.sb([128, 16], F32)
    t_esink = Tok()
    brt = P.sb([128, 36], F32)
    t_brt = Tok()
    small = Rot([P.sb([128, 16], F32) for _ in range(6)])
    ARENA_W = 18700
    arena = P.sb([128, ARENA_W], F32)

    class Arena:
        def __init__(self):
            self.off = 0

        def reset(self):
            self.off = 0

        def alloc(self, shape, dtype):
            n = int(np.prod(shape[1:]))
            nw = n if dtype == F32 else (n + 1) // 2
            assert self.off + nw <= ARENA_W, (self.off, nw)
            v = arena[0:shape[0], self.off:self.off + nw]
            self.off += nw
            if dtype != F32:
                v = v.bitcast(dtype)
                if n % 2:
                    v = v[:, 0:n]
            if len(shape) > 2:
                names = " ".join(f"d{i}" for i in range(1, len(shape)))
                v = v.rearrange(f"p ({names}) -> p {names}", **{f"d{i}": shape[i] for i in range(1, len(shape) - 1)})
            return v

        def rot(self, n, shape, dtype):
            return Rot([self.alloc(shape, dtype) for _ in range(n)])

    AR = Arena()

    def wblock(srcs):
        slot, tok = ring.next()
        views = []
        off = 0
        for s in srcs:
            shp = list(s.shape)
            n = int(np.prod(shp[1:]))
            assert n <= 1024
            st, stok = stg.next()
            sv_ = st[0:shp[0], 0:n]
            dv = slot[0:shp[0], off:off + n]
            if len(shp) == 3:
                sv_ = sv_.rearrange("p (a b) -> p a b", a=shp[1])
                dv = dv.rearrange("p (a b) -> p a b", a=shp[1])
            dma(sv_, s, w=[stok])
            cast(dv, sv_, r=[stok], w=[tok])
            views.append(dv)
            off += n
        return views, tok

    cast_state = {'i': 0, 'pat': ['act', 'dve', 'act', 'dve', 'pool']}

    def cast(dst, src, r, w):
        pat = cast_state['pat']
        eng = pat[cast_state['i'] % len(pat)]
        cast_state['i'] += 1
        cp(eng, dst, src, r=r, w=w)

    def wcols(w2d, c0, ncols, kcs=8):
        v = w2d[:, c0:c0 + ncols].rearrange("(kc p) n -> p kc n", p=128)
        per = max(1, 1024 // ncols)
        return [v[:, k0:min(k0 + per, kcs), :] for k0 in range(0, kcs, per)]

    def join_views(views):
        return views

    def wmat(w2d, c0, ncols, kcs=8):
        slot, tok = ring.next()
        v = w2d[:, c0:c0 + ncols].rearrange("(kc p) n -> p kc n", p=128)
        per = max(1, 1024 // ncols)
        full = slot[:, 0:kcs * ncols].rearrange("p (a b) -> p a b", a=kcs)
        for k0 in range(0, kcs, per):
            k1 = min(k0 + per, kcs)
            st, stok = stg.next()
            sv_ = st[:, 0:(k1 - k0) * ncols].rearrange("p (a b) -> p a b", a=k1 - k0)
            dma(sv_, v[:, k0:k1, :], w=[stok])
            cast(full[:, k0:k1, :], sv_, r=[stok], w=[tok])
        return full, tok

    def wmulti(parts):
        slot, tok = ring.next()
        tot = sum(p[2] for p in parts)
        assert 8 * tot <= 4096
        full = slot[:, 0:8 * tot].rearrange("p (a b) -> p a b", a=8)
        o = 0
        for (w2d, c0, ncols) in parts:
            v = w2d[:, c0:c0 + ncols].rearrange("(kc p) n -> p kc n", p=128)
            per = max(1, 1024 // ncols)
            for k0 in range(0, 8, per):
                k1 = min(k0 + per, 8)
                st, stok = stg.next()
                sv_ = st[:, 0:(k1 - k0) * ncols].rearrange("p (a b) -> p a b", a=k1 - k0)
                dma(sv_, v[:, k0:k1, :], w=[stok])
                cast(full[:, k0:k1, o:o + ncols], sv_, r=[stok], w=[tok])
            o += ncols
        return full, tok

    PD = 2

    class WStream:
        def __init__(self, thunks):
            self.thunks = thunks
            self.res = []
            self.consumed = 0

        def next(self):
            i = self.consumed
            self.consumed += 1
            lim = min(len(self.thunks), i + 1 + PD)
            while len(self.res) < lim:
                self.res.append(self.thunks[len(self.res)]())
            return self.res[i]

    dma(ident_f[:], c_ident, w=[t_c])
    dma(maskbig[:], c_maskbig, w=[t_c])
    dma(sel4[:].rearrange("p a b -> p (a b)"), c_sel4, w=[t_c])
    cp('dve', ident_b[:], ident_f[:], r=[t_c], w=[t_c])
    memset('dve', ones_b[:], 1.0, w=[t_c])
    memset('dve', mhalf[:], -0.5, w=[t_c])
    AR.reset()
    rt = AR.alloc([32, 16], F32)
    ohd = AR.alloc([32, FW], F32)
    mvec = AR.alloc([1, FW], F32)
    one1 = AR.alloc([1, 16], F32)
    fsb = AR.alloc([16, FW], F32)
    t_tmp = Tok()
    dma(rt, rel_table, w=[t_tmp])
    dma(ohd, c_ohd, w=[t_tmp])
    dma(mvec, c_maskvec, w=[t_tmp])
    memset('dve', one1, 1.0, w=[t_tmp])
    pb, pt_ = psum()
    mm(pb[0:16, 0:FW], rt, ohd, True, False, r=[t_tmp], w=[pt_])
    mm(pb[0:16, 0:FW], one1, mvec, False, True, r=[t_tmp], w=[pt_])
    t_f = Tok()
    cp('dve', fsb, pb[0:16, 0:FW], r=[pt_], w=[t_f])
    t_scr = Tok()
    dma(scratch, fsb.unsqueeze(1).to_broadcast([16, 128, FW]), r=[t_f], w=[t_scr])
    btmp = AR.rot(3, [128, 128], F32)
    for h in range(16):
        for kind, cc in ((0, 144), (1, 272)):
            base = scratch[h, 0, cc:cc + 128]
            src = bass.AP(tensor=base.tensor, offset=base.offset, ap=[[FW - 1, 128], [1, 128]])
            bt_, btk = btmp.next()
            dma(bt_, src, r=[t_scr], w=[btk])
            act(EB[:, h, kind, :], bt_, AF.Exp, r=[btk], w=[t_c])
    t_c.const = True
    P.barrier()

    dma(x_tok[:, 0:8, :], xin[0:1024, :].rearrange("(j p) d -> p j d", p=128), w=t_x[0:8])
    dma(x_tok[:, 8:16, :], xin[1024:2048, :].rearrange("(j p) d -> p j d", p=128), w=t_x[8:16])
    dma(x_tok[0:64, 16, :], xin[2048:2112, :], w=[t_x[16]])

    def tile_rows(j):
        return 64 if j == 16 else 128

    def transpose_tiles(dst, t_dst, tiles, col0):
        k = 0
        for j in tiles:
            n = tile_rows(j)
            c = col0 + (j - tiles[0]) * 128
            for half in range(2):
                pb, ptk = psum()
                for q in range(4):
                    kc = half * 4 + q
                    tr(pb[:, q * 128:q * 128 + n], x_tok[0:n, j, kc * 128:(kc + 1) * 128], ident_f[0:n, 0:n], r=[t_x[j], t_c], w=[ptk])
                src = pb[:, :].rearrange("p (a b) -> p a b", a=4)[:, :, 0:n]
                cp('act' if k % 2 == 0 else 'dve', dst[:, half * 4:half * 4 + 4, c:c + n], src, r=[ptk], w=[t_dst])
                k += 1

    def layer_norm(j, l, idx, t_g):
        n = tile_rows(j)
        xs = x_tok[0:n, j, :]
        s_, st_ = small.next()
        P.op('dve', lambda e: e.bn_stats(out=s_[0:n, 0:6], in_=x_tok[0:n, j, 0:512]), r=[t_x[j]], w=[st_])
        P.op('dve', lambda e: e.bn_stats(out=s_[0:n, 6:12], in_=x_tok[0:n, j, 512:1024]), r=[t_x[j]], w=[st_])
        P.op('dve', lambda e: e.bn_aggr(out=s_[0:n, 12:14], in_=s_[0:n, 0:12]), r=[st_], w=[st_])
        ts('dve', s_[0:n, 14:15], s_[0:n, 13:14], EPS, ALU.add, r=[st_], w=[st_])
        tt('pool', s_[0:n, 15:16], s_[0:n, 14:15], mhalf[0:n, 0:1], ALU.pow, r=[st_, t_c], w=[st_])
        ts('dve', s_[0:n, 14:15], s_[0:n, 12:13], s_[0:n, 15:16], ALU.mult, r=[st_], w=[st_], s2=-1.0, op1=ALU.mult)
        act(xs, xs, AF.Identity, r=[t_x[j], st_], w=[t_x[j]], bias=s_[0:n, 14:15], scale=s_[0:n, 15:16])
        tt('dve', xs, xs, gbt[0:n, 0, :], ALU.mult, r=[t_x[j], t_g], w=[t_x[j]])
        tt('dve', xs, xs, gbt[0:n, 1, :], ALU.add, r=[t_x[j], t_g], w=[t_x[j]])

    def load_gb(l, idx):
        dma(gbt[:, 0, :], ln_g[l, idx:idx + 1, :].to_broadcast([128, D]), w=[t_gbt])
        dma(gbt[:, 1, :], ln_b[l, idx:idx + 1, :].to_broadcast([128, D]), w=[t_gbt])

    SL_MQ, SL_MK, SL_IG, SL_FG, SL_SQ, SL_SK, SL_GA, SL_GB = 0, 4, 8, 9, 10, 26, 30, 38

    def load_bcol(l):
        def col(slot, c0, n):
            dma(bcol[0:n, slot:slot + 1], b_in[l, c0:c0 + n].unsqueeze(1), w=[t_bcol])
        for h in range(4):
            col(SL_MQ + h, O_MQ + h * 128, 128)
            col(SL_MK + h, O_MK + h * 128, 128)
        col(SL_IG, O_IG, 4)
        col(SL_FG, O_FG, 4)
        for hh in range(16):
            col(SL_SQ + hh, O_SQ + hh * 64, 64)
        for g in range(4):
            col(SL_SK + g, O_SK + g * 64, 64)
        for m in range(8):
            col(SL_GA + m, O_GA + m * 128, 128)
            col(SL_GB + m, O_GB + m * 128, 128)

    for l in range(depth):
        AR.reset()
        load_bcol(l)
        ts('dve', bcolh[:, :], bcol[:, SL_GA:SL_GA + 16], 0.5, ALU.mult, r=[t_bcol], w=[t_bcolh])
        dma(esink[:], w_sink[l:l + 1, :].to_broadcast([128, 16]), w=[t_esink])
        act(esink[:], esink[:], AF.Exp, r=[t_esink], w=[t_esink])
        gain_r = AR.rot(2, [128, 256], F32)
        xT_c = AR.alloc([128, 8, CH], BF16)
        t_xT = Tok()
        yaT = AR.alloc([128, 8, CH], BF16)
        t_yaT = Tok()
        ybT = AR.alloc([128, 8, CH], BF16)
        t_ybT = Tok()
        merged = AR.alloc([128, 8, CH], BF16)
        t_mg = Tok()
        fgr = AR.alloc([4, CH], F32)
        Brow = AR.alloc([4, CH], F32)
        Urow = AR.alloc([4, CH], F32)
        Arow = AR.alloc([4, CH], F32)
        onec = AR.alloc([4, 2], F32)
        t_rows = Tok()
        memset('dve', onec, 1.0, w=[t_rows])
        ones_row = onec[:, 0:1].to_broadcast([4, CH])
        carry = AR.alloc([4, 2], F32)
        t_carry = Tok()
        cols = AR.alloc([128, TPC, 8], F32)
        t_cols = Tok()
        cols_s = AR.alloc([4, 16, 8], F32)
        m0row = AR.alloc([4, 16], F32)
        m0rep = AR.alloc([128, 64], F32)
        t_m0 = Tok()
        msrow = AR.alloc([4, 16], F32)
        acs_c = AR.alloc([128, 4], F32)
        t_acs = Tok()
        memset('dve', acs_c, 0.0, w=[t_acs])
        ArepU = AR.rot(1, [128, CH], F32)
        qT_r = AR.rot(1, [128, CH], BF16)
        kT_r = AR.rot(1, [128, CH], BF16)
        bt_r = AR.rot(1, [128, 512], F32)
        hb_r = AR.rot(2, [128, 512], F32)
        vaug_r = AR.rot(2, [128, 257], BF16)
        for vb_ in vaug_r.bufs:
            memset('pool', vb_[:, 256:257], 1.0, w=[Tok()])
        ogt_r = AR.rot(2, [128, 256], F32)
        AM_r = AR.rot(2, [128, 128], F32)
        WT_r = AR.rot(2, [128, 128], F32)
        Wi_r = AR.rot(2, [128, 128], F32)
        STw_r = AR.rot(2, [128, 128], BF16)
        qw_r = AR.rot(2, [128, 128], BF16)
        kw_r = AR.rot(2, [128, 128], BF16)
        ya_r = AR.rot(2, [128, 256], F32)
        C_f = AR.alloc([128, 4, 257], F32)
        C_b = AR.alloc([128, 4, 257], BF16)
        t_C = [Tok() for _ in range(4)]
        for h in range(4):
            memset('pool', C_f[:, h, :], 0.0, w=[t_C[h]])
            memset('pool', C_b[:, h, :], 0.0, w=[t_C[h]])
        Cs_f = AR.rot(2, [128, 257], F32)
        Cs_b = AR.rot(2, [128, 257], BF16)
        Cs_o = AR.rot(2, [128, 257], F32)
        qTg = AR.alloc([64, 4, CH], BF16)
        t_qTg = Tok()
        kTg = AR.alloc([64, 4, CH + 128], BF16)
        t_kTg = [Tok() for _ in range(4)]
        vtd = AR.alloc([128, TPC + 1, 512], BF16)
        t_vtd = [Tok() for _ in range(TPC + 1)]
        T512 = AR.rot(4, [128, 512], F32)
        kvo_r = E_r = rd_r = sg_r = T512
        PT_r = AR.rot(3, [128, 512], BF16)
        kbuf_r = AR.rot(2, [128, 64], F32)
        vbuf_r = AR.rot(2, [128, 64], F32)
        kbT_r = AR.rot(2, [64, 1, 128], BF16)
        vbd_r = AR.rot(2, [128, 256], BF16)

        def mlstm_core(n, h, kTv, qTv, ArU, Ucol, emcol, Acs, Ace, vaug, gs, t_in, Cb_ap, Cf_ap, Cout_ap, t_Cst, t_Cout, yaT_dst):
            pS, tS = psum()
            mm(pS[0:n, 0:n], kTv, qTv, True, True, r=t_in, w=[tS])
            AM, tAM = AM_r.next()
            tt('dve', AM[0:n, 0:n], ArU[0:n, :], maskbig[0:n, 0:n], ALU.add, r=t_in + [t_c], w=[tAM])
            WT, tWT = WT_r.next()
            act(WT[0:n, 0:n], AM[0:n, 0:n], AF.Exp, r=[tAM] + t_in, w=[tWT], bias=Ucol, scale=-1.0)
            STw, tST = STw_r.next()
            tt('dve', STw[0:n, 0:n], pS[0:n, 0:n], WT[0:n, 0:n], ALU.mult, r=[tS, tWT], w=[tST])
            Wi, tWi = Wi_r.next()
            act(Wi[:, 0:n], ArU, AF.Exp, r=t_in, w=[tWi], bias=Acs, scale=-1.0)
            qw, tqw = qw_r.next()
            tt('dve', qw[:, 0:n], qTv, Wi[:, 0:n], ALU.mult, r=t_in + [tWi], w=[tqw])
            pN, tN = psum()
            mm(pN[0:n, 0:257], STw[0:n, 0:n], vaug, True, False, r=[tST] + t_in, w=[tN])
            mm(pN[0:n, 0:257], qw[:, 0:n], Cb_ap, False, True, r=[tqw, t_Cst], w=[tN])
            s_, st_ = small.next()
            act(s_[0:n, 0:1], pN[0:n, 256:257], AF.Abs, r=[tN], w=[st_])
            ts('dve', s_[0:n, 1:2], s_[0:n, 0:1], emcol, ALU.max, r=[st_] + t_in, w=[st_])
            P.op('dve', lambda e: e.bn_stats(out=s_[0:n, 2:8], in_=pN[0:n, 0:256]), r=[tN], w=[st_])
            P.op('dve', lambda e: e.bn_aggr(out=s_[0:n, 8:10], in_=s_[0:n, 2:8]), r=[st_], w=[st_])
            ts('dve', s_[0:n, 10:11], s_[0:n, 1:2], s_[0:n, 1:2], ALU.mult, r=[st_], w=[st_], s2=EPS, op1=ALU.mult)
            tt('dve', s_[0:n, 10:11], s_[0:n, 10:11], s_[0:n, 9:10], ALU.add, r=[st_], w=[st_])
            tt('pool', s_[0:n, 11:12], s_[0:n, 10:11], mhalf[0:n, 0:1], ALU.pow, r=[st_, t_c], w=[st_])
            ya, tya = ya_r.next()
            ts('dve', ya[0:n, :], pN[0:n, 0:256], s_[0:n, 8:9], ALU.subtract, r=[tN, st_], w=[tya], s2=s_[0:n, 11:12], op1=ALU.mult)
            tt('dve', ya[0:n, :], ya[0:n, :], gs, ALU.mult, r=[tya] + t_in, w=[tya])
            pY, tY = psum()
            for vc in range(2):
                tr(pY[:, vc * 128:vc * 128 + n], ya[0:n, vc * 128:(vc + 1) * 128], ident_f[0:n, 0:n], r=[tya, t_c], w=[tY])
            cp('act', yaT_dst, pY[:, 0:256].rearrange("p (a b) -> p a b", a=2)[:, :, 0:n], r=[tY], w=[t_yaT])
            s2_, st2 = small.next()
            ts('dve', s2_[0:n, 0:1], Ucol, Ace[0:n, :], ALU.subtract, r=t_in, w=[st2])
            act(s2_[0:n, 0:1], s2_[0:n, 0:1], AF.Exp, r=[st2], w=[st2])
            tt('dve', s2_[:, 1:2], Acs, Ace, ALU.subtract, r=t_in, w=[st2])
            act(s2_[:, 1:2], s2_[:, 1:2], AF.Exp, r=[st2], w=[st2])
            pK, tK = psum()
            mm(pK[0:n, 0:128], kTv, ident_b[:], True, True, r=t_in + [t_c], w=[tK])
            kw, tkw = kw_r.next()
            ts('dve', kw[0:n, :], pK[0:n, 0:128], s2_[0:n, 0:1], ALU.mult, r=[tK, st2], w=[tkw])
            pU, tU = psum()
            mm(pU[:, 0:257], kw[0:n, :], vaug, True, True, r=[tkw] + t_in, w=[tU])
            stt(Cout_ap, Cf_ap, s2_[:, 1:2], pU[:, 0:257], ALU.mult, ALU.add, r=[t_Cst, st2, tU], w=[t_Cout])

        NPC = TP // CH
        chunks = [(c, CH) for c in range(NPC)] + [(NPC, 64)]
        th = []
        for (c_, N_) in chunks:
            s_chunk = (c_ == NPC)
            th.append(lambda l=l: wblock([w_in[l][:, O_IG:O_IG + 8].rearrange("(kc p) n -> p kc n", p=128)]))
            for h_ in range(4):
                th.append(lambda l=l, h_=h_: wmulti([(w_in[l], O_MQ + h_ * 128, 128), (w_in[l], O_MK + h_ * 128, 128)]))
                th.append(lambda l=l, h_=h_: wmulti([(w_in[l], O_MV + h_ * 256, 256), (w_in[l], O_OG + h_ * 256, 256)]))
            if not s_chunk:
                th.append(lambda l=l: wmulti([(w_in[l], O_SK, 256), (w_in[l], O_SV, 256)]))
            for g_ in range(4):
                th.append(lambda l=l, g_=g_: wmulti([(w_in[l], O_SQ + g_ * 256, 256), (w_in[l], O_SK + g_ * 64, 64)]))
                if s_chunk:
                    th.append(lambda l=l: wmulti([(w_in[l], O_SK, 256), (w_in[l], O_SV, 256)]))
            for m_ in range(8):
                th.append(lambda l=l, m_=m_: wmulti([(w_a[l], m_ * 128, 128), (w_b[l], m_ * 128, 128), (w_in[l], O_GA + m_ * 128, 128), (w_in[l], O_GB + m_ * 128, 128)]))
            for half_ in range(2):
                th.append(lambda l=l, half_=half_: wmat(w_out[l], half_ * 512, 512))
        WS = WStream(th)
        for (c, N) in chunks:
            is_s = (c == NPC)
            tiles = [16] if is_s else list(range(TPC * c, TPC * c + TPC))
            transpose_tiles(xT_c, t_xT, tiles, 0)
            wg_v, wg_t = WS.next()
            wg = wg_v[0]
            pI, tI = psum()
            pF, tF = psum()
            for kc in range(8):
                mm(pI[0:4, 0:N], wg[:, kc, 0:4], xT_c[:, kc, 0:N], kc == 0, kc == 7, r=[wg_t, t_xT], w=[tI])
            for kc in range(8):
                mm(pF[0:4, 0:N], wg[:, kc, 4:8], xT_c[:, kc, 0:N], kc == 0, kc == 7, r=[wg_t, t_xT], w=[tF])
            act(Urow[:, 0:N], pI[0:4, 0:N], AF.Identity, r=[tI, t_bcol], w=[t_rows], bias=bcol[0:4, SL_IG:SL_IG + 1])
            act(fgr[:, 0:N], pF[0:4, 0:N], AF.Identity, r=[tF, t_bcol], w=[t_rows], bias=bcol[0:4, SL_FG:SL_FG + 1])
            act(fgr[:, 0:N], fgr[:, 0:N], AF.Exp, r=[t_rows], w=[t_rows], scale=-1.0)
            act(fgr[:, 0:N], fgr[:, 0:N], AF.Ln, r=[t_rows], w=[t_rows], bias=1.0)
            if not is_s:
                if c == 0:
                    P.op('dve', lambda e: e.tensor_tensor_scan(out=Brow[:, :], data0=ones_row[:, :], data1=fgr[:, :], initial=0.0, op0=ALU.mult, op1=ALU.subtract), r=[t_rows], w=[t_rows])
                else:
                    P.op('dve', lambda e: e.tensor_tensor_scan(out=Brow[:, :], data0=ones_row[:, :], data1=fgr[:, :], initial=carry[:, 0:1], op0=ALU.mult, op1=ALU.subtract), r=[t_rows, t_carry], w=[t_rows])
                tt('dve', Urow[:, :], Urow[:, :], Brow[:, :], ALU.subtract, r=[t_rows], w=[t_rows])
                if c == 0:
                    P.op('dve', lambda e: e.tensor_tensor_scan(out=Arow[:, :], data0=Urow[:, :], data1=Urow[:, :], initial=0.0, op0=ALU.max, op1=ALU.max), r=[t_rows], w=[t_rows])
                else:
                    P.op('dve', lambda e: e.tensor_tensor_scan(out=Arow[:, :], data0=Urow[:, :], data1=Urow[:, :], initial=carry[:, 1:2], op0=ALU.max, op1=ALU.max), r=[t_rows, t_carry], w=[t_rows])
                cp('dve', carry[:, 0:1], Brow[:, CH - 1:CH], r=[t_rows], w=[t_carry])
                cp('dve', carry[:, 1:2], Arow[:, CH - 1:CH], r=[t_rows], w=[t_carry])
            else:
                dma(m0row, sm[l].rearrange("(s h) -> h s", h=4), w=[t_m0], slow=True)
                dma(m0rep, sm[l:l + 1, :].to_broadcast([128, 64]), w=[t_m0])
                f3 = fgr[:, 0:64].rearrange("p (s t) -> p s t", t=4)
                B3 = Brow[:, 0:64].rearrange("p (s t) -> p s t", t=4)
                U3 = Urow[:, 0:64].rearrange("p (s t) -> p s t", t=4)
                A3 = Arow[:, 0:64].rearrange("p (s t) -> p s t", t=4)
                ts('dve', B3[:, :, 0], f3[:, :, 0], -1.0, ALU.mult, r=[t_rows], w=[t_rows])
                for t in range(1, 4):
                    tt('dve', B3[:, :, t], B3[:, :, t - 1], f3[:, :, t], ALU.subtract, r=[t_rows], w=[t_rows])
                tt('dve', Urow[:, 0:64], Urow[:, 0:64], Brow[:, 0:64], ALU.subtract, r=[t_rows], w=[t_rows])
                tt('dve', A3[:, :, 0], U3[:, :, 0], m0row[:, :], ALU.max, r=[t_rows, t_m0], w=[t_rows])
                for t in range(1, 4):
                    tt('dve', A3[:, :, t], A3[:, :, t - 1], U3[:, :, t], ALU.max, r=[t_rows], w=[t_rows])
                tt('dve', msrow[:, :], A3[:, :, 3], B3[:, :, 3], ALU.add, r=[t_rows], w=[t_m0])
                dma(o_ms[l].rearrange("(s h) -> h s", h=4), msrow[:, :], r=[t_m0], slow=True)
            stt(fgr[:, 0:N], Arow[:, 0:N], -1.0, Brow[:, 0:N], ALU.mult, ALU.subtract, r=[t_rows], w=[t_rows])
            pC, tC = psum()
            if not is_s:
                for jj in range(TPC):
                    tr(pC[:, jj * 8:jj * 8 + 4], Urow[:, jj * 128:(jj + 1) * 128], ident_f[0:4, 0:4], r=[t_rows, t_c], w=[tC])
                    tr(pC[:, jj * 8 + 4:jj * 8 + 8], fgr[:, jj * 128:(jj + 1) * 128], ident_f[0:4, 0:4], r=[t_rows, t_c], w=[tC])
                cp('dve', cols[:, :, 0:4], pC[:, 0:8 * TPC].rearrange("p (a b) -> p a b", a=TPC)[:, :, 0:4], r=[tC], w=[t_cols])
                act(cols[:, :, 4:8], pC[:, 0:8 * TPC].rearrange("p (a b) -> p a b", a=TPC)[:, :, 4:8], AF.Exp, r=[tC], w=[t_cols])
            else:
                for s in range(16):
                    tr(pC[0:4, s * 8:s * 8 + 4], Urow[:, s * 4:(s + 1) * 4], ident_f[0:4, 0:4], r=[t_rows, t_c], w=[tC])
                    tr(pC[0:4, s * 8 + 4:s * 8 + 8], fgr[:, s * 4:(s + 1) * 4], ident_f[0:4, 0:4], r=[t_rows, t_c], w=[tC])
                cp('dve', cols_s[:, :, 0:4], pC[0:4, 0:128].rearrange("p (a b) -> p a b", a=16)[:, :, 0:4], r=[tC], w=[t_cols])
                act(cols_s[:, :, 4:8], pC[0:4, 0:128].rearrange("p (a b) -> p a b", a=16)[:, :, 4:8], AF.Exp, r=[tC], w=[t_cols])

            for h in range(4):
                ArU, tAr = ArepU.next()
                pA, tA = psum()
                mm(pA[:, 0:N], sel4[:, h, :], Arow[:, 0:N], True, True, r=[t_c, t_rows], w=[tA])
                cp('act', ArU[:, 0:N], pA[:, 0:N], r=[tA], w=[tAr])
                wA, wA_t = WS.next()
                wB, wB_t = WS.next()
                qT, tq = qT_r.next()
                kT, tk = kT_r.next()
                pq, tpq = psum()
                for kc in range(8):
                    mm(pq[:, 0:N], wA[:, kc, 0:128], xT_c[:, kc, 0:N], kc == 0, kc == 7, r=[wA_t, t_xT], w=[tpq])
                ts('dve', qT[:, 0:N], pq[:, 0:N], bcol[:, SL_MQ + h:SL_MQ + h + 1], ALU.add, r=[tpq, t_bcol], w=[tq], s2=float(128 ** -0.5), op1=ALU.mult)
                pk, tpk = psum()
                for kc in range(8):
                    mm(pk[:, 0:N], wA[:, kc, 128:256], xT_c[:, kc, 0:N], kc == 0, kc == 7, r=[wA_t, t_xT], w=[tpk])
                act(kT[:, 0:N], pk[:, 0:N], AF.Identity, r=[tpk, t_bcol], w=[tk], bias=bcol[:, SL_MK + h:SL_MK + h + 1])
                def issue_hb(h_):
                    g_, tg_ = gain_r.next()
                    dma(g_[:, :], mh_gain[l:l + 1, h_ * 256:(h_ + 1) * 256].to_broadcast([128, 256]), w=[tg_])
                    b_, tb_ = hb_r.next()
                    dma(b_[:, 0:256], b_in[l:l + 1, O_MV + h_ * 256:O_MV + (h_ + 1) * 256].to_broadcast([128, 256]), w=[tb_])
                    dma(b_[:, 256:512], b_in[l:l + 1, O_OG + h_ * 256:O_OG + (h_ + 1) * 256].to_broadcast([128, 256]), w=[tb_])
                    return g_, tg_, b_, tb_
                if h == 0:
                    nxt_hb = issue_hb(0)
                gain, t_gain, bt_, tbt = nxt_hb
                if h + 1 < 4:
                    nxt_hb = issue_hb(h + 1)
                units = [(jj, 128, jj * 128) for jj in range(TPC)] if not is_s else [(s, 4, s * 4) for s in range(16)]
                def issue_c0(u_, h=h):
                    cf_, tcf_ = Cs_f.next()
                    cb_, tcb_ = Cs_b.next()
                    dma(cf_[:, 0:256], sC[l, u_, h], w=[tcf_])
                    dma(cf_[:, 256:257], sn[l, u_, h, :].unsqueeze(1), w=[tcf_])
                    cp('pool', cb_[:, :], cf_[:, :], r=[tcf_], w=[tcb_])
                    return cf_, tcf_, cb_, tcb_
                nxt_c0 = issue_c0(0) if is_s else None
                for (u, n, c0) in units:
                    cs = slice(c0, c0 + n)
                    if is_s:
                        cur_c0 = nxt_c0
                        nxt_c0 = issue_c0(u + 1) if u + 1 < 16 else None
                    pv, tpv = psum()
                    for kc in range(8):
                        mm(pv[0:n, :], xT_c[:, kc, cs], wB[:, kc, :], kc == 0, kc == 7, r=[wB_t, t_xT], w=[tpv])
                    va, tva = vaug_r.next()
                    tt('dve', va[0:n, 0:256], pv[0:n, 0:256], bt_[0:n, 0:256], ALU.add, r=[tpv, tbt], w=[tva])
                    og, tog = ogt_r.next()
                    tt('dve', og[0:n, :], pv[0:n, 256:512], bt_[0:n, 256:512], ALU.add, r=[tpv, tbt], w=[tog])
                    act(og[0:n, :], og[0:n, :], AF.Tanh, r=[tog], w=[tog], scale=0.5)
                    act(og[0:n, :], og[0:n, :], AF.Identity, r=[tog], w=[tog], bias=0.5, scale=0.5)
                    tt('dve', og[0:n, :], og[0:n, :], gain[0:n, :], ALU.mult, r=[tog, t_gain], w=[tog])
                    t_in = [tq, tk, tAr, t_cols, tva, tog, t_acs, t_m0]
                    yaT_dst = yaT[:, 2 * h:2 * h + 2, cs]
                    if not is_s:
                        Acs = acs_c[:, h:h + 1] if u == 0 else ArU[:, c0 - 1:c0]
                        Ace = ArU[:, c0 + n - 1:c0 + n]
                        mlstm_core(n, h, kT[:, cs], qT[:, cs], ArU[:, cs], cols[:, u, h:h + 1], cols[:, u, 4 + h:5 + h],
                                   Acs, Ace, va[0:n, :], og[0:n, :], t_in, C_b[:, h, :], C_f[:, h, :], C_f[:, h, :], t_C[h], t_C[h], yaT_dst)
                        cp('act', C_b[:, h, :], C_f[:, h, :], r=[t_C[h]], w=[t_C[h]])
                    else:
                        cf, tcf, cb, tcb = cur_c0
                        co, tco = Cs_o.next()
                        Acs = m0rep[:, u * 4 + h:u * 4 + h + 1]
                        Ace = ArU[:, c0 + n - 1:c0 + n]
                        mlstm_core(n, h, kT[:, cs], qT[:, cs], ArU[:, cs], cols_s[0:4, u, h:h + 1], cols_s[0:4, u, 4 + h:5 + h],
                                   Acs, Ace, va[0:n, :], og[0:n, :], t_in + [tcb], cb[:, :], cf[:, :], co[:, :], tcf, tco, yaT_dst)
                        dma(o_Cs[l, u, h], co[:, 0:256], r=[tco])
                        dma(o_ns[l, u, h, :].unsqueeze(1), co[:, 256:257], r=[tco])
                if not is_s:
                    cp('dve', acs_c[:, h:h + 1], ArU[:, CH - 1:CH], r=[tAr], w=[t_acs])
                    if c == NPC - 1:
                        dma(o_Cp[l, h], C_f[:, h, 0:256], r=[t_C[h]])
                        dma(o_np[l, h, :].unsqueeze(1), C_f[:, h, 256:257], r=[t_C[h]])
            if c == NPC - 1:
                s_, st_ = small.next()
                tt('dve', s_[0:4, 0:1], carry[:, 0:1], carry[:, 1:2], ALU.add, r=[t_carry], w=[st_])
                dma(o_mp[l, :].unsqueeze(1), s_[0:4, 0:1], r=[st_])

            if not is_s:
                wKV, wKV_t = WS.next()
            btk_, tbtk = bt_r.next()
            dma(btk_[:, :], b_in[l:l + 1, O_SK:O_SK + 512].to_broadcast([128, 512]), w=[tbtk])
            if not is_s:
                for jj in range(TPC):
                    j = TPC * c + jj
                    pkv, tpkv = psum()
                    for kc in range(8):
                        mm(pkv[:, :], xT_c[:, kc, jj * 128:(jj + 1) * 128], wKV[:, kc, :], kc == 0, kc == 7, r=[wKV_t, t_xT], w=[tpkv])
                    dst = vtd[:, jj + 1, :].rearrange("p (g u d) -> p g u d", g=4, u=2)
                    src = pkv[:, 256:512].rearrange("p (g d) -> p g d", g=4).unsqueeze(2).to_broadcast([128, 4, 2, 64])
                    bsrc = btk_[:, 256:512].rearrange("p (g d) -> p g d", g=4).unsqueeze(2).to_broadcast([128, 4, 2, 64])
                    tt('dve', dst, src, bsrc, ALU.add, r=[tpkv, tbtk], w=[t_vtd[jj + 1]])
                    if j == 15:
                        kvo, tkvo = kvo_r.next()
                        tt('dve', kvo[:, :], pkv[:, :], btk_[:, :], ALU.add, r=[tpkv, tbtk], w=[tkvo])
                        dma(o_kp[l], kvo[:, 0:256], r=[tkvo])
                        dma(o_vp[l], kvo[:, 256:512], r=[tkvo])
            else:
                dma(o_ks[l, :, 0:124, :], sk[l, :, 4:128, :])
                dma(o_vs[l, :, 0:124, :], sv[l, :, 4:128, :])
            for g in range(4):
                wS, wS_t = WS.next()
                if is_s:
                    wKV, wKV_t = WS.next()
                for i in range(4):
                    pq, tpq = psum()
                    for kc in range(8):
                        mm(pq[0:64, 0:N], wS[:, kc, i * 64:(i + 1) * 64], xT_c[:, kc, 0:N], kc == 0, kc == 7, r=[wS_t, t_xT], w=[tpq])
                    act(qTg[:, i, 0:N], pq[0:64, 0:N], AF.Identity, r=[tpq, t_bcol], w=[t_qTg], bias=bcol[0:64, SL_SQ + 4 * g + i:SL_SQ + 4 * g + i + 1])
                pk, tpk = psum()
                for kc in range(8):
                    mm(pk[0:64, 0:N], wS[:, kc, 256:320], xT_c[:, kc, 0:N], kc == 0, kc == 7, r=[wS_t, t_xT], w=[tpk])
                act(kTg[:, g, 128:128 + N], pk[0:64, 0:N], AF.Identity, r=[tpk, t_bcol], w=[t_kTg[g]], bias=bcol[0:64, SL_SK + g:SL_SK + g + 1])
                if not is_s:
                    for jj in range(TPC):
                        j = TPC * c + jj
                        cs = slice(jj * 128, (jj + 1) * 128)
                        pNm, tNm = psum()
                        pDn, tDn = psum()
                        kts = ([(1, jj)] if j > 0 else []) + [(0, jj + 1)]
                        for ki, (kind, slot) in enumerate(kts):
                            pS, tS = psum()
                            mm(pS[:, :].rearrange("p (a b) -> p a b", a=4), kTg[:, g, slot * 128:(slot + 1) * 128], qTg[:, :, cs], True, True, r=[t_kTg[g], t_qTg], w=[tS])
                            E_, tE = E_r.next()
                            act(E_[:, :], pS[:, :], AF.Exp, r=[tS], w=[tE], scale=0.125)
                            PT, tPT = PT_r.next()
                            tt('dve', PT[:, :].rearrange("p (a b) -> p a b", a=4), E_[:, :].rearrange("p (a b) -> p a b", a=4), EB[:, 4 * g:4 * g + 4, kind, :], ALU.mult, r=[tE, t_c], w=[tPT])
                            first = ki == 0
                            last = ki == len(kts) - 1
                            mm(pNm[:, :], vtd[:, slot, g * 128:(g + 1) * 128], PT[:, :], first, last, r=[t_vtd[slot], tPT], w=[tNm])
                            mm(pDn[:, :], ones_b[:, :], PT[:, :], first, last, r=[tPT, t_c], w=[tDn])
                        rd, trd = rd_r.next()
                        for i in range(4):
                            ts('dve', rd[:, i * 128:(i + 1) * 128], pDn[:, i * 128:(i + 1) * 128], esink[:, 4 * g + i:4 * g + i + 1], ALU.add, r=[tDn, t_esink], w=[trd])
                        P.op('dve', lambda e, rd=rd: e.reciprocal(out=rd[:, :], in_=rd[:, :]), r=[trd], w=[trd])
                        for par in range(2):
                            ps_ = slice(par * 64, (par + 1) * 64)
                            nv = pNm[ps_, :].rearrange("p (a b) -> p a b", a=4)[:, par::2, :]
                            rv = rd[ps_, :].rearrange("p (a b) -> p a b", a=4)[:, par::2, :]
                            tt('dve', ybT[ps_, 2 * g:2 * g + 2, cs], nv, rv, ALU.mult, r=[tNm, trd], w=[t_ybT])
                else:
                    def issue_kv(s_, g=g):
                        kb, tkb = kbuf_r.next()
                        vb, tvb = vbuf_r.next()
                        dma(kb[:, :], sk[l, s_, :, g * 64:(g + 1) * 64], w=[tkb])
                        dma(vb[:, :], sv[l, s_, :, g * 64:(g + 1) * 64], w=[tvb])
                        pT_, tT_ = psum()
                        tr(pT_[0:64, 0:128], kb[:, :], ident_f[:, :], r=[tkb, t_c], w=[tT_])
                        kbT_, tkbT_ = kbT_r.next()
                        cp('act', kbT_[:, 0, :], pT_[0:64, 0:128], r=[tT_], w=[tkbT_])
                        vbd_, tvbd_ = vbd_r.next()
                        cp('pool', vbd_[:, 0:128].rearrange("p (u d) -> p u d", u=2), vb[:, :].unsqueeze(1).to_broadcast([128, 2, 64]), r=[tvb], w=[tvbd_])
                        return kbT_, tkbT_, vbd_, tvbd_
                    nxt_kv = issue_kv(0)
                    for s in range(16):
                        cs = slice(4 * s, 4 * s + 4)
                        kbT, tkbT, vbd, tvbd = nxt_kv
                        nxt_kv = issue_kv(s + 1) if s + 1 < 16 else None
                        pkv, tpkv = psum()
                        for kc in range(8):
                            mm(pkv[0:4, :], xT_c[:, kc, cs], wKV[:, kc, :], kc == 0, kc == 7, r=[wKV_t, t_xT], w=[tpkv])
                        kvo, tkvo = kvo_r.next()
                        tt('dve', kvo[0:4, :], pkv[0:4, :], btk_[0:4, :], ALU.add, r=[tpkv, tbtk], w=[tkvo])
                        if g == 0:
                            dma(o_ks[l, s, 124:128, :], kvo[0:4, 0:256], r=[tkvo])
                            dma(o_vs[l, s, 124:128, :], kvo[0:4, 256:512], r=[tkvo])
                        cp('pool', vbd[0:4, 128:256].rearrange("p (u d) -> p u d", u=2), kvo[0:4, 256 + g * 64:256 + (g + 1) * 64].unsqueeze(1).to_broadcast([4, 2, 64]), r=[tkvo], w=[tvbd])
                        pS1, tS1 = psum()
                        mm(pS1[:, 0:16].rearrange("p (a b) -> p a b", a=4), kbT[:, 0, :], qTg[:, :, cs], True, True, r=[tkbT, t_qTg], w=[tS1])
                        pS2, tS2 = psum()
                        mm(pS2[0:4, 0:16].rearrange("p (a b) -> p a b", a=4), kTg[:, g, 128 + 4 * s:128 + 4 * s + 4], qTg[:, :, cs], True, True, r=[t_kTg[g], t_qTg], w=[tS2])
                        E_, tE = E_r.next()
                        act(E_[:, 0:16], pS1[:, 0:16], AF.Exp, r=[tS1], w=[tE], scale=0.125)
                        act(E_[0:4, 16:32], pS2[0:4, 0:16], AF.Exp, r=[tS2], w=[tE], scale=0.125)
                        PT, tPT = PT_r.next()
                        tt('dve', PT[:, 0:16].rearrange("p (a b) -> p a b", a=4), E_[:, 0:16].rearrange("p (a b) -> p a b", a=4), EB[:, 4 * g:4 * g + 4, 1, 0:4], ALU.mult, r=[tE, t_c], w=[tPT])
                        tt('dve', PT[0:4, 16:32].rearrange("p (a b) -> p a b", a=4), E_[0:4, 16:32].rearrange("p (a b) -> p a b", a=4), EB[0:4, 4 * g:4 * g + 4, 0, 0:4], ALU.mult, r=[tE, t_c], w=[tPT])
                        pNm, tNm = psum()
                        pDn, tDn = psum()
                        mm(pNm[:, 0:16], vbd[:, 0:128], PT[:, 0:16], True, False, r=[tvbd, tPT], w=[tNm])
                        mm(pNm[:, 0:16], vbd[0:4, 128:256], PT[0:4, 16:32], False, True, r=[tvbd, tPT], w=[tNm])
                        mm(pDn[:, 0:16], ones_b[:, :], PT[:, 0:16], True, False, r=[tPT, t_c], w=[tDn])
                        mm(pDn[:, 0:16], ones_b[0:4, :], PT[0:4, 16:32], False, True, r=[tPT, t_c], w=[tDn])
                        rd, trd = rd_r.next()
                        for i in range(4):
                            ts('dve', rd[:, i * 4:(i + 1) * 4], pDn[:, i * 4:(i + 1) * 4], esink[:, 4 * g + i:4 * g + i + 1], ALU.add, r=[tDn, t_esink], w=[trd])
                        P.op('dve', lambda e, rd=rd: e.reciprocal(out=rd[:, 0:16], in_=rd[:, 0:16]), r=[trd], w=[trd])
                        for par in range(2):
                            ps_ = slice(par * 64, (par + 1) * 64)
                            nv = pNm[ps_, 0:16].rearrange("p (a b) -> p a b", a=4)[:, par::2, :]
                            rv = rd[ps_, 0:16].rearrange("p (a b) -> p a b", a=4)[:, par::2, :]
                            tt('dve', ybT[ps_, 2 * g:2 * g + 2, cs], nv, rv, ALU.mult, r=[tNm, trd], w=[t_ybT])
                if not is_s:
                    cp('pool', kTg[:, g, 0:128], kTg[:, g, CH:CH + 128], r=[t_kTg[g]], w=[t_kTg[g]])
            if not is_s:
                cp('pool', vtd[:, 0, :], vtd[:, TPC, :], r=[t_vtd[TPC]], w=[t_vtd[0]])

            for m in range(8):
                wM, wM_t = WS.next()
                res = []
                for (wi, gi, srcT, tsrc, slot) in ((0, 2, yaT, t_yaT, SL_GA + m), (1, 3, ybT, t_ybT, SL_GB + m)):
                    pg, tpg = psum()
                    for kc in range(8):
                        mm(pg[:, 0:N], wM[:, kc, gi * 128:(gi + 1) * 128], xT_c[:, kc, 0:N], kc == 0, kc == 7, r=[wM_t, t_xT], w=[tpg])
                    sg, tsg = sg_r.next()
                    act(sg[:, 0:N], pg[:, 0:N], AF.Tanh, r=[tpg, t_bcolh], w=[tsg], bias=bcolh[:, slot - SL_GA:slot - SL_GA + 1], scale=0.5)
                    act(sg[:, 0:N], sg[:, 0:N], AF.Identity, r=[tsg], w=[tsg], bias=0.5, scale=0.5)
                    pa, tpa = psum()
                    for kc in range(8):
                        mm(pa[:, 0:N], wM[:, kc, wi * 128:(wi + 1) * 128], srcT[:, kc, 0:N], kc == 0, kc == 7, r=[wM_t, tsrc], w=[tpa])
                    tt('dve', sg[:, 0:N], sg[:, 0:N], pa[:, 0:N], ALU.mult, r=[tsg, tpa], w=[tsg])
                    res.append((sg, tsg))
                tt('dve', merged[:, m, 0:N], res[0][0][:, 0:N], res[1][0][:, 0:N], ALU.add, r=[res[0][1], res[1][1]], w=[t_mg])
            if c == 0:
                load_gb(l, 0)
            for half in range(2):
                wO, wO_t = WS.next()
                for u, j in enumerate(tiles):
                    n = tile_rows(j)
                    po, tpo = psum()
                    for kc in range(8):
                        mm(po[0:n, :], merged[:, kc, u * 128:u * 128 + n], wO[:, kc, :], kc == 0, kc == 7, r=[wO_t, t_mg], w=[tpo])
                    xs = x_tok[0:n, j, half * 512:(half + 1) * 512]
                    stt(xs, xs, ALPHA, po[0:n, :], ALU.mult, ALU.add, r=[t_x[j], tpo], w=[t_x[j]])
            for j in tiles:
                layer_norm(j, l, 0, t_gbt)

        P.barrier()
        AR.reset()
        xT_all = AR.alloc([128, 8, TALL], BF16)
        t_xTa = Tok()
        HT = AR.alloc([128, 4, TALL], BF16)
        t_HT = [[Tok() for _ in range(4)] for _ in range(5)]
        sgm_r = AR.rot(2, [128, 512], F32)
        lg_r = AR.rot(2, [128, 80], F32)
        transpose_tiles(xT_all, t_xTa, list(range(NT)), 0)
        load_gb(l, 1)
        dma(brt[:, 0:4], b_rg[l:l + 1, :].to_broadcast([128, 4]), w=[t_brt])
        dma(brt[:, 4:36], b_re[l:l + 1, :].to_broadcast([128, 32]), w=[t_brt])
        th = [lambda l=l: wmulti([(w_rg[l], 0, 4), (w_re[l], 0, 32)])]
        for ex_ in range(32):
            th.append(lambda l=l, ex_=ex_: wmat(w_eg[l, ex_], 0, 512))
            th.append(lambda l=l, ex_=ex_: wmat(w_eu[l, ex_], 0, 512))
            th.append(lambda l=l, ex_=ex_: wmat(w_ed[l, ex_], 0, 1024, kcs=4))
        WS = WStream(th)
        wR, wR_t = WS.next()
        for j in range(NT):
            n = tile_rows(j)
            c0 = j * 128
            pr, tpr = psum()
            for kc in range(8):
                mm(pr[0:n, 0:36], xT_all[:, kc, c0:c0 + n], wR[:, kc, :], kc == 0, kc == 7, r=[wR_t, t_xTa], w=[tpr])
            lg, tlg = lg_r.next()
            L = lg[0:n, :]
            tt('dve', L[:, 0:36], pr[0:n, 0:36], brt[0:n, :], ALU.add, r=[tpr, t_brt], w=[tlg])
            s_, st_ = small.next()
            S = s_[0:n, :]
            P.op('dve', lambda e, S=S, L=L: e.reduce_max(out=S[:, 0:1], in_=L[:, 0:4], axis=mybir.AxisListType.X), r=[tlg], w=[st_])
            ts('dve', S[:, 1:2], S[:, 0:1], -1.0, ALU.mult, r=[st_], w=[st_])
            P.op('act', lambda e, S=S, L=L: e.activation(out=L[:, 36:40], in_=L[:, 0:4], func=AF.Exp, bias=S[:, 1:2], scale=1.0, accum_out=S[:, 2:3]), r=[tlg, st_], w=[tlg, st_])
            ts('dve', L[:, 40:44], L[:, 0:4], S[:, 0:1], ALU.is_equal, r=[tlg, st_], w=[tlg])
            ts('dve', L[:, 40:44], L[:, 40:44], BIG, ALU.mult, r=[tlg], w=[tlg], s2=-BIG, op1=ALU.add)
            tt('dve', L[:, 44:76].rearrange("p (g i) -> p g i", g=4), L[:, 4:36].rearrange("p (g i) -> p g i", g=4), L[:, 40:44].unsqueeze(2).to_broadcast([n, 4, 8]), ALU.add, r=[tlg], w=[tlg])
            P.op('dve', lambda e, S=S, L=L: e.max(out=S[:, 8:16], in_=L[:, 44:76]), r=[tlg], w=[st_])
            ts('dve', S[:, 3:4], S[:, 8:9], -1.0, ALU.mult, r=[st_], w=[st_])
            ts('dve', L[:, 4:36], L[:, 44:76], S[:, 9:10], ALU.is_ge, r=[tlg, st_], w=[tlg])
            act(L[:, 44:76], L[:, 44:76], AF.Exp, r=[tlg, st_], w=[tlg], bias=S[:, 3:4])
            stt(L[:, 44:76], L[:, 44:76], 1.0, L[:, 4:36], ALU.mult, ALU.mult, r=[tlg], w=[tlg, st_], accum=S[:, 4:5])
            tt('dve', S[:, 5:6], S[:, 4:5], S[:, 2:3], ALU.mult, r=[st_], w=[st_])
            P.op('dve', lambda e, S=S: e.reciprocal(out=S[:, 6:7], in_=S[:, 5:6]), r=[st_], w=[st_])
            ts('dve', gate_full[0:n, j, :], L[:, 44:76], S[:, 6:7], ALU.mult, r=[tlg, st_], w=[t_gate[j]])
            P.op('act', lambda e, n=n, j=j: e.mul(out=x_tok[0:n, j, :], in_=x_tok[0:n, j, :], mul=ALPHA), r=[t_x[j], t_xTa], w=[t_x[j]])
        mchunks = [(cc * 512, 512) for cc in range(4)] + [(2048, 64)]
        for ex in range(32):
            wG, wG_t = WS.next()
            wU, wU_t = WS.next()
            for ci, (c0, N) in enumerate(mchunks):
                for fc in range(4):
                    pg, tpg = psum()
                    pu, tpu = psum()
                    for kc in range(8):
                        mm(pg[:, 0:N], wG[:, kc, fc * 128:(fc + 1) * 128], xT_all[:, kc, c0:c0 + N], kc == 0, kc == 7, r=[wG_t, t_xTa], w=[tpg])
                    for kc in range(8):
                        mm(pu[:, 0:N], wU[:, kc, fc * 128:(fc + 1) * 128], xT_all[:, kc, c0:c0 + N], kc == 0, kc == 7, r=[wU_t, t_xTa], w=[tpu])
                    sg, tsg = sgm_r.next()
                    act(sg[:, 0:N], pg[:, 0:N], AF.Silu, r=[tpg], w=[tsg])
                    tt('dve', HT[:, fc, c0:c0 + N], sg[:, 0:N], pu[:, 0:N], ALU.mult, r=[tsg, tpu], w=[t_HT[ci][fc]])
            wD, wD_t = WS.next()
            for j in range(NT):
                n = tile_rows(j)
                ci = min(j // 4, 4)
                for half in range(2):
                    py, tpy = psum()
                    for fc in range(4):
                        mm(py[0:n, :], HT[:, fc, j * 128:j * 128 + n], wD[:, fc, half * 512:(half + 1) * 512], fc == 0, fc == 3, r=[wD_t, t_HT[ci][fc]], w=[tpy])
                    xs = x_tok[0:n, j, half * 512:(half + 1) * 512]
                    stt(xs, py[0:n, :], gate_full[0:n, j, ex:ex + 1], xs, ALU.mult, ALU.add, r=[tpy, t_gate[j], t_x[j]], w=[t_x[j]])
        for j in range(NT):
            layer_norm(j, l, 1, t_gbt)

        P.barrier()
        AR.reset()
        xT_all = AR.alloc([128, 8, TALL], BF16)
        t_xTa = Tok()
        pT_all = AR.alloc([128, 2, TALL], BF16)
        t_pT = Tok()
        ptile_r = AR.rot(2, [128, 256], F32)
        sgp_r = AR.rot(2, [128, 512], F32)
        transpose_tiles(xT_all, t_xTa, list(range(NT)), 0)
        load_gb(l, 2)
        for j in range(NT):
            n = tile_rows(j)
            c0 = j * 128
            pt, tpt = ptile_r.next()
            dma(pt[0:n, :], pin[l, c0:c0 + n, :], w=[tpt])
            pb, ptk = psum()
            for q in range(2):
                tr(pb[:, q * 128:q * 128 + n], pt[0:n, q * 128:(q + 1) * 128], ident_f[0:n, 0:n], r=[tpt, t_c], w=[ptk])
            cp('act', pT_all[:, :, c0:c0 + n], pb[:, 0:256].rearrange("p (a b) -> p a b", a=2)[:, :, 0:n], r=[ptk], w=[t_pT])
            P.op('act', lambda e, n=n, j=j: e.mul(out=x_tok[0:n, j, :], in_=x_tok[0:n, j, :], mul=ALPHA), r=[t_x[j], t_xTa], w=[t_x[j]])
        th = []
        for half_ in range(2):
            th.append(lambda l=l, half_=half_: wmat(w_pg[l], half_ * 512, 512))
            th.append(lambda l=l, half_=half_: wmat(w_pp[l], half_ * 512, 512, kcs=2))
        WS = WStream(th)
        for half in range(2):
            wPG, wPG_t = WS.next()
            wPP, wPP_t = WS.next()
            for j in range(NT):
                n = tile_rows(j)
                c0 = j * 128
                p1, tp1 = psum()
                for kc in range(8):
                    mm(p1[0:n, :], xT_all[:, kc, c0:c0 + n], wPG[:, kc, :], kc == 0, kc == 7, r=[wPG_t, t_xTa], w=[tp1])
                p2, tp2 = psum()
                for kc in range(2):
                    mm(p2[0:n, :], pT_all[:, kc, c0:c0 + n], wPP[:, kc, :], kc == 0, kc == 1, r=[wPP_t, t_pT], w=[tp2])
                sg, tsg = sgp_r.next()
                act(sg[0:n, :], p1[0:n, :], AF.Sigmoid, r=[tp1], w=[tsg])
                tt('dve', sg[0:n, :], sg[0:n, :], p2[0:n, :], ALU.mult, r=[tsg, tp2], w=[tsg])
                xs = x_tok[0:n, j, half * 512:(half + 1) * 512]
                tt('dve', xs, xs, sg[0:n, :], ALU.add, r=[t_x[j], tsg], w=[t_x[j]])
        for j in range(NT):
            layer_norm(j, l, 2, t_gbt)
        P.barrier()

    dma(o_y[0:1024, :].rearrange("(j p) d -> p j d", p=128), x_tok[:, 0:8, :], r=t_x[0:8])
    dma(o_y[1024:2048, :].rearrange("(j p) d -> p j d", p=128), x_tok[:, 8:16, :], r=t_x[8:16])
    dma(o_y[2048:2112, :], x_tok[0:64, 16, :], r=[t_x[16]])
    P.emit()
    return nc, P


_CACHE = {}


def kernel(x_prompt, x_sample, state_mlstm_C, state_mlstm_n, state_mlstm_m, state_swa_k, state_swa_v,
           p_prompt, p_sample, w_in, b_in, mh_gain, w_a, w_b, w_out, rel_table, w_sink, ln_g, ln_b,
           w_rg, b_rg, w_re, b_re, w_eg, w_eu, w_ed, w_pg, w_pp, _depth=4):
    f = lambda a: np.ascontiguousarray(np.asarray(a, dtype=np.float32))
    if _depth not in _CACHE:
        _CACHE[_depth] = build(_depth)
    nc, P = _CACHE[_depth]
    consts = make_consts()
    shared = dict(w_in=f(w_in), b_in=f(b_in), mh_gain=f(mh_gain), w_a=f(w_a), w_b=f(w_b), w_out=f(w_out),
                  rel_table=f(rel_table), w_sink=f(w_sink), ln_g=f(ln_g), ln_b=f(ln_b), w_rg=f(w_rg), b_rg=f(b_rg),
                  w_re=f(w_re), b_re=f(b_re), w_eg=f(w_eg), w_eu=f(w_eu), w_ed=f(w_ed), w_pg=f(w_pg), w_pp=f(w_pp))
    shared.update(consts)
    x_prompt = f(x_prompt); x_sample = f(x_sample); p_prompt = f(p_prompt); p_sample = f(p_sample)
    sCa = f(state_mlstm_C); sna = f(state_mlstm_n); sma = f(state_mlstm_m); ska = f(state_swa_k); sva = f(state_swa_v)
    in_maps = []
    for c in range(8):
        sl = slice(16 * c, 16 * c + 16)
        m = dict(shared)
        m['xin'] = np.ascontiguousarray(np.concatenate([x_prompt[c], x_sample[sl].reshape(64, D)], 0))
        m['pin'] = np.ascontiguousarray(np.concatenate([p_prompt[:, c], p_sample[:, sl].reshape(4, 64, 256)], 1))
        m['sC'] = np.ascontiguousarray(sCa[:, sl])
        m['sn'] = np.ascontiguousarray(sna[:, sl])
        m['sm'] = np.ascontiguousarray(sma[:, sl].reshape(4, 64))
        m['sk'] = np.ascontiguousarray(ska[:, sl].reshape(4, 16, 128, 256))
        m['sv'] = np.ascontiguousarray(sva[:, sl].reshape(4, 16, 128, 256))
        in_maps.append(m)
    res = run_bass_kernel_spmd(nc, in_maps, core_ids=list(range(8)))
    R = res.results
    y_p = np.stack([R[c]['o_y'][:2048] for c in range(8)], 0)
    y_s = np.concatenate([R[c]['o_y'][2048:].reshape(16, 4, D) for c in range(8)], 0)
    C_p = np.stack([R[c]['o_Cp'] for c in range(8)], 1)
    n_p = np.stack([R[c]['o_np'] for c in range(8)], 1)
    m_p = np.stack([R[c]['o_mp'] for c in range(8)], 1)
    k_p = np.stack([R[c]['o_kp'].reshape(4, 128, 4, 64) for c in range(8)], 1)
    v_p = np.stack([R[c]['o_vp'].reshape(4, 128, 4, 64) for c in range(8)], 1)
    C_s = np.concatenate([R[c]['o_Cs'] for c in range(8)], 1)
    n_s = np.concatenate([R[c]['o_ns'] for c in range(8)], 1)
    m_s = np.concatenate([R[c]['o_ms'].reshape(4, 16, 4) for c in range(8)], 1)
    k_s = np.concatenate([R[c]['o_ks'].reshape(4, 16, 128, 4, 64) for c in range(8)], 1)
    v_s = np.concatenate([R[c]['o_vs'].reshape(4, 16, 128, 4, 64) for c in range(8)], 1)
    outs = (y_p, y_s, C_p, n_p, m_p, k_p, v_p, C_s, n_s, m_s, k_s, v_s)
    return tuple(np.ascontiguousarray(o, dtype=np.float32) for o in outs)
```
